# Optimizing a Trainium2 kernel written in Bass

```python
import math
import jax, jax.numpy as jnp
from jax import lax
import numpy as np

D_MODEL = 1024
BATCH = 4
SEQ = 8192
DEPTH = 4

D_MIX = D_MODEL
CONV_DIM = D_MIX // 2
CONV_GROUPS = 8
CONV_WIDTH = 31
GLA_HEADS = 4
GLA_VAL_DIM = D_MIX // 2
GLA_HEAD_V = GLA_VAL_DIM // GLA_HEADS
GLA_KEY_DIM = GLA_VAL_DIM // 2
GLA_HEAD_K = GLA_KEY_DIM // GLA_HEADS
GATE_RANK = 16
GATE_TAU = 16.0
GLA_CHUNK = 64
IN_COLS = 2 * CONV_DIM + 2 * GLA_KEY_DIM + 2 * GLA_VAL_DIM + GATE_RANK
MEM_LEN = 256
X_HEADS = 4
X_HEAD_DIM = D_MODEL // X_HEADS
D_FF = int(math.ceil(8 * D_MODEL / 3 / 256) * 256)
DEEPNORM_ALPHA = (2 * DEPTH) ** 0.25
DEEPNORM_BETA = (8 * DEPTH) ** -0.25
LN_EPS = 1e-5

kernel_name = "hybrid_conv_gla_deepnorm_trunk"


def layer_norm(x, g, b):
    xf = x.astype(jnp.float32)
    mu = jnp.mean(xf, axis=-1, keepdims=True)
    xc = xf - mu
    var = jnp.mean(xc * xc, axis=-1, keepdims=True)
    return (xc * lax.rsqrt(var + LN_EPS) * g.astype(jnp.float32) + b.astype(jnp.float32)).astype(x.dtype)


def rms_norm(x, g):
    xf = x.astype(jnp.float32)
    ms = jnp.mean(xf * xf, axis=-1, keepdims=True)
    return (xf * lax.rsqrt(ms + LN_EPS) * g.astype(jnp.float32)).astype(x.dtype)


def causal_depthwise_conv(u, w, b):
    c = u.shape[-1]
    y = lax.conv_general_dilated(
        u, w[:, None, :].astype(u.dtype), window_strides=(1,),
        padding=[(w.shape[0] - 1, 0)],
        dimension_numbers=("NWC", "WIO", "NWC"),
        feature_group_count=c)
    return y + b


def gla_chunked(q, k, v, log_a):
    out_dtype = v.dtype
    bsz, t, h, dk = q.shape
    dv = v.shape[-1]
    n = t // GLA_CHUNK

    def to_chunks(z):
        return z.astype(jnp.float32).reshape(bsz, n, GLA_CHUNK, h, z.shape[-1]).transpose(0, 3, 1, 2, 4)

    qc = to_chunks(q) * (dk ** -0.5)
    kc, vc, gc = to_chunks(k), to_chunks(v), to_chunks(log_a)
    bcum = jnp.cumsum(gc, axis=3)
    b_last = bcum[:, :, :, -1:, :]
    qe = qc * jnp.exp(bcum)
    ke = kc * jnp.exp(-bcum)
    kd = kc * jnp.exp(b_last - bcum)
    causal = jnp.tril(jnp.ones((GLA_CHUNK, GLA_CHUNK), dtype=bool))
    att = jnp.einsum("bhncd,bhnsd->bhncs", qe, ke)
    att = jnp.where(causal, att, 0.0)
    o_intra = jnp.einsum("bhncs,bhnse->bhnce", att, vc)
    upd = jnp.einsum("bhncd,bhnce->bhnde", kd, vc)
    decay = jnp.exp(b_last[:, :, :, 0, :])

    def step(state, inp):
        dec, u = inp
        return dec[..., None] * state + u, state

    s0 = jnp.zeros((bsz, h, dk, dv), jnp.float32)
    _, s_prev = lax.scan(step, s0, (jnp.moveaxis(decay, 2, 0), jnp.moveaxis(upd, 2, 0)))
    s_prev = jnp.moveaxis(s_prev, 0, 2)
    o_inter = jnp.einsum("bhncd,bhnde->bhnce", qe, s_prev)
    o = (o_intra + o_inter).transpose(0, 2, 3, 1, 4).reshape(bsz, t, h, dv)
    return o.astype(out_dtype)


def hybrid_mixer(h, w_in, w_a2, b_a, conv_w, conv_b, conv_ln_g, conv_ln_b, gla_norm_g, w_out):
    bsz, t, _ = h.shape
    proj = h @ w_in
    cuts = np.cumsum([CONV_DIM, CONV_DIM, GLA_KEY_DIM, GLA_KEY_DIM, GLA_VAL_DIM, GLA_VAL_DIM]).tolist()
    c_a, c_g, q, k, v, r, a_low = jnp.split(proj, cuts, axis=-1)
    u = c_a * jax.nn.sigmoid(c_g)
    u = causal_depthwise_conv(u, conv_w, conv_b)
    u = jax.nn.silu(layer_norm(u, conv_ln_g, conv_ln_b))
    z = a_low @ w_a2 + b_a
    log_a = jax.nn.log_sigmoid(z.astype(jnp.float32)) / GATE_TAU
    heads = lambda y, d: y.reshape(bsz, t, GLA_HEADS, d)
    o = gla_chunked(heads(q, GLA_HEAD_K), heads(k, GLA_HEAD_K), heads(v, GLA_HEAD_V),
                    heads(log_a, GLA_HEAD_K))
    o = rms_norm(o, gla_norm_g) * jax.nn.silu(heads(r, GLA_HEAD_V))
    o = o.reshape(bsz, t, GLA_VAL_DIM)
    return jnp.concatenate([u, o], axis=-1) @ w_out


def memory_cross_attention(x, mem, w_q, w_kv, w_o):
    bsz, t, _ = x.shape
    q = (x @ w_q).reshape(bsz, t, X_HEADS, X_HEAD_DIM)
    kv = mem @ w_kv
    k, v = jnp.split(kv, 2, axis=-1)
    k = k.reshape(bsz, MEM_LEN, X_HEADS, X_HEAD_DIM)
    v = v.reshape(bsz, MEM_LEN, X_HEADS, X_HEAD_DIM)
    s = jnp.einsum("bthd,bmhd->bhtm", q.astype(jnp.float32), k.astype(jnp.float32)) * (X_HEAD_DIM ** -0.5)
    p = jax.nn.softmax(s, axis=-1).astype(v.dtype)
    o = jnp.einsum("bhtm,bmhd->bthd", p, v).reshape(bsz, t, D_MODEL)
    return o @ w_o


def swiglu(x, w_in, w_out):
    g, u = jnp.split(x @ w_in, 2, axis=-1)
    return (jax.nn.silu(g) * u) @ w_out


def setup_inputs(seed: int = 0) -> dict:
    key = jax.random.key(seed)
    ks = jax.random.split(key, 24)
    nrm = lambda k, shape, scale: jax.random.normal(k, shape, jnp.float32) * scale
    gain = lambda k, shape: 1.0 + nrm(k, shape, 0.02)
    L = DEPTH
    return {
        "x": nrm(ks[0], (BATCH, SEQ, D_MODEL), 1.0),
        "mem": nrm(ks[1], (BATCH, MEM_LEN, D_MODEL), 1.0),
        "ln0_g": gain(ks[2], (D_MODEL,)),
        "ln0_b": nrm(ks[3], (D_MODEL,), 0.02),
        "w_in": nrm(ks[4], (L, D_MODEL, IN_COLS), D_MODEL ** -0.5),
        "w_a2": nrm(ks[5], (L, GATE_RANK, GLA_KEY_DIM), GATE_RANK ** -0.5),
        "b_a": nrm(ks[6], (L, GLA_KEY_DIM), 0.01),
        "conv_w": nrm(ks[7], (L, CONV_WIDTH, CONV_DIM), CONV_WIDTH ** -0.5),
        "conv_b": nrm(ks[8], (L, CONV_DIM), 0.02),
        "conv_ln_g": gain(ks[9], (L, CONV_DIM)),
        "conv_ln_b": nrm(ks[10], (L, CONV_DIM), 0.02),
        "gla_norm_g": gain(ks[11], (L, GLA_HEAD_V)),
        "w_mix_out": nrm(ks[12], (L, D_MIX, D_MODEL), DEEPNORM_BETA * D_MIX ** -0.5),
        "ln1_g": gain(ks[13], (L, D_MODEL)),
        "ln1_b": nrm(ks[14], (L, D_MODEL), 0.02),
        "w_xq": nrm(ks[15], (L, D_MODEL, D_MODEL), D_MODEL ** -0.5),
        "w_xkv": jnp.concatenate([
            nrm(ks[16], (L, D_MODEL, D_MODEL), D_MODEL ** -0.5),
            nrm(jax.random.fold_in(ks[16], 1), (L, D_MODEL, D_MODEL), DEEPNORM_BETA * D_MODEL ** -0.5)], axis=-1),
        "w_xo": nrm(ks[17], (L, D_MODEL, D_MODEL), DEEPNORM_BETA * D_MODEL ** -0.5),
        "ln2_g": gain(ks[18], (L, D_MODEL)),
        "ln2_b": nrm(ks[19], (L, D_MODEL), 0.02),
        "w_ffn_in": nrm(ks[20], (L, D_MODEL, 2 * D_FF), D_MODEL ** -0.5),
        "w_ffn_out": nrm(ks[21], (L, D_FF, D_MODEL), DEEPNORM_BETA * D_FF ** -0.5),
        "ln3_g": gain(ks[22], (L, D_MODEL)),
        "ln3_b": nrm(ks[23], (L, D_MODEL), 0.02),
    }


def reference(x, mem, ln0_g, ln0_b, w_in, w_a2, b_a, conv_w, conv_b, conv_ln_g, conv_ln_b,
              gla_norm_g, w_mix_out, ln1_g, ln1_b, w_xq, w_xkv, w_xo, ln2_g, ln2_b,
              w_ffn_in, w_ffn_out, ln3_g, ln3_b):
    h = layer_norm(x, ln0_g, ln0_b)
    for l in range(DEPTH):
        mix = hybrid_mixer(h, w_in[l], w_a2[l], b_a[l], conv_w[l], conv_b[l], conv_ln_g[l],
                           conv_ln_b[l], gla_norm_g[l], w_mix_out[l])
        h = layer_norm(DEEPNORM_ALPHA * h + mix, ln1_g[l], ln1_b[l])
        xa = memory_cross_attention(h, mem, w_xq[l], w_xkv[l], w_xo[l])
        h = layer_norm(DEEPNORM_ALPHA * h + xa, ln2_g[l], ln2_b[l])
        ff = swiglu(h, w_ffn_in[l], w_ffn_out[l])
        h = layer_norm(DEEPNORM_ALPHA * h + ff, ln3_g[l], ln3_b[l])
    return h
```

```python
import numpy as np
import concourse.bass as bass
import concourse.mybir as mybir
from concourse.bass_utils import run_bass_kernel_spmd

F32 = mybir.dt.float32
BF16 = mybir.dt.bfloat16
AF = mybir.ActivationFunctionType
ALU = mybir.AluOpType

D = 1024
KC = 8
NT = 512
DFF = 2816
HC = 22
INC = 2576
MEM = 256
DEPTH = 4
SEQ = 8192
ALPHA = float(8 ** 0.25)
EPS = 1e-5
CONVW = 31
HALO = 30
NPL = 185
LSLOTS = 2
NPC = LSLOTS * NPL + 16

SAME_ENG_SYNC = True
DG_ENG = 'pool'


class Sched:
    def __init__(self, nc):
        self.nc = nc
        self.eng = {'pe': nc.tensor, 'act': nc.scalar, 'dve': nc.vector, 'pool': nc.gpsimd, 'sp': nc.sync}
        self.ops = []
        self.alias = {}
        self.sems = {}
        self.bank_i = 0
        self.frozen = False

    def op(self, eng, fn, reads=(), writes=()):
        if self.frozen:
            return
        self.ops.append(dict(eng=eng, fn=fn, r=list(reads), w=list(writes), dma=None))

    def dma(self, eng, out, in_, stream, reads=(), writes=()):
        if self.frozen:
            return
        self.ops.append(dict(eng=eng, fn=(lambda E, o=out, i=in_: E.dma_start(out=o, in_=i)),
                             r=list(reads), w=list(writes), dma=stream))

    def cc(self, fn, key, reads=(), writes=()):
        if self.frozen:
            return
        self.ops.append(dict(eng='pool', fn=fn, r=list(reads), w=list(writes), dma=key, inc=1))

    def mm(self, out, lhsT, rhs, start, stop, reads, writes, sgc=False):
        if sgc:
            self.op('pe', lambda E, o=out, l=lhsT, r=rhs, a=start, b=stop: E.matmul(o, l, r, start=a, stop=b, skip_group_check=True),
                    reads, writes)
        else:
            self.op('pe', lambda E, o=out, l=lhsT, r=rhs, a=start, b=stop: E.matmul(o, l, r, start=a, stop=b),
                    reads, writes)

    def act(self, out, in_, func, reads, writes, bias=0.0, scale=1.0):
        self.op('act', lambda E, o=out, i=in_, f=func, b=bias, s=scale: E.activation(out=o, in_=i, func=f, bias=b, scale=s),
                reads, writes)

    def tt(self, eng, out, in0, in1, op, reads, writes):
        self.op(eng, lambda E, o=out, a=in0, b=in1, p=op: E.tensor_tensor(out=o, in0=a, in1=b, op=p), reads, writes)

    def ts(self, eng, out, in0, s1, s2, op0, op1, reads, writes):
        if s2 is None:
            self.op(eng, lambda E, o=out, a=in0, x=s1, p=op0: E.tensor_scalar(out=o, in0=a, scalar1=x, scalar2=None, op0=p),
                    reads, writes)
        else:
            self.op(eng, lambda E, o=out, a=in0, x=s1, y=s2, p=op0, q=op1: E.tensor_scalar(out=o, in0=a, scalar1=x, scalar2=y, op0=p, op1=q),
                    reads, writes)

    def fence(self, eng, streams):
        if self.frozen:
            return
        self.ops.append(dict(eng=eng, fn=None, r=[], w=[], dma=None, fence=list(streams)))

    def stt(self, eng, out, in0, scalar, in1, op0, op1, reads, writes):
        self.op(eng, lambda E, o=out, a=in0, s=scalar, b=in1, p=op0, q=op1: E.scalar_tensor_tensor(out=o, in0=a, scalar=s, in1=b, op0=p, op1=q),
                reads, writes)

    def copy(self, eng, out, in_, reads, writes):
        if eng == 'act':
            self.act(out, in_, AF.Copy, reads, writes)
        else:
            self.op(eng, lambda E, o=out, i=in_: E.tensor_copy(out=o, in_=i), reads, writes)

    def memset(self, eng, ap, val, writes):
        self.op(eng, lambda E, a=ap, v=val: E.memset(a, v), (), writes)

    def _expand(self, toks):
        out = []
        for t in toks:
            a = self.alias.get(t)
            if a is None:
                out.append(t)
            else:
                out.extend(a)
        return out

    def _sem(self, v):
        s = self.sems.get(v)
        if s is None:
            s = self.nc.alloc_semaphore(name=("s_" + str(v)).replace("'", "").replace(" ", "").replace("(", "").replace(")", "").replace(",", "_"))
            self.sems[v] = s
        return s

    def emit(self, final_streams):
        ops = self.ops
        lastw = {}
        readers = {}
        last_in_stream = {}
        for i, o in enumerate(ops):
            v = ('dma', o['dma']) if o['dma'] else o['eng']
            o['v'] = v
            o['needed'] = False
            deps = set()
            R = self._expand(o['r'])
            W = self._expand(o['w'])
            for t in R:
                j = lastw.get(t)
                if j is not None:
                    deps.add(j)
            for t in W:
                j = lastw.get(t)
                if j is not None:
                    deps.add(j)
                for j in readers.get(t, {}).values():
                    deps.add(j)
            if o['dma']:
                j = last_in_stream.get(o['dma'])
                if j is not None:
                    deps.add(j)
                last_in_stream[o['dma']] = i
            deps.discard(i)
            dl = []
            for j in deps:
                vj = ops[j]['v']
                if vj == 'pe' and v == 'pe':
                    continue
                if (not SAME_ENG_SYNC) and vj == v and not o['dma']:
                    continue
                dl.append(j)
                ops[j]['needed'] = True
            o['deps'] = dl
            for t in W:
                lastw[t] = i
                readers[t] = {}
            for t in R:
                readers.setdefault(t, {})[v] = i
        cnt = {}
        seen = {e: {} for e in self.eng}
        for o in ops:
            e = o['eng']
            E = self.eng[e]
            v = o['v']
            if o['fn'] is None:
                for st in o['fence']:
                    vv = ('dma', st)
                    c = cnt.get(vv, 0)
                    if c and seen[e].get(vv, 0) < c:
                        E.wait_ge(self._sem(vv), c)
                        seen[e][vv] = c
                o['cnt'] = None
                continue
            need = {}
            for j in o['deps']:
                dj = ops[j]
                need[dj['v']] = max(need.get(dj['v'], 0), dj['cnt'])
            for vv, c in need.items():
                if seen[e].get(vv, 0) < c:
                    E.wait_ge(self._sem(vv), c)
                    seen[e][vv] = c
            ins = o['fn'](E)
            if o['dma']:
                inc = o.get('inc', 16)
                cnt[v] = cnt.get(v, 0) + inc
                ins.then_inc(self._sem(v), inc)
                o['cnt'] = cnt[v]
            elif o['needed']:
                cnt[v] = cnt.get(v, 0) + 1
                ins.then_inc(self._sem(v), 1)
                o['cnt'] = cnt[v]
            else:
                o['cnt'] = None
        for v in list(cnt):
            if isinstance(v, tuple) and v[0] == 'dma':
                self.eng['sp'].wait_ge(self._sem(v), cnt[v])
        self.max_cnt = dict(cnt)


def build_program(L=LSLOTS, TT=SEQ, dumps=(), stop_after=None, ncores=8):
    ntiles = TT // NT
    nsteps = ntiles + 1
    nc = bass.Bass("TRN2", target_bir_lowering=False)
    S = Sched(nc)
    dumps = set(dumps)
    dump_out = {}

    def din(name, shape, dt=F32):
        return nc.dram_tensor(name, list(shape), dt, kind="ExternalInput").ap()

    xT = din("xT", [D, TT])
    memT = din("memT", [D, MEM])
    w_in = din("w_in", [L, D, INC])
    w_mo = din("w_mix_out", [L, D, D])
    w_xq = din("w_xq", [L, D, D])
    w_xkv = din("w_xkv", [L, D, 2 * D])
    w_xo = din("w_xo", [L, D, D])
    w_fi = din("w_ffn_in", [L, D, 2 * DFF])
    w_fo = din("w_ffn_out", [L, DFF, D])
    pcols = din("pcols", [128, NPC])
    wa2e = din("wa2e", [L, 17, 256])
    consts = din("consts", [128, 900])
    flags = din("flags", [128, 4])
    outT = nc.dram_tensor("outT", [D, TT], F32, kind="ExternalOutput").ap()

    def dscr(name, shape, dt):
        return nc.dram_tensor(name, list(shape), dt).ap()

    contrib = dscr("contrib", [D, NT], F32)
    gath = dscr("gath", [2 * D, NT], F32)
    b_in = dscr("b_in", [L, D, INC], BF16)
    b_mo = dscr("b_mo", [L, D, D], BF16)
    b_xq = dscr("b_xq", [L, D, D], BF16)
    b_xkv = dscr("b_xkv", [L, D, 2 * D], BF16)
    b_xo = dscr("b_xo", [L, D, D], BF16)
    b_fi = dscr("b_fi", [L, D, 2 * DFF], BF16)
    b_fo = dscr("b_fo", [L, DFF, D], BF16)

    def sb(name, shape, dt):
        return nc.alloc_sbuf_tensor(name, list(shape), dt).ap()

    pc = sb("pc", [128, NPC], F32)
    cst = sb("cst", [128, 900], F32)
    Uneg = cst[:, 0:128]
    Lneg = cst[:, 128:256]
    mask4 = cst[:, 256:768]
    ones_bf = sb("ones_bf", [128, 128], BF16)
    wa2_f = sb("wa2_f", [32, 256], F32)
    wa2_b = sb("wa2_b", [32, L, 256], BF16)
    KT = sb("KT", [128, L, KC, MEM], BF16)
    Vb = sb("Vb", [128, L, 2, D], BF16)
    fl = sb("fl", [128, 4], F32)
    fA = fl[:, 0:1]
    fB = fl[:, 1:2]
    h32 = [sb("h32_%d" % i, [128, KC, NT], F32) for i in range(1)]
    hb = sb("hb", [128, KC, NT], BF16)
    NSLOT = 3
    wring = [sb("wr_%d" % i, [128, KC, 512], BF16) for i in range(NSLOT)]
    bufA = sb("bufA", [128, KC, NT], BF16)
    bufB = sb("bufB", [128, KC, NT], BF16)
    bigf = sb("big", [128, 12 * 512], F32)
    big = bigf.rearrange("p (a b) -> p a b", b=512)
    big_bf = bigf.bitcast(BF16).rearrange("p (j n) -> p j n", n=512)
    alT = sb("alT", [32, NT], BF16)
    lsb = sb("lsb", [128, 2, 256], F32)
    l_hi = sb("l_hi", [128, 4, 256], BF16)
    l_lo = sb("l_lo", [128, 4, 256], BF16)
    cst_b = sb("cst_b", [128, 384], BF16)
    ident_b = cst_b[:, 256:384]
    dg = sb("dg", [128, 4, CONVW, 128], BF16)
    Uneg_b = cst_b[:, 0:128]
    Lneg_b = cst_b[:, 128:256]
    ez = sb("ez", [128, 2, 256], F32)
    Ed = sb("Ed", [128, 4, 256], F32)
    EbT = sb("EbT", [128, 2, NT], F32)
    EnT = sb("EnT", [128, 2, NT], F32)
    sg32 = EbT
    decT = sb("decT", [128, 2, 8], F32)
    v_bf = sb("v_bf", [128, 4, 512], BF16)
    mem_b = v_bf.rearrange("p a b -> p (a b)").rearrange("p (k m) -> p k m", m=MEM)
    kd_bf = sb("kd_bf", [128, 4, 2, 256], BF16)
    qeT = sb("qeT", [128, 2, 2, NT], BF16)
    keT = sb("keT", [128, 2, NT], BF16)
    att_bf = sb("att_bf", [128, 2, 512], BF16)
    S32 = sb("S32", [128, L, 2, 256], F32)
    S_bf = sb("S_bf", [128, L, 2, 2, 256], BF16)
    ubuf = sb("ubuf", [128, L, 4, NT + HALO], BF16)
    m32 = sb("m32", [128, NT], F32)
    t32 = sb("t32", [128, NT], F32)
    r32 = sb("r32", [128, NT], F32)
    PT = EnT.rearrange("p a b -> p (a b)").bitcast(BF16).rearrange("p (s m t) -> p s m t", s=2, m=2)
    o32 = big[:, 0:4, :]
    cacc = big[:, 4:8, :]
    sr = big[:, 8:12, :]
    banks = [nc.alloc_psum_tensor("bank%d" % i, [128, 512], F32).ap() for i in range(8)]

    S.alias['bufA'] = [('bufA', k) for k in range(KC)]
    S.alias['bufB'] = [('bufB', k) for k in range(KC)]
    for k in range(KC):
        S.alias[('xo', k)] = [('bufA', k)]
        S.alias[('qT', k)] = [('bufB', k)]
    for k in range(4):
        S.alias[('mixin', 4 + k)] = [('bufA', 4 + k)]
    pinned = set()
    for i_ in range(2):
        S.alias[('sg', i_)] = [('EbT', i_)]
        for m_ in range(2):
            S.alias[('PT', i_, m_)] = [('EnT', i_)]
    for hs_ in range(1):
        S.alias[('h32', hs_)] = [('h32c', hs_, k) for k in range(KC)]
    S.alias['bigall'] = [('big', u) for u in range(12)]
    for h in range(4):
        S.alias[('o32', h)] = [('big', h)]
        S.alias[('cacc', h)] = [('big', 4 + h)]
        S.alias[('sr', h)] = [('big', 8 + h)]
    for j in range(HC):
        S.alias[('hid', j)] = [('big', j // 2)]

    def nbank(pin=False):
        i = S.bank_i
        while i in pinned:
            i = (i + 1) % 8
        S.bank_i = (i + 1) % 8
        if pin:
            pinned.add(i)
        return banks[i], ('bank', i)

    def unpin(tok):
        pinned.discard(tok[1])

    def stage(name):
        if stop_after == name:
            S.frozen = True

    def dump(name, ap, reads):
        if name not in dumps or S.frozen:
            return
        d = nc.dram_tensor("dbg_" + name, list(ap.shape), ap.dtype, kind="ExternalOutput").ap()
        dump_out[name] = d
        S.dma('sp', d, ap, 'dbg_' + name, reads=reads, writes=[('dbg', name)])

    S.dma('sp', pc, pcols, 'misc', writes=['pc'])
    S.dma('sp', cst, consts, 'misc', writes=['cst'])
    S.dma('sp', fl, flags, 'misc', writes=['fl'])
    S.memset('dve', ones_bf, 1.0, ['ones'])
    S.copy('dve', cst_b[:, 0:256], cst[:, 0:256], ['cst'], ['cst_b'])
    S.copy('dve', cst_b[:, 256:384], cst[:, 772:900], ['cst'], ['cst_b'])
    S.memset('pool', wa2_b, 0.0, [('wa2_b', l_) for l_ in range(L)])
    S.memset('pool', alT, 1.0, ['alT'])

    cast_rr = [0]
    stage_i = [0]

    def convert(src, dst, R, C):
        cw = 4096
        for rc in range(R // 128):
            c0 = 0
            while c0 < C:
                n = min(cw, C - c0)
                si = stage_i[0] % 2
                stage_i[0] += 1
                s32 = (h32[0].rearrange("p a b -> p (a b)") if si == 0 else bigf)[:, 0:n]
                sbf = (bufA if si == 0 else bufB).rearrange("p a b -> p (a b)")[:, 0:n]
                t32 = ('h32', 0) if si == 0 else 'bigall'
                tbf = 'bufA' if si == 0 else 'bufB'
                S.dma('sp', s32, src[rc * 128:(rc + 1) * 128, c0:c0 + n], 'cv_in%d' % si, writes=[t32])
                e = ['act', 'dve', 'pool'][cast_rr[0] % 3]
                cast_rr[0] += 1
                S.copy(e, sbf, s32, [t32], [tbf])
                S.dma('sp', dst[rc * 128:(rc + 1) * 128, c0:c0 + n], sbf, 'cv_out%d' % si, reads=[tbf], writes=[])
                c0 += n

    wsets = [(w_in, b_in, D, INC), (w_mo, b_mo, D, D), (w_xq, b_xq, D, D), (w_xkv, b_xkv, D, 2 * D),
             (w_xo, b_xo, D, D), (w_fi, b_fi, D, 2 * DFF), (w_fo, b_fo, DFF, D)]
    wb_tok = {}
    for l in range(L):
        for (src, dst, R, C) in wsets:
            convert(src[l], dst[l], R, C)
    S.fence('sp', ['cv_out0', 'cv_out1'])
    stage('prologue')

    ring_i = [0]

    def wload(src2d, r0, nk, c0, ncols):
        s = ring_i[0] % NSLOT
        ring_i[0] += 1
        dst = wring[s][:, 0:nk, 0:ncols]
        S.dma('sp', dst, src2d[r0:r0 + nk * 128, c0:c0 + ncols].rearrange("(kc p) c -> p kc c", p=128),
              'wr%d' % s, reads=[], writes=[('wr', s)])
        return wring[s], ('wr', s)

    def mm_acc(out, pairs, reads, btok):
        n = len(pairs)
        for i, (l, r) in enumerate(pairs):
            S.mm(out, l, r, i == 0, i == n - 1, reads, [btok])

    def layer_norm(y, ytok_list, gcol, bcol, nk, inv_n, out32, out32_tok, outb, outb_tok, func=AF.Identity,
                   flagged=False, pre=None):
        if pre is None:
            ybv = bufA[:, 0:nk, :]
            ysv = bufB[:, 0:nk, :]
            ta = [('bufA', k) for k in range(nk)]
            tb = [('bufB', k) for k in range(nk)]
            S.copy('dve', ybv, y, ytok_list, ta)
            S.act(ysv, y, AF.Square, ytok_list, tb)
            b1, t1 = nbank()
            mm_acc(b1, [(ones_bf, ybv[:, k, :]) for k in range(nk)], ['ones'] + ta, t1)
            b2, t2 = nbank()
            mm_acc(b2, [(ones_bf, ysv[:, k, :]) for k in range(nk)], ['ones'] + tb, t2)
        else:
            b1, t1, b2, t2 = pre
        S.ts('dve', m32, b1, inv_n, None, ALU.mult, ALU.bypass, [t1], ['m32'])
        if flagged:
            S.ts('dve', m32, m32, fA, None, ALU.mult, None, ['m32', 'fl'], ['m32'])
        S.tt('dve', t32, m32, m32, ALU.mult, ['m32'], ['t32'])
        S.stt('dve', r32, b2, inv_n, t32, ALU.mult, ALU.subtract, [t2, 't32'], ['r32'])
        S.ts('dve', r32, r32, EPS, None, ALU.add, None, ['r32'], ['r32'])
        S.act(r32, r32, AF.Ln, ['r32'], ['r32'])
        S.act(r32, r32, AF.Exp, ['r32'], ['r32'], scale=-0.5)
        if flagged:
            S.ts('dve', r32, r32, fA, fB, ALU.mult, ALU.add, ['r32', 'fl'], ['r32'])
        for k in range(nk):
            S.tt('dve', y[:, k, :], y[:, k, :], m32, ALU.subtract, [ytok_list[k], 'm32'], [ytok_list[k]])
        for k in range(nk):
            S.tt('dve', y[:, k, :], y[:, k, :], r32, ALU.mult, [ytok_list[k], 'r32'], [ytok_list[k]])
            if outb is not None:
                S.act(outb[:, k, :], y[:, k, :], func, [ytok_list[k], 'pc'], [outb_tok[k]],
                      bias=pc[:, bcol + k:bcol + k + 1], scale=pc[:, gcol + k:gcol + k + 1])
        if out32 is not None:
            for k in range(nk):
                S.act(out32[:, k, :], y[:, k, :], func, [ytok_list[k], 'pc'], [out32_tok[k]],
                      bias=pc[:, bcol + k:bcol + k + 1], scale=pc[:, gcol + k:gcol + k + 1])

    def resid_stats(j, hcur, htoks, acc_bank, acc_tok, st):
        b1, t1, b2, t2 = st
        S.stt('dve', hcur[:, j, :], hcur[:, j, :], ALPHA, acc_bank, ALU.mult, ALU.add, [htoks[j], acc_tok], [htoks[j]])
        S.copy('dve', hb[:, j, :], hcur[:, j, :], [htoks[j]], [('hb', j)])
        S.act(bufB[:, j, :], hcur[:, j, :], AF.Square, [htoks[j]], [('bufB', j)])
        S.mm(b1, ones_bf, hb[:, j, :], (j == 0), (j == KC - 1), ['ones', ('hb', j)], [t1], sgc=True)
        S.mm(b2, ones_bf, bufB[:, j, :], (j == 0), (j == KC - 1), ['ones', ('bufB', j)], [t2], sgc=True)

    def proj_residual_ln(l, wsrc, rhs_buf, rhs_tok, nkin, hcur, htoks, gcol, bcol):
        b1, t1 = nbank(pin=True)
        b2, t2 = nbank(pin=True)
        st = (b1, t1, b2, t2)
        for cg in range(2):
            wt, wtok = wload(wsrc, 0, nkin, cg * 512, 512)
            for jj in range(4):
                j = cg * 4 + jj
                bk, bt = nbank()
                mm_acc(bk, [(wt[:, k, jj * 128:(jj + 1) * 128], rhs_buf[:, k, :]) for k in range(nkin)],
                       [wtok] + rhs_tok, bt)
                resid_stats(j, hcur, htoks, bk, bt, st)
        unpin(t1)
        unpin(t2)
        layer_norm(hcur, htoks, gcol, bcol, KC, 1.0 / D, hcur, htoks, hb, [('hb', k) for k in range(KC)], pre=st)

    hbt = [('hb', k) for k in range(KC)]

    hbt = [('hb', k) for k in range(KC)]
    mem32 = big[:, 0:4, :].rearrange("p a b -> p (a b)").rearrange("p (k m) -> p k m", m=MEM)
    S.dma('sp', mem32, memT.rearrange("(kc p) m -> p kc m", p=128), 'misc', writes=[('o32', h) for h in range(4)])
    S.copy('dve', mem_b, mem32, [('o32', h) for h in range(4)], [('v_bf', i) for i in range(4)])
    S.alias['mem_b'] = [('v_bf', i) for i in range(4)]
    sslot = [[0, 0] for _ in range(L)]
    for l in range(L):
        S.dma('sp', wa2_f[0:17, :], wa2e[l], 'misc', writes=['wa2_f'])
        S.copy('dve', wa2_b[0:17, l, :], wa2_f[0:17, :], ['wa2_f'], [('wa2_b', l)])
        S.memset('dve', S32[:, l, :, :], 0.0, [('S32', l, 0), ('S32', l, 1)])
        S.memset('pool', S_bf[:, l, :, :, :], 0.0, [('S_bf', l, a, b) for a in range(2) for b in range(2)])
        S.memset('pool', ubuf[:, l, :, :], 0.0, [('ubuf', l, c) for c in range(4)])
        for cg in range(2):
            wt, wtok = wload(b_xkv[l], 0, KC, cg * 512, 512)
            for jj in range(4):
                j = cg * 4 + jj
                bk, bt = nbank()
                mm_acc(bk[:, 0:MEM], [(wt[:, k, jj * 128:(jj + 1) * 128], mem_b[:, k, :]) for k in range(KC)],
                       [wtok, 'mem_b'], bt)
                S.copy('act', KT[:, l, j, :], bk[:, 0:MEM], [bt], [('KT', l)])
        for cg in range(2):
            wt, wtok = wload(b_xkv[l], 0, KC, D + cg * 512, 512)
            for mc in range(2):
                bk, bt = nbank()
                mm_acc(bk, [(mem_b[:, k, mc * 128:(mc + 1) * 128], wt[:, k, :]) for k in range(KC)],
                       [wtok, 'mem_b'], bt)
                S.copy('act', Vb[:, l, mc, cg * 512:(cg + 1) * 512], bk, [bt], [('Vb', l)])
    S.dma('sp', gath[0:D, :], xT[:, 0:NT], 'misc', writes=['gath'])
    groups = [[2 * i, 2 * i + 1] for i in range(ncores // 2)]
    hcur = h32[0]
    htoks = [('h32c', 0, k) for k in range(KC)]
    hs = 0
    xa32 = bufA.rearrange("p a b -> p (a b)").bitcast(F32).rearrange("p (k t) -> p k t", t=NT)
    xb32 = bufB.rearrange("p a b -> p (a b)").bitcast(F32).rearrange("p (k t) -> p k t", t=NT)

    for s_ in range(nsteps):
        ti_in = min(s_, ntiles - 1)
        ti_out = max(s_ - 1, 0)
        for half in range(2):
            ksl = slice(half * 4, (half + 1) * 4)
            S.dma('sp', xa32, xT[half * 512:(half + 1) * 512, ti_in * NT:(ti_in + 1) * NT].rearrange("(kc p) t -> p kc t", p=128),
                  'xin', writes=['bufA'])
            S.dma('sp', xb32, gath[half * 512:(half + 1) * 512, :].rearrange("(kc p) t -> p kc t", p=128),
                  'gin', reads=['gath'], writes=['bufB'])
            S.ts('dve', hcur[:, ksl, :], xa32, fA, None, ALU.mult, None, ['bufA', 'fl'], htoks[half * 4:(half + 1) * 4])
            S.stt('dve', hcur[:, ksl, :], xb32, fB, hcur[:, ksl, :], ALU.mult, ALU.add,
                  ['bufB', 'fl'] + htoks[half * 4:(half + 1) * 4], htoks[half * 4:(half + 1) * 4])
        layer_norm(hcur, htoks, 0, 8, KC, 1.0 / D, hcur, htoks, hb, hbt, flagged=True)
        for l in range(L):
            P0 = 16 + l * NPL
            c_ln1g, c_ln1b, c_ln2g, c_ln2b, c_ln3g, c_ln3b = [P0 + 8 * i for i in range(6)]
            c_cw = P0 + 48
            c_cb = c_cw + 124
            c_clg = c_cb + 4
            c_clb = c_clg + 4
            c_gg = c_clb + 4

            wt6, wtok6 = wload(b_in[l], 0, KC, 2560, 16)
            bk, bt = nbank()
            mm_acc(bk[0:16, :], [(wt6[:, k, 0:16], hb[:, k, :]) for k in range(KC)], [wtok6] + hbt, bt)
            S.copy('act', alT[0:16, :], bk[0:16, :], [bt], ['alT'])
            bc0, bct0 = nbank(pin=True)
            bc1, bct1 = nbank(pin=True)
            bcs = [(bc0, bct0), (bc1, bct1)]
            for sbk in range(4):
                tsl = slice(sbk * 128, (sbk + 1) * 128)
                zb, zt = nbank()
                S.mm(zb[:, 0:256], alT[0:17, tsl], wa2_b[0:17, l, :], True, True, ['alT', ('wa2_b', l)], [zt])
                es = sbk % 2
                S.act(ez[:, es, :], zb[:, 0:256], AF.Exp, [zt], [('ez', es)], scale=-1.0)
                S.act(lsb[:, es, :], ez[:, es, :], AF.Ln, [('ez', es)], [('lsb', es)], bias=1.0)
                S.copy('dve', l_hi[:, sbk, :], lsb[:, es, :], [('lsb', es)], [('l_hi', sbk)])
                S.tt('dve', l_lo[:, sbk, :], lsb[:, es, :], l_hi[:, sbk, :], ALU.subtract, [('lsb', es), ('l_hi', sbk)], [('l_lo', sbk)])
                for fc in range(2):
                    S.mm(bcs[fc][0][:, tsl], l_hi[:, sbk, fc * 128:(fc + 1) * 128], Uneg_b, (sbk == 0), False,
                         [('l_hi', sbk), 'cst_b'], [bcs[fc][1]], sgc=True)
                    S.mm(bcs[fc][0][:, tsl], l_lo[:, sbk, fc * 128:(fc + 1) * 128], Uneg_b, False, True,
                         [('l_lo', sbk), 'cst_b'], [bcs[fc][1]], sgc=True)
                db, dt_ = nbank()
                S.mm(db[:, 0:256], Lneg_b, l_hi[:, sbk, :], True, False, [('l_hi', sbk), 'cst_b'], [dt_])
                S.mm(db[:, 0:256], Lneg_b, l_lo[:, sbk, :], False, True, [('l_lo', sbk), 'cst_b'], [dt_])
                S.act(Ed[:, sbk, :], db[:, 0:256], AF.Exp, [dt_], [('Ed', sbk)])
            for fc in range(2):
                S.act(EbT[:, fc, :], bcs[fc][0], AF.Exp, [bcs[fc][1]], [('EbT', fc)])
                S.act(EnT[:, fc, :], bcs[fc][0], AF.Exp, [bcs[fc][1]], [('EnT', fc)], scale=-1.0)
                S.act(decT[:, fc, :], bcs[fc][0].rearrange("p (c t) -> p c t", t=64)[:, :, 63], AF.Exp,
                      [bcs[fc][1]], [('decT', fc)])
            unpin(bct0)
            unpin(bct1)
            stage('gates')
            for ch in range(4):
                for k in range(CONVW):
                    wc_ = pc[:, c_cw + ch * CONVW + k:c_cw + ch * CONVW + k + 1]
                    if ch < 2:
                        S.act(dg[:, ch, k, :], ident_b, AF.Identity, ['cst_b', 'pc'], [('dg', ch)], scale=wc_)
                    else:
                        S.ts('dve', dg[:, ch, k, :], ident_b, wc_, None, ALU.mult, None, ['cst_b', 'pc'], [('dg', ch)])
            wt3, wtok3 = wload(b_in[l], 0, KC, 1024, 512)
            wt4, wtok4 = wload(b_in[l], 0, KC, 1536, 512)
            for sbk in range(4):
                tsl = slice(sbk * 128, (sbk + 1) * 128)
                vb_, vt_ = nbank()
                mm_acc(vb_, [(hb[:, k, tsl], wt4[:, k, :]) for k in range(KC)], [wtok4] + hbt, vt_)
                S.copy('act', v_bf[:, sbk, :], vb_, [vt_], [('v_bf', sbk)])
                kb_, kt_ = nbank()
                mm_acc(kb_[:, 0:256], [(hb[:, k, tsl], wt3[:, k, 256:512]) for k in range(KC)], [wtok3] + hbt, kt_)
                for cc in range(2):
                    S.stt('dve', kd_bf[:, sbk, cc, :], kb_[:, 0:256], cst[:, 768 + cc:769 + cc], Ed[:, sbk, :], ALU.mult, ALU.mult,
                          [kt_, ('Ed', sbk), 'cst'], [('kd_bf', sbk)])
            stage('kv')
            for fc in range(2):
                qb_, qt_ = nbank()
                mm_acc(qb_, [(wt3[:, k, fc * 128:(fc + 1) * 128], hb[:, k, :]) for k in range(KC)], [wtok3] + hbt, qt_)
                for hl in range(2):
                    S.stt('dve', qeT[:, fc, hl, :], qb_, cst[:, 770 + hl:771 + hl], EbT[:, fc, :], ALU.mult, ALU.mult,
                          [qt_, ('EbT', fc), 'cst'], [('qeT', fc)])
                kb_, kt_ = nbank()
                mm_acc(kb_, [(wt3[:, k, 256 + fc * 128:256 + (fc + 1) * 128], hb[:, k, :]) for k in range(KC)], [wtok3] + hbt, kt_)
                S.tt('dve', keT[:, fc, :], kb_, EnT[:, fc, :], ALU.mult, [kt_, ('EnT', fc)], [('keT', fc)])
            if l == 0 and s_ == 0:
                dump('qeT', qeT, [('qeT', 0), ('qeT', 1)])
                dump('keT', keT, [('keT', 0), ('keT', 1)])
                dump('kd', kd_bf, [('kd_bf', i) for i in range(4)])
                dump('vbf', v_bf, [('v_bf', i) for i in range(4)])
                dump('decT', decT, [('decT', 0), ('decT', 1)])
            stage('qk')
            wt1, wtok1 = wload(b_in[l], 0, KC, 0, 512)
            wt2, wtok2 = wload(b_in[l], 0, KC, 512, 512)

            def conv_a(ch):
                gb, gt = nbank()
                mm_acc(gb, [(wt2[:, k, ch * 128:(ch + 1) * 128], hb[:, k, :]) for k in range(KC)], [wtok2] + hbt, gt)
                ss = ch % 2
                S.act(sg32[:, ss, :], gb, AF.Sigmoid, [gt], [('sg', ss)])
                ab_, at_ = nbank()
                mm_acc(ab_, [(wt1[:, k, ch * 128:(ch + 1) * 128], hb[:, k, :]) for k in range(KC)], [wtok1] + hbt, at_)
                S.tt('dve', ubuf[:, l, ch, HALO:HALO + NT], ab_, sg32[:, ss, :], ALU.mult, [at_, ('sg', ss)], [('ubuf', l, ch)])

            def conv_b(ch):
                cb_, ct_ = nbank()
                mm_acc(cb_, [(dg[:, ch, k, :], ubuf[:, l, ch, k:k + NT]) for k in range(CONVW)], [('dg', ch), ('ubuf', l, ch)], ct_)
                S.act(cacc[:, ch, :], cb_, AF.Identity, [ct_, 'pc'], [('cacc', ch)], bias=pc[:, c_cb + ch:c_cb + ch + 1])
            for sbk in range(4):
                tsl = slice(sbk * 128, (sbk + 1) * 128)
                ab, at = nbank()
                for h in range(4):
                    fc, hl = h // 2, h % 2
                    S.mm(ab[:, h * 128:(h + 1) * 128], keT[:, fc, tsl], qeT[:, fc, hl, tsl],
                         True, True, [('keT', fc), ('qeT', fc)], [at])
                asl = sbk % 2
                S.tt('dve', att_bf[:, asl, :], ab, mask4, ALU.mult, [at, 'cst'], [('att', asl)])
                stage('att')
                ob, ot = nbank(pin=True)
                for h in range(4):
                    S.mm(ob[:, h * 128:(h + 1) * 128], v_bf[:, sbk, h * 128:(h + 1) * 128], att_bf[:, asl, h * 128:(h + 1) * 128],
                         (h == 0), False, [('v_bf', sbk), ('att', asl)], [ot], sgc=True)
                stage('intra')
                for cc in range(2):
                    ci = sbk * 2 + cc
                    csl = slice(sbk * 128 + cc * 64, sbk * 128 + (cc + 1) * 64)
                    for h in range(4):
                        fc, hl = h // 2, h % 2
                        sl = sslot[l][fc]
                        S.mm(ob[:, h * 128 + cc * 64:h * 128 + (cc + 1) * 64],
                             S_bf[:, l, fc, sl, hl * 128:(hl + 1) * 128],
                             qeT[:, fc, hl, csl], False, (cc == 1),
                             [('S_bf', l, fc, sl), ('qeT', fc)], [ot], sgc=True)
                    stage('inter0')
                    for hp in range(2):
                        ub, ut = nbank()
                        S.mm(ub[:, 0:256], kd_bf[:, sbk, cc, hp * 128:(hp + 1) * 128],
                             v_bf[:, sbk, hp * 256:(hp + 1) * 256], True, True,
                             [('kd_bf', sbk), ('v_bf', sbk)], [ut])
                        S.stt('dve', S32[:, l, hp, :], S32[:, l, hp, :], decT[:, hp, ci:ci + 1], ub[:, 0:256], ALU.mult, ALU.add,
                              [('S32', l, hp), ('decT', hp), ut], [('S32', l, hp)])
                        ns = 1 - sslot[l][hp]
                        S.copy('act', S_bf[:, l, hp, ns, :], S32[:, l, hp, :], [('S32', l, hp)], [('S_bf', l, hp, ns)])
                        sslot[l][hp] = ns
                        stage('upd0')
                    if cc == 0:
                        conv_a(sbk)
                    else:
                        conv_b(sbk)
                unpin(ot)
                S.copy('act', o32[:, :, tsl], ob.rearrange("p (h c) -> p h c", c=128), [ot], [('o32', h) for h in range(4)])
            if l == 0 and s_ == 0:
                dump('o32', o32, [('o32', h) for h in range(4)])
            stage('gla')
            for ch in range(4):
                S.copy('pool', ubuf[:, l, ch, 0:HALO], ubuf[:, l, ch, NT:NT + HALO], [('ubuf', l, ch)], [('ubuf', l, ch)])
            if l == 0 and s_ == 0:
                dump('cacc', cacc, [('cacc', c) for c in range(4)])
            wt5, wtok5 = wload(b_in[l], 0, KC, 2048, 512)
            for h in range(4):
                rb, rt = nbank()
                mm_acc(rb, [(wt5[:, k, h * 128:(h + 1) * 128], hb[:, k, :]) for k in range(KC)], [wtok5] + hbt, rt)
                S.act(sr[:, h, :], rb, AF.Silu, [rt], [('sr', h)])
            S.act(bufB[:, 0:4, :], o32, AF.Square, [('o32', h) for h in range(4)], [('bufB', k) for k in range(4)],
                  scale=float(128 ** -0.5))
            for h in range(4):
                mb, mt = nbank()
                S.mm(mb, ones_bf, bufB[:, h, :], True, True, ['ones', ('bufB', h)], [mt])
                S.ts('dve', r32, mb, EPS, None, ALU.add, None, [mt], ['r32'])
                S.act(r32, r32, AF.Ln, ['r32'], ['r32'])
                S.act(r32, r32, AF.Exp, ['r32'], ['r32'], scale=-0.5)
                S.tt('dve', o32[:, h, :], o32[:, h, :], r32, ALU.mult, [('o32', h), 'r32'], [('o32', h)])
                S.stt('dve', bufA[:, 4 + h, :], o32[:, h, :], pc[:, c_gg:c_gg + 1], sr[:, h, :], ALU.mult, ALU.mult,
                      [('o32', h), ('sr', h), 'pc'], [('mixin', 4 + h)])
            layer_norm(cacc, [('cacc', c) for c in range(4)], c_clg, c_clb, 4, 1.0 / 512, None, None,
                       bufA[:, 0:4, :], [('bufA', c) for c in range(4)], func=AF.Silu)
            if l == 0 and s_ == 0:
                dump('mixin', bufA, ['bufA'])
            proj_residual_ln(l, b_mo[l], bufA, ['bufA'], KC, hcur, htoks, c_ln1g, c_ln1b)
            if l == 0 and s_ == 0:
                dump('h1', hcur, [('h32', hs)])
            stage('mixer')

            for cg in range(2):
                wt, wtok = wload(b_xq[l], 0, KC, cg * 512, 512)
                for jj in range(4):
                    j = cg * 4 + jj
                    bk, bt = nbank()
                    mm_acc(bk, [(wt[:, k, jj * 128:(jj + 1) * 128], hb[:, k, :]) for k in range(KC)], [wtok] + hbt, bt)
                    S.act(bufB[:, j, :], bk, AF.Identity, [bt], [('qT', j)], scale=1.0 / 16)
            def x_scores(h):
                psl = h % 2
                for mc in range(2):
                    sbk_, st_ = nbank()
                    mm_acc(sbk_, [(KT[:, l, 2 * h + dc, mc * 128:(mc + 1) * 128], bufB[:, 2 * h + dc, :]) for dc in range(2)],
                           [('KT', l), ('qT', 2 * h), ('qT', 2 * h + 1)], st_)
                    S.act(PT[:, psl, mc, :], sbk_, AF.Exp, [st_], [('PT', psl, mc)])

            def x_pv(h):
                psl = h % 2
                db_, dt2 = nbank()
                mm_acc(db_, [(ones_bf, PT[:, psl, mc, :]) for mc in range(2)], ['ones', ('PT', psl, 0), ('PT', psl, 1)], dt2)
                S.op('dve', lambda E, o=r32, i=db_: E.reciprocal(out=o, in_=i), [dt2], ['r32'])
                for dc in range(2):
                    ob_, ot_ = nbank()
                    mm_acc(ob_, [(Vb[:, l, mc, h * 256 + dc * 128:h * 256 + (dc + 1) * 128], PT[:, psl, mc, :]) for mc in range(2)],
                           [('Vb', l), ('PT', psl, 0), ('PT', psl, 1)], ot_)
                    S.tt('dve', bufA[:, 2 * h + dc, :], ob_, r32, ALU.mult, [ot_, 'r32'], [('xo', 2 * h + dc)])

            x_scores(0)
            for h in range(4):
                if h + 1 < 4:
                    x_scores(h + 1)
                x_pv(h)
            if l == 0 and s_ == 0:
                dump('xo', bufA, [('xo', j) for j in range(8)])
            proj_residual_ln(l, b_xo[l], bufA, [('xo', j) for j in range(8)], KC, hcur, htoks, c_ln2g, c_ln2b)
            if l == 0 and s_ == 0:
                dump('h2', hcur, [('h32', hs)])
            stage('xattn')

            for g in range(6):
                ncol = 512 if g < 5 else 256
                wg, wgt = wload(b_fi[l], 0, KC, g * 512, ncol)
                wu, wut = wload(b_fi[l], 0, KC, DFF + g * 512, ncol)
                for jj in range(ncol // 128):
                    j = g * 4 + jj
                    gb, gt = nbank()
                    mm_acc(gb, [(wg[:, k, jj * 128:(jj + 1) * 128], hb[:, k, :]) for k in range(KC)], [wgt] + hbt, gt)
                    ss = j % 2
                    S.act(sg32[:, ss, :], gb, AF.Silu, [gt], [('sg', ss)])
                    ub_, ut_ = nbank()
                    mm_acc(ub_, [(wu[:, k, jj * 128:(jj + 1) * 128], hb[:, k, :]) for k in range(KC)], [wut] + hbt, ut_)
                    S.tt('dve', big_bf[:, j, :], ub_, sg32[:, ss, :], ALU.mult, [ut_, ('sg', ss)], [('hid', j)])
            b1_, t1_ = nbank(pin=True)
            b2_, t2_ = nbank(pin=True)
            st3 = (b1_, t1_, b2_, t2_)
            for cg in range(2):
                accs = [nbank(pin=True) for _ in range(4)]
                for rg in range(3):
                    nk = 8 if rg < 2 else 6
                    wt, wtok = wload(b_fo[l], rg * 1024, nk, cg * 512, 512)
                    for jj in range(4):
                        for k in range(nk):
                            hc = rg * 8 + k
                            S.mm(accs[jj][0], wt[:, k, jj * 128:(jj + 1) * 128], big_bf[:, hc, :], (hc == 0), (hc == HC - 1),
                                 [wtok, ('hid', hc)], [accs[jj][1]])
                for jj in range(4):
                    j = cg * 4 + jj
                    unpin(accs[jj][1])
                    resid_stats(j, hcur, htoks, accs[jj][0], accs[jj][1], st3)
            unpin(t1_)
            unpin(t2_)
            if l < L - 1:
                layer_norm(hcur, htoks, c_ln3g, c_ln3b, KC, 1.0 / D, hcur, htoks, hb, hbt, pre=st3)
            else:
                layer_norm(hcur, htoks, c_ln3g, c_ln3b, KC, 1.0 / D, hcur, htoks, None, None, pre=st3)
        for k in range(KC):
            S.dma('sp', contrib[k * 128:(k + 1) * 128, :], hcur[:, k, :], 'hst0', reads=[htoks[k]], writes=['contrib'])
        S.dma('sp', outT[:, ti_out * NT:(ti_out + 1) * NT].rearrange("(kc p) t -> p kc t", p=128), hcur, 'hst1',
              reads=[('h32', 0)], writes=['out_dram'])
        if s_ == 0:
            for l in range(L):
                for hp in range(2):
                    S.ts('dve', S32[:, l, hp, :], S32[:, l, hp, :], fA, None, ALU.mult, None, [('S32', l, hp), 'fl'], [('S32', l, hp)])
                    for sl in range(2):
                        S.ts('dve', S_bf[:, l, hp, sl, :], S_bf[:, l, hp, sl, :], fA, None, ALU.mult, None,
                             [('S_bf', l, hp, sl), 'fl'], [('S_bf', l, hp, sl)])
                for ch in range(4):
                    S.ts('dve', ubuf[:, l, ch, 0:HALO], ubuf[:, l, ch, 0:HALO], fA, None, ALU.mult, None,
                         [('ubuf', l, ch), 'fl'], [('ubuf', l, ch)])
        if s_ < nsteps - 1:
            S.cc(lambda E, g=groups: E.collective_compute("AllGather", ALU.bypass, replica_groups=g,
                                                          ins=[contrib.opt()], outs=[gath.opt()]),
                 'cc%d' % s_, reads=['contrib'], writes=['gath'])

    S.sbuf_free = nc.sbuf_bytes_remaining
    S.emit(None)
    return nc, S, dump_out


def make_consts():
    s = np.arange(128)[:, None]
    t = np.arange(128)[None, :]
    same = (s // 64) == (t // 64)
    U = np.where(same & (s <= t), -1.0 / 16.0, 0.0).astype(np.float32)
    Lm = np.where(same & (s > t), -1.0 / 16.0, 0.0).astype(np.float32)
    M = np.where(same & (s <= t), 1.0, 0.0).astype(np.float32)
    p = np.arange(128)[:, None]
    m0 = (p < 64).astype(np.float32)
    m1 = (p >= 64).astype(np.float32)
    return np.concatenate([U, Lm, np.tile(M, (1, 4)), m0, m1, 0.125 * m0, 0.125 * m1, np.eye(128, dtype=np.float32)], axis=1).astype(np.float32)


def pack_pcols(inp, layers, ln0_identity):
    pcw = np.zeros((128, NPC), np.float32)

    def fm(v):
        v = np.asarray(v, np.float32)
        return v.reshape(-1, 128).T

    if ln0_identity:
        pcw[:, 0:8] = 1.0
        pcw[:, 8:16] = 0.0
    else:
        pcw[:, 0:8] = fm(inp["ln0_g"])
        pcw[:, 8:16] = fm(inp["ln0_b"])
    for li, l in enumerate(layers):
        P0 = 16 + li * NPL
        for i, nm in enumerate(["ln1_g", "ln1_b", "ln2_g", "ln2_b", "ln3_g", "ln3_b"]):
            pcw[:, P0 + 8 * i:P0 + 8 * i + 8] = fm(inp[nm][l])
        cw = np.asarray(inp["conv_w"][l], np.float32)
        c = P0 + 48
        for ch in range(4):
            pcw[:, c + ch * CONVW:c + (ch + 1) * CONVW] = cw[:, ch * 128:(ch + 1) * 128].T
        c += 124
        pcw[:, c:c + 4] = fm(inp["conv_b"][l]); c += 4
        pcw[:, c:c + 4] = fm(inp["conv_ln_g"][l]); c += 4
        pcw[:, c:c + 4] = fm(inp["conv_ln_b"][l]); c += 4
        pcw[:, c:c + 1] = fm(inp["gla_norm_g"][l])
    return pcw


_CACHE = {}
WNAMES = ["w_in", "w_mix_out", "w_xq", "w_xkv", "w_xo", "w_ffn_in", "w_ffn_out"]


def make_in_maps(inp, nb, T):
    cst = make_consts()
    wa2e_all = np.concatenate([inp["w_a2"], inp["b_a"][:, None, :]], axis=1).astype(np.float32)
    stage = []
    for st in range(2):
        layers = [2 * st, 2 * st + 1]
        d = {nm: np.ascontiguousarray(np.asarray(inp[nm], np.float32)[layers[0]:layers[1] + 1]) for nm in WNAMES}
        d["pcols"] = pack_pcols(inp, layers, ln0_identity=(st == 1))
        d["wa2e"] = np.ascontiguousarray(wa2e_all[layers[0]:layers[1] + 1])
        d["consts"] = cst
        fl = np.zeros((128, 4), np.float32)
        fl[:, 0] = 1.0 if st == 0 else 0.0
        fl[:, 1] = 0.0 if st == 0 else 1.0
        d["flags"] = fl
        stage.append(d)
    in_maps = []
    for b in range(nb):
        xT = np.ascontiguousarray(np.asarray(inp["x"][b][:T]).T, np.float32)
        mT = np.ascontiguousarray(np.asarray(inp["mem"][b]).T, np.float32)
        for st in range(2):
            m = dict(stage[st])
            m["xT"] = xT
            m["memT"] = mT
            in_maps.append(m)
    return in_maps


def kernel(**inputs):
    inp = {k: np.asarray(v) for k, v in inputs.items()}
    x = inp["x"]
    B, T, _ = x.shape
    key = (T, 2 * B)
    if key not in _CACHE:
        _CACHE[key] = build_program(LSLOTS, T, ncores=2 * B)
    nc, S, _ = _CACHE[key]
    in_maps = make_in_maps(inp, B, T)
    res = run_bass_kernel_spmd(nc, in_maps, core_ids=list(range(2 * B)))
    out = np.stack([np.ascontiguousarray(res.results[2 * b + 1]["outT"].T) for b in range(B)], axis=0)
    return out.astype(np.float32)
```

```python
import numpy as np
import concourse.bass as bass
import concourse.mybir as mybir
from concourse.bass_utils import run_bass_kernel_spmd

F32 = mybir.dt.float32
BF16 = mybir.dt.bfloat16
AF = mybir.ActivationFunctionType
ALU = mybir.AluOpType

D = 1024
KC = 8
NT = 512
DFF = 2816
HC = 22
INC = 2576
MEM = 256
DEPTH = 4
SEQ = 8192
ALPHA = float(8 ** 0.25)
EPS = 1e-5
CONVW = 31
HALO = 30
NPL = 185
LSLOTS = 2
NPC = LSLOTS * NPL + 16

SAME_ENG_SYNC = True
DG_ENG = 'pool'


class Sched:
    def __init__(self, nc):
        self.nc = nc
        self.eng = {'pe': nc.tensor, 'act': nc.scalar, 'dve': nc.vector, 'pool': nc.gpsimd, 'sp': nc.sync}
        self.ops = []
        self.alias = {}
        self.sems = {}
        self.bank_i = 0
        self.frozen = False

    def op(self, eng, fn, reads=(), writes=()):
        if self.frozen:
            return
        self.ops.append(dict(eng=eng, fn=fn, r=list(reads), w=list(writes), dma=None))

    def dma(self, eng, out, in_, stream, reads=(), writes=()):
        if self.frozen:
            return
        self.ops.append(dict(eng=eng, fn=(lambda E, o=out, i=in_: E.dma_start(out=o, in_=i)),
                             r=list(reads), w=list(writes), dma=stream))

    def cc(self, fn, key, reads=(), writes=()):
        if self.frozen:
            return
        self.ops.append(dict(eng='pool', fn=fn, r=list(reads), w=list(writes), dma=key, inc=1))

    def mm(self, out, lhsT, rhs, start, stop, reads, writes, sgc=False):
        if sgc:
            self.op('pe', lambda E, o=out, l=lhsT, r=rhs, a=start, b=stop: E.matmul(o, l, r, start=a, stop=b, skip_group_check=True),
                    reads, writes)
        else:
            self.op('pe', lambda E, o=out, l=lhsT, r=rhs, a=start, b=stop: E.matmul(o, l, r, start=a, stop=b),
                    reads, writes)

    def act(self, out, in_, func, reads, writes, bias=0.0, scale=1.0):
        self.op('act', lambda E, o=out, i=in_, f=func, b=bias, s=scale: E.activation(out=o, in_=i, func=f, bias=b, scale=s),
                reads, writes)

    def tt(self, eng, out, in0, in1, op, reads, writes):
        self.op(eng, lambda E, o=out, a=in0, b=in1, p=op: E.tensor_tensor(out=o, in0=a, in1=b, op=p), reads, writes)

    def ts(self, eng, out, in0, s1, s2, op0, op1, reads, writes):
        if s2 is None:
            self.op(eng, lambda E, o=out, a=in0, x=s1, p=op0: E.tensor_scalar(out=o, in0=a, scalar1=x, scalar2=None, op0=p),
                    reads, writes)
        else:
            self.op(eng, lambda E, o=out, a=in0, x=s1, y=s2, p=op0, q=op1: E.tensor_scalar(out=o, in0=a, scalar1=x, scalar2=y, op0=p, op1=q),
                    reads, writes)

    def fence(self, eng, streams):
        if self.frozen:
            return
        self.ops.append(dict(eng=eng, fn=None, r=[], w=[], dma=None, fence=list(streams)))

    def stt(self, eng, out, in0, scalar, in1, op0, op1, reads, writes):
        self.op(eng, lambda E, o=out, a=in0, s=scalar, b=in1, p=op0, q=op1: E.scalar_tensor_tensor(out=o, in0=a, scalar=s, in1=b, op0=p, op1=q),
                reads, writes)

    def copy(self, eng, out, in_, reads, writes):
        if eng == 'act':
            self.act(out, in_, AF.Copy, reads, writes)
        else:
            self.op(eng, lambda E, o=out, i=in_: E.tensor_copy(out=o, in_=i), reads, writes)

    def memset(self, eng, ap, val, writes):
        self.op(eng, lambda E, a=ap, v=val: E.memset(a, v), (), writes)

    def _expand(self, toks):
        out = []
        for t in toks:
            a = self.alias.get(t)
            if a is None:
                out.append(t)
            else:
                out.extend(a)
        return out

    def _sem(self, v):
        s = self.sems.get(v)
        if s is None:
            s = self.nc.alloc_semaphore(name=("s_" + str(v)).replace("'", "").replace(" ", "").replace("(", "").replace(")", "").replace(",", "_"))
            self.sems[v] = s
        return s

    def emit(self, final_streams):
        ops = self.ops
        lastw = {}
        readers = {}
        last_in_stream = {}
        for i, o in enumerate(ops):
            v = ('dma', o['dma']) if o['dma'] else o['eng']
            o['v'] = v
            o['needed'] = False
            deps = set()
            R = self._expand(o['r'])
            W = self._expand(o['w'])
            for t in R:
                j = lastw.get(t)
                if j is not None:
                    deps.add(j)
            for t in W:
                j = lastw.get(t)
                if j is not None:
                    deps.add(j)
                for j in readers.get(t, {}).values():
                    deps.add(j)
            if o['dma']:
                j = last_in_stream.get(o['dma'])
                if j is not None:
                    deps.add(j)
                last_in_stream[o['dma']] = i
            deps.discard(i)
            dl = []
            for j in deps:
                vj = ops[j]['v']
                if vj == 'pe' and v == 'pe':
                    continue
                if (not SAME_ENG_SYNC) and vj == v and not o['dma']:
                    continue
                dl.append(j)
                ops[j]['needed'] = True
            o['deps'] = dl
            for t in W:
                lastw[t] = i
                readers[t] = {}
            for t in R:
                readers.setdefault(t, {})[v] = i
        cnt = {}
        seen = {e: {} for e in self.eng}
        for o in ops:
            e = o['eng']
            E = self.eng[e]
            v = o['v']
            if o['fn'] is None:
                for st in o['fence']:
                    vv = ('dma', st)
                    c = cnt.get(vv, 0)
                    if c and seen[e].get(vv, 0) < c:
                        E.wait_ge(self._sem(vv), c)
                        seen[e][vv] = c
                o['cnt'] = None
                continue
            need = {}
            for j in o['deps']:
                dj = ops[j]
                need[dj['v']] = max(need.get(dj['v'], 0), dj['cnt'])
            for vv, c in need.items():
                if seen[e].get(vv, 0) < c:
                    E.wait_ge(self._sem(vv), c)
                    seen[e][vv] = c
            ins = o['fn'](E)
            if o['dma']:
                inc = o.get('inc', 16)
                cnt[v] = cnt.get(v, 0) + inc
                ins.then_inc(self._sem(v), inc)
                o['cnt'] = cnt[v]
            elif o['needed']:
                cnt[v] = cnt.get(v, 0) + 1
                ins.then_inc(self._sem(v), 1)
                o['cnt'] = cnt[v]
            else:
                o['cnt'] = None
        for v in list(cnt):
            if isinstance(v, tuple) and v[0] == 'dma':
                self.eng['sp'].wait_ge(self._sem(v), cnt[v])
        self.max_cnt = dict(cnt)


def build_program(L=LSLOTS, TT=SEQ, dumps=(), stop_after=None, ncores=8):
    ntiles = TT // NT
    nsteps = ntiles + 1
    nc = bass.Bass("TRN2", target_bir_lowering=False)
    S = Sched(nc)
    dumps = set(dumps)
    dump_out = {}

    def din(name, shape, dt=F32):
        return nc.dram_tensor(name, list(shape), dt, kind="ExternalInput").ap()

    xT = din("xT", [D, TT])
    memT = din("memT", [D, MEM])
    w_in = din("w_in", [L, D, INC])
    w_mo = din("w_mix_out", [L, D, D])
    w_xq = din("w_xq", [L, D, D])
    w_xkv = din("w_xkv", [L, D, 2 * D])
    w_xo = din("w_xo", [L, D, D])
    w_fi = din("w_ffn_in", [L, D, 2 * DFF])
    w_fo = din("w_ffn_out", [L, DFF, D])
    pcols = din("pcols", [128, NPC])
    wa2e = din("wa2e", [L, 17, 256])
    consts = din("consts", [128, 900])
    flags = din("flags", [128, 4])
    outT = nc.dram_tensor("outT", [D, TT], F32, kind="ExternalOutput").ap()

    def dscr(name, shape, dt):
        return nc.dram_tensor(name, list(shape), dt).ap()

    contrib = dscr("contrib", [D, NT], F32)
    gath = dscr("gath", [2 * D, NT], F32)
    b_in = dscr("b_in", [L, D, INC], BF16)
    b_mo = dscr("b_mo", [L, D, D], BF16)
    b_xq = dscr("b_xq", [L, D, D], BF16)
    b_xkv = dscr("b_xkv", [L, D, 2 * D], BF16)
    b_xo = dscr("b_xo", [L, D, D], BF16)
    b_fi = dscr("b_fi", [L, D, 2 * DFF], BF16)
    b_fo = dscr("b_fo", [L, DFF, D], BF16)

    def sb(name, shape, dt):
        return nc.alloc_sbuf_tensor(name, list(shape), dt).ap()

    pc = sb("pc", [128, NPC], F32)
    cst = sb("cst", [128, 900], F32)
    Uneg = cst[:, 0:128]
    Lneg = cst[:, 128:256]
    mask4 = cst[:, 256:768]
    ones_bf = sb("ones_bf", [128, 128], BF16)
    wa2_f = sb("wa2_f", [32, 256], F32)
    wa2_b = sb("wa2_b", [32, L, 256], BF16)
    KT = sb("KT", [128, L, KC, MEM], BF16)
    Vb = sb("Vb", [128, L, 2, D], BF16)
    fl = sb("fl", [128, 4], F32)
    fA = fl[:, 0:1]
    fB = fl[:, 1:2]
    h32 = [sb("h32_%d" % i, [128, KC, NT], F32) for i in range(1)]
    hb = sb("hb", [128, KC, NT], BF16)
    NSLOT = 3
    wring = [sb("wr_%d" % i, [128, KC, 512], BF16) for i in range(NSLOT)]
    bufA = sb("bufA", [128, KC, NT], BF16)
    bufB = sb("bufB", [128, KC, NT], BF16)
    bigf = sb("big", [128, 12 * 512], F32)
    big = bigf.rearrange("p (a b) -> p a b", b=512)
    big_bf = bigf.bitcast(BF16).rearrange("p (j n) -> p j n", n=512)
    alT = sb("alT", [32, NT], BF16)
    lsb = sb("lsb", [128, 2, 256], F32)
    l_hi = sb("l_hi", [128, 4, 256], BF16)
    l_lo = sb("l_lo", [128, 4, 256], BF16)
    cst_b = sb("cst_b", [128, 384], BF16)
    ident_b = cst_b[:, 256:384]
    dg = sb("dg", [128, 4, CONVW, 128], BF16)
    Uneg_b = cst_b[:, 0:128]
    Lneg_b = cst_b[:, 128:256]
    ez = sb("ez", [128, 2, 256], F32)
    Ed = sb("Ed", [128, 4, 256], F32)
    EbT = sb("EbT", [128, 2, NT], F32)
    EnT = sb("EnT", [128, 2, NT], F32)
    sg32 = EbT
    decT = sb("decT", [128, 2, 8], F32)
    v_bf = sb("v_bf", [128, 4, 512], BF16)
    mem_b = v_bf.rearrange("p a b -> p (a b)").rearrange("p (k m) -> p k m", m=MEM)
    kd_bf = sb("kd_bf", [128, 4, 2, 256], BF16)
    qeT = sb("qeT", [128, 2, 2, NT], BF16)
    keT = sb("keT", [128, 2, NT], BF16)
    att_bf = sb("att_bf", [128, 2, 512], BF16)
    S32 = sb("S32", [128, L, 2, 256], F32)
    S_bf = sb("S_bf", [128, L, 2, 2, 256], BF16)
    ubuf = sb("ubuf", [128, L, 4, NT + HALO], BF16)
    m32 = sb("m32", [128, NT], F32)
    t32 = sb("t32", [128, NT], F32)
    r32 = sb("r32", [128, NT], F32)
    PT = EnT.rearrange("p a b -> p (a b)").bitcast(BF16).rearrange("p (s m t) -> p s m t", s=2, m=2)
    o32 = big[:, 0:4, :]
    cacc = big[:, 4:8, :]
    sr = big[:, 8:12, :]
    banks = [nc.alloc_psum_tensor("bank%d" % i, [128, 512], F32).ap() for i in range(8)]

    S.alias['bufA'] = [('bufA', k) for k in range(KC)]
    S.alias['bufB'] = [('bufB', k) for k in range(KC)]
    for k in range(KC):
        S.alias[('xo', k)] = [('bufA', k)]
        S.alias[('qT', k)] = [('bufB', k)]
    for k in range(4):
        S.alias[('mixin', 4 + k)] = [('bufA', 4 + k)]
    pinned = set()
    for i_ in range(2):
        S.alias[('sg', i_)] = [('EbT', i_)]
        for m_ in range(2):
            S.alias[('PT', i_, m_)] = [('EnT', i_)]
    for hs_ in range(1):
        S.alias[('h32', hs_)] = [('h32c', hs_, k) for k in range(KC)]
    S.alias['bigall'] = [('big', u) for u in range(12)]
    for h in range(4):
        S.alias[('o32', h)] = [('big', h)]
        S.alias[('cacc', h)] = [('big', 4 + h)]
        S.alias[('sr', h)] = [('big', 8 + h)]
    for j in range(HC):
        S.alias[('hid', j)] = [('big', j // 2)]

    def nbank(pin=False):
        i = S.bank_i
        while i in pinned:
            i = (i + 1) % 8
        S.bank_i = (i + 1) % 8
        if pin:
            pinned.add(i)
        return banks[i], ('bank', i)

    def unpin(tok):
        pinned.discard(tok[1])

    def stage(name):
        if stop_after == name:
            S.frozen = True

    def dump(name, ap, reads):
        if name not in dumps or S.frozen:
            return
        d = nc.dram_tensor("dbg_" + name, list(ap.shape), ap.dtype, kind="ExternalOutput").ap()
        dump_out[name] = d
        S.dma('sp', d, ap, 'dbg_' + name, reads=reads, writes=[('dbg', name)])

    S.dma('sp', pc, pcols, 'misc', writes=['pc'])
    S.dma('sp', cst, consts, 'misc', writes=['cst'])
    S.dma('sp', fl, flags, 'misc', writes=['fl'])
    S.memset('dve', ones_bf, 1.0, ['ones'])
    S.copy('dve', cst_b[:, 0:256], cst[:, 0:256], ['cst'], ['cst_b'])
    S.copy('dve', cst_b[:, 256:384], cst[:, 772:900], ['cst'], ['cst_b'])
    S.memset('pool', wa2_b, 0.0, [('wa2_b', l_) for l_ in range(L)])
    S.memset('pool', alT, 1.0, ['alT'])

    cast_rr = [0]
    stage_i = [0]

    def convert(src, dst, R, C):
        cw = 4096
        for rc in range(R // 128):
            c0 = 0
            while c0 < C:
                n = min(cw, C - c0)
                si = stage_i[0] % 2
                stage_i[0] += 1
                s32 = (h32[0].rearrange("p a b -> p (a b)") if si == 0 else bigf)[:, 0:n]
                sbf = (bufA if si == 0 else bufB).rearrange("p a b -> p (a b)")[:, 0:n]
                t32 = ('h32', 0) if si == 0 else 'bigall'
                tbf = 'bufA' if si == 0 else 'bufB'
                S.dma('sp', s32, src[rc * 128:(rc + 1) * 128, c0:c0 + n], 'cv_in%d' % si, writes=[t32])
                e = ['act', 'dve'][cast_rr[0] % 2]
                cast_rr[0] += 1
                S.copy(e, sbf, s32, [t32], [tbf])
                S.dma('sp', dst[rc * 128:(rc + 1) * 128, c0:c0 + n], sbf, 'cv_out%d' % si, reads=[tbf], writes=[])
                c0 += n

    wsets = [(w_in, b_in, D, INC), (w_mo, b_mo, D, D), (w_xq, b_xq, D, D), (w_xkv, b_xkv, D, 2 * D),
             (w_xo, b_xo, D, D), (w_fi, b_fi, D, 2 * DFF), (w_fo, b_fo, DFF, D)]
    wb_tok = {}
    for l in range(L):
        for (src, dst, R, C) in wsets:
            convert(src[l], dst[l], R, C)
    S.fence('sp', ['cv_out0', 'cv_out1'])
    stage('prologue')

    ring_i = [0]

    def wload(src2d, r0, nk, c0, ncols):
        s = ring_i[0] % NSLOT
        ring_i[0] += 1
        dst = wring[s][:, 0:nk, 0:ncols]
        S.dma('sp', dst, src2d[r0:r0 + nk * 128, c0:c0 + ncols].rearrange("(kc p) c -> p kc c", p=128),
              'wr%d' % s, reads=[], writes=[('wr', s)])
        return wring[s], ('wr', s)

    def mm_acc(out, pairs, reads, btok):
        n = len(pairs)
        for i, (l, r) in enumerate(pairs):
            S.mm(out, l, r, i == 0, i == n - 1, reads, [btok])

    def layer_norm(y, ytok_list, gcol, bcol, nk, inv_n, out32, out32_tok, outb, outb_tok, func=AF.Identity,
                   flagged=False, pre=None):
        if pre is None:
            ybv = bufA[:, 0:nk, :]
            ysv = bufB[:, 0:nk, :]
            ta = [('bufA', k) for k in range(nk)]
            tb = [('bufB', k) for k in range(nk)]
            S.copy('dve', ybv, y, ytok_list, ta)
            S.act(ysv, y, AF.Square, ytok_list, tb)
            b1, t1 = nbank()
            mm_acc(b1, [(ones_bf, ybv[:, k, :]) for k in range(nk)], ['ones'] + ta, t1)
            b2, t2 = nbank()
            mm_acc(b2, [(ones_bf, ysv[:, k, :]) for k in range(nk)], ['ones'] + tb, t2)
        else:
            b1, t1, b2, t2 = pre
        S.ts('dve', m32, b1, inv_n, None, ALU.mult, ALU.bypass, [t1], ['m32'])
        if flagged:
            S.ts('dve', m32, m32, fA, None, ALU.mult, None, ['m32', 'fl'], ['m32'])
        S.tt('dve', t32, m32, m32, ALU.mult, ['m32'], ['t32'])
        S.stt('dve', r32, b2, inv_n, t32, ALU.mult, ALU.subtract, [t2, 't32'], ['r32'])
        S.ts('dve', r32, r32, EPS, None, ALU.add, None, ['r32'], ['r32'])
        S.act(r32, r32, AF.Ln, ['r32'], ['r32'])
        S.act(r32, r32, AF.Exp, ['r32'], ['r32'], scale=-0.5)
        if flagged:
            S.ts('dve', r32, r32, fA, fB, ALU.mult, ALU.add, ['r32', 'fl'], ['r32'])
        for k in range(nk):
            S.tt('dve', y[:, k, :], y[:, k, :], m32, ALU.subtract, [ytok_list[k], 'm32'], [ytok_list[k]])
        for k in range(nk):
            S.tt('dve', y[:, k, :], y[:, k, :], r32, ALU.mult, [ytok_list[k], 'r32'], [ytok_list[k]])
            if outb is not None:
                S.act(outb[:, k, :], y[:, k, :], func, [ytok_list[k], 'pc'], [outb_tok[k]],
                      bias=pc[:, bcol + k:bcol + k + 1], scale=pc[:, gcol + k:gcol + k + 1])
        if out32 is not None:
            for k in range(nk):
                S.act(out32[:, k, :], y[:, k, :], func, [ytok_list[k], 'pc'], [out32_tok[k]],
                      bias=pc[:, bcol + k:bcol + k + 1], scale=pc[:, gcol + k:gcol + k + 1])

    def resid_stats(j, hcur, htoks, acc_bank, acc_tok, st):
        b1, t1, b2, t2 = st
        S.stt('dve', hcur[:, j, :], hcur[:, j, :], ALPHA, acc_bank, ALU.mult, ALU.add, [htoks[j], acc_tok], [htoks[j]])
        S.copy('dve', hb[:, j, :], hcur[:, j, :], [htoks[j]], [('hb', j)])
        S.act(bufB[:, j, :], hcur[:, j, :], AF.Square, [htoks[j]], [('bufB', j)])
        S.mm(b1, ones_bf, hb[:, j, :], (j == 0), (j == KC - 1), ['ones', ('hb', j)], [t1], sgc=True)
        S.mm(b2, ones_bf, bufB[:, j, :], (j == 0), (j == KC - 1), ['ones', ('bufB', j)], [t2], sgc=True)

    def proj_residual_ln(l, wsrc, rhs_buf, rhs_tok, nkin, hcur, htoks, gcol, bcol):
        b1, t1 = nbank(pin=True)
        b2, t2 = nbank(pin=True)
        st = (b1, t1, b2, t2)
        for cg in range(2):
            wt, wtok = wload(wsrc, 0, nkin, cg * 512, 512)
            for jj in range(4):
                j = cg * 4 + jj
                bk, bt = nbank()
                mm_acc(bk, [(wt[:, k, jj * 128:(jj + 1) * 128], rhs_buf[:, k, :]) for k in range(nkin)],
                       [wtok] + rhs_tok, bt)
                resid_stats(j, hcur, htoks, bk, bt, st)
        unpin(t1)
        unpin(t2)
        layer_norm(hcur, htoks, gcol, bcol, KC, 1.0 / D, hcur, htoks, hb, [('hb', k) for k in range(KC)], pre=st)

    hbt = [('hb', k) for k in range(KC)]

    hbt = [('hb', k) for k in range(KC)]
    mem32 = big[:, 0:4, :].rearrange("p a b -> p (a b)").rearrange("p (k m) -> p k m", m=MEM)
    S.dma('sp', mem32, memT.rearrange("(kc p) m -> p kc m", p=128), 'misc', writes=[('o32', h) for h in range(4)])
    S.copy('dve', mem_b, mem32, [('o32', h) for h in range(4)], [('v_bf', i) for i in range(4)])
    S.alias['mem_b'] = [('v_bf', i) for i in range(4)]
    sslot = [[0, 0] for _ in range(L)]
    for l in range(L):
        S.dma('sp', wa2_f[0:17, :], wa2e[l], 'misc', writes=['wa2_f'])
        S.copy('dve', wa2_b[0:17, l, :], wa2_f[0:17, :], ['wa2_f'], [('wa2_b', l)])
        S.memset('dve', S32[:, l, :, :], 0.0, [('S32', l, 0), ('S32', l, 1)])
        S.memset('pool', S_bf[:, l, :, :, :], 0.0, [('S_bf', l, a, b) for a in range(2) for b in range(2)])
        S.memset('pool', ubuf[:, l, :, :], 0.0, [('ubuf', l, c) for c in range(4)])
        for cg in range(2):
            wt, wtok = wload(b_xkv[l], 0, KC, cg * 512, 512)
            for jj in range(4):
                j = cg * 4 + jj
                bk, bt = nbank()
                mm_acc(bk[:, 0:MEM], [(wt[:, k, jj * 128:(jj + 1) * 128], mem_b[:, k, :]) for k in range(KC)],
                       [wtok, 'mem_b'], bt)
                S.copy('act', KT[:, l, j, :], bk[:, 0:MEM], [bt], [('KT', l)])
        for cg in range(2):
            wt, wtok = wload(b_xkv[l], 0, KC, D + cg * 512, 512)
            for mc in range(2):
                bk, bt = nbank()
                mm_acc(bk, [(mem_b[:, k, mc * 128:(mc + 1) * 128], wt[:, k, :]) for k in range(KC)],
                       [wtok, 'mem_b'], bt)
                S.copy('act', Vb[:, l, mc, cg * 512:(cg + 1) * 512], bk, [bt], [('Vb', l)])
    S.dma('sp', gath[0:D, :], xT[:, 0:NT], 'misc', writes=['gath'])
    groups = [[2 * i, 2 * i + 1] for i in range(ncores // 2)]
    hcur = h32[0]
    htoks = [('h32c', 0, k) for k in range(KC)]
    hs = 0
    xa32 = bufA.rearrange("p a b -> p (a b)").bitcast(F32).rearrange("p (k t) -> p k t", t=NT)
    xb32 = bufB.rearrange("p a b -> p (a b)").bitcast(F32).rearrange("p (k t) -> p k t", t=NT)

    def build_dg(l_):
        c_cw_ = 16 + l_ * NPL + 48
        for ch in range(4):
            for k in range(CONVW):
                wc_ = pc[:, c_cw_ + ch * CONVW + k:c_cw_ + ch * CONVW + k + 1]
                if ch < 2:
                    S.act(dg[:, ch, k, :], ident_b, AF.Identity, ['cst_b', 'pc'], [('dg', ch)], scale=wc_)
                else:
                    S.ts('dve', dg[:, ch, k, :], ident_b, wc_, None, ALU.mult, None, ['cst_b', 'pc'], [('dg', ch)])

    build_dg(0)
    for s_ in range(nsteps):
        ti_in = min(s_, ntiles - 1)
        ti_out = max(s_ - 1, 0)
        for half in range(2):
            ksl = slice(half * 4, (half + 1) * 4)
            S.dma('sp', xa32, xT[half * 512:(half + 1) * 512, ti_in * NT:(ti_in + 1) * NT].rearrange("(kc p) t -> p kc t", p=128),
                  'xin', writes=['bufA'])
            S.dma('sp', xb32, gath[half * 512:(half + 1) * 512, :].rearrange("(kc p) t -> p kc t", p=128),
                  'gin', reads=['gath'], writes=['bufB'])
            S.ts('dve', hcur[:, ksl, :], xa32, fA, None, ALU.mult, None, ['bufA', 'fl'], htoks[half * 4:(half + 1) * 4])
            S.stt('dve', hcur[:, ksl, :], xb32, fB, hcur[:, ksl, :], ALU.mult, ALU.add,
                  ['bufB', 'fl'] + htoks[half * 4:(half + 1) * 4], htoks[half * 4:(half + 1) * 4])
        layer_norm(hcur, htoks, 0, 8, KC, 1.0 / D, hcur, htoks, hb, hbt, flagged=True)
        for l in range(L):
            P0 = 16 + l * NPL
            c_ln1g, c_ln1b, c_ln2g, c_ln2b, c_ln3g, c_ln3b = [P0 + 8 * i for i in range(6)]
            c_cw = P0 + 48
            c_cb = c_cw + 124
            c_clg = c_cb + 4
            c_clb = c_clg + 4
            c_gg = c_clb + 4

            wt6, wtok6 = wload(b_in[l], 0, KC, 2560, 16)
            bk, bt = nbank()
            mm_acc(bk[0:16, :], [(wt6[:, k, 0:16], hb[:, k, :]) for k in range(KC)], [wtok6] + hbt, bt)
            S.copy('act', alT[0:16, :], bk[0:16, :], [bt], ['alT'])
            bc0, bct0 = nbank(pin=True)
            bc1, bct1 = nbank(pin=True)
            bcs = [(bc0, bct0), (bc1, bct1)]
            for sbk in range(4):
                tsl = slice(sbk * 128, (sbk + 1) * 128)
                zb, zt = nbank()
                S.mm(zb[:, 0:256], alT[0:17, tsl], wa2_b[0:17, l, :], True, True, ['alT', ('wa2_b', l)], [zt])
                es = sbk % 2
                S.act(ez[:, es, :], zb[:, 0:256], AF.Exp, [zt], [('ez', es)], scale=-1.0)
                S.act(lsb[:, es, :], ez[:, es, :], AF.Ln, [('ez', es)], [('lsb', es)], bias=1.0)
                S.copy('dve', l_hi[:, sbk, :], lsb[:, es, :], [('lsb', es)], [('l_hi', sbk)])
                S.tt('dve', l_lo[:, sbk, :], lsb[:, es, :], l_hi[:, sbk, :], ALU.subtract, [('lsb', es), ('l_hi', sbk)], [('l_lo', sbk)])
                for fc in range(2):
                    S.mm(bcs[fc][0][:, tsl], l_hi[:, sbk, fc * 128:(fc + 1) * 128], Uneg_b, (sbk == 0), False,
                         [('l_hi', sbk), 'cst_b'], [bcs[fc][1]], sgc=True)
                    S.mm(bcs[fc][0][:, tsl], l_lo[:, sbk, fc * 128:(fc + 1) * 128], Uneg_b, False, True,
                         [('l_lo', sbk), 'cst_b'], [bcs[fc][1]], sgc=True)
                db, dt_ = nbank()
                S.mm(db[:, 0:256], Lneg_b, l_hi[:, sbk, :], True, False, [('l_hi', sbk), 'cst_b'], [dt_])
                S.mm(db[:, 0:256], Lneg_b, l_lo[:, sbk, :], False, True, [('l_lo', sbk), 'cst_b'], [dt_])
                S.act(Ed[:, sbk, :], db[:, 0:256], AF.Exp, [dt_], [('Ed', sbk)])
            for fc in range(2):
                S.act(EbT[:, fc, :], bcs[fc][0], AF.Exp, [bcs[fc][1]], [('EbT', fc)])
                S.act(EnT[:, fc, :], bcs[fc][0], AF.Exp, [bcs[fc][1]], [('EnT', fc)], scale=-1.0)
                S.act(decT[:, fc, :], bcs[fc][0].rearrange("p (c t) -> p c t", t=64)[:, :, 63], AF.Exp,
                      [bcs[fc][1]], [('decT', fc)])
            unpin(bct0)
            unpin(bct1)
            stage('gates')
            if l > 0:
                build_dg(l)
            wt3, wtok3 = wload(b_in[l], 0, KC, 1024, 512)
            wt4, wtok4 = wload(b_in[l], 0, KC, 1536, 512)
            for sbk in range(4):
                tsl = slice(sbk * 128, (sbk + 1) * 128)
                vb_, vt_ = nbank()
                mm_acc(vb_, [(hb[:, k, tsl], wt4[:, k, :]) for k in range(KC)], [wtok4] + hbt, vt_)
                S.copy('act', v_bf[:, sbk, :], vb_, [vt_], [('v_bf', sbk)])
                kb_, kt_ = nbank()
                mm_acc(kb_[:, 0:256], [(hb[:, k, tsl], wt3[:, k, 256:512]) for k in range(KC)], [wtok3] + hbt, kt_)
                for cc in range(2):
                    S.stt('dve', kd_bf[:, sbk, cc, :], kb_[:, 0:256], cst[:, 768 + cc:769 + cc], Ed[:, sbk, :], ALU.mult, ALU.mult,
                          [kt_, ('Ed', sbk), 'cst'], [('kd_bf', sbk)])
            stage('kv')
            for fc in range(2):
                qb_, qt_ = nbank()
                mm_acc(qb_, [(wt3[:, k, fc * 128:(fc + 1) * 128], hb[:, k, :]) for k in range(KC)], [wtok3] + hbt, qt_)
                for hl in range(2):
                    S.stt('dve', qeT[:, fc, hl, :], qb_, cst[:, 770 + hl:771 + hl], EbT[:, fc, :], ALU.mult, ALU.mult,
                          [qt_, ('EbT', fc), 'cst'], [('qeT', fc)])
                kb_, kt_ = nbank()
                mm_acc(kb_, [(wt3[:, k, 256 + fc * 128:256 + (fc + 1) * 128], hb[:, k, :]) for k in range(KC)], [wtok3] + hbt, kt_)
                S.tt('dve', keT[:, fc, :], kb_, EnT[:, fc, :], ALU.mult, [kt_, ('EnT', fc)], [('keT', fc)])
            if l == 0 and s_ == 0:
                dump('qeT', qeT, [('qeT', 0), ('qeT', 1)])
                dump('keT', keT, [('keT', 0), ('keT', 1)])
                dump('kd', kd_bf, [('kd_bf', i) for i in range(4)])
                dump('vbf', v_bf, [('v_bf', i) for i in range(4)])
                dump('decT', decT, [('decT', 0), ('decT', 1)])
            stage('qk')
            wt1, wtok1 = wload(b_in[l], 0, KC, 0, 512)
            wt2, wtok2 = wload(b_in[l], 0, KC, 512, 512)

            def conv_a(ch):
                gb, gt = nbank()
                mm_acc(gb, [(wt2[:, k, ch * 128:(ch + 1) * 128], hb[:, k, :]) for k in range(KC)], [wtok2] + hbt, gt)
                ss = ch % 2
                S.act(sg32[:, ss, :], gb, AF.Sigmoid, [gt], [('sg', ss)])
                ab_, at_ = nbank()
                mm_acc(ab_, [(wt1[:, k, ch * 128:(ch + 1) * 128], hb[:, k, :]) for k in range(KC)], [wtok1] + hbt, at_)
                S.tt('dve', ubuf[:, l, ch, HALO:HALO + NT], ab_, sg32[:, ss, :], ALU.mult, [at_, ('sg', ss)], [('ubuf', l, ch)])

            def conv_b(ch):
                cb_, ct_ = nbank()
                mm_acc(cb_, [(dg[:, ch, k, :], ubuf[:, l, ch, k:k + NT]) for k in range(CONVW)], [('dg', ch), ('ubuf', l, ch)], ct_)
                S.act(cacc[:, ch, :], cb_, AF.Identity, [ct_, 'pc'], [('cacc', ch)], bias=pc[:, c_cb + ch:c_cb + ch + 1])
            for sbk in range(4):
                tsl = slice(sbk * 128, (sbk + 1) * 128)
                ab, at = nbank()
                for h in range(4):
                    fc, hl = h // 2, h % 2
                    S.mm(ab[:, h * 128:(h + 1) * 128], keT[:, fc, tsl], qeT[:, fc, hl, tsl],
                         True, True, [('keT', fc), ('qeT', fc)], [at])
                asl = sbk % 2
                S.tt('dve', att_bf[:, asl, :], ab, mask4, ALU.mult, [at, 'cst'], [('att', asl)])
                stage('att')
                ob, ot = nbank(pin=True)
                for h in range(4):
                    S.mm(ob[:, h * 128:(h + 1) * 128], v_bf[:, sbk, h * 128:(h + 1) * 128], att_bf[:, asl, h * 128:(h + 1) * 128],
                         (h == 0), False, [('v_bf', sbk), ('att', asl)], [ot], sgc=True)
                stage('intra')
                for cc in range(2):
                    ci = sbk * 2 + cc
                    csl = slice(sbk * 128 + cc * 64, sbk * 128 + (cc + 1) * 64)
                    for h in range(4):
                        fc, hl = h // 2, h % 2
                        sl = sslot[l][fc]
                        S.mm(ob[:, h * 128 + cc * 64:h * 128 + (cc + 1) * 64],
                             S_bf[:, l, fc, sl, hl * 128:(hl + 1) * 128],
                             qeT[:, fc, hl, csl], False, (cc == 1),
                             [('S_bf', l, fc, sl), ('qeT', fc)], [ot], sgc=True)
                    stage('inter0')
                    for hp in range(2):
                        ub, ut = nbank()
                        S.mm(ub[:, 0:256], kd_bf[:, sbk, cc, hp * 128:(hp + 1) * 128],
                             v_bf[:, sbk, hp * 256:(hp + 1) * 256], True, True,
                             [('kd_bf', sbk), ('v_bf', sbk)], [ut])
                        S.stt('dve', S32[:, l, hp, :], S32[:, l, hp, :], decT[:, hp, ci:ci + 1], ub[:, 0:256], ALU.mult, ALU.add,
                              [('S32', l, hp), ('decT', hp), ut], [('S32', l, hp)])
                        ns = 1 - sslot[l][hp]
                        S.copy('act', S_bf[:, l, hp, ns, :], S32[:, l, hp, :], [('S32', l, hp)], [('S_bf', l, hp, ns)])
                        sslot[l][hp] = ns
                        stage('upd0')
                    if cc == 0:
                        conv_a(sbk)
                    else:
                        conv_b(sbk)
                unpin(ot)
                S.copy('act', o32[:, :, tsl], ob.rearrange("p (h c) -> p h c", c=128), [ot], [('o32', h) for h in range(4)])
            if l == 0 and s_ == 0:
                dump('o32', o32, [('o32', h) for h in range(4)])
            stage('gla')
            for ch in range(4):
                S.copy('pool', ubuf[:, l, ch, 0:HALO], ubuf[:, l, ch, NT:NT + HALO], [('ubuf', l, ch)], [('ubuf', l, ch)])
            if l == 0 and s_ == 0:
                dump('cacc', cacc, [('cacc', c) for c in range(4)])
            wt5, wtok5 = wload(b_in[l], 0, KC, 2048, 512)
            for h in range(4):
                rb, rt = nbank()
                mm_acc(rb, [(wt5[:, k, h * 128:(h + 1) * 128], hb[:, k, :]) for k in range(KC)], [wtok5] + hbt, rt)
                S.act(sr[:, h, :], rb, AF.Silu, [rt], [('sr', h)])
            S.act(bufB[:, 0:4, :], o32, AF.Square, [('o32', h) for h in range(4)], [('bufB', k) for k in range(4)],
                  scale=float(128 ** -0.5))
            for h in range(4):
                mb, mt = nbank()
                S.mm(mb, ones_bf, bufB[:, h, :], True, True, ['ones', ('bufB', h)], [mt])
                S.ts('dve', r32, mb, EPS, None, ALU.add, None, [mt], ['r32'])
                S.act(r32, r32, AF.Ln, ['r32'], ['r32'])
                S.act(r32, r32, AF.Exp, ['r32'], ['r32'], scale=-0.5)
                S.tt('dve', o32[:, h, :], o32[:, h, :], r32, ALU.mult, [('o32', h), 'r32'], [('o32', h)])
                S.stt('dve', bufA[:, 4 + h, :], o32[:, h, :], pc[:, c_gg:c_gg + 1], sr[:, h, :], ALU.mult, ALU.mult,
                      [('o32', h), ('sr', h), 'pc'], [('mixin', 4 + h)])
            layer_norm(cacc, [('cacc', c) for c in range(4)], c_clg, c_clb, 4, 1.0 / 512, None, None,
                       bufA[:, 0:4, :], [('bufA', c) for c in range(4)], func=AF.Silu)
            if l == 0 and s_ == 0:
                dump('mixin', bufA, ['bufA'])
            proj_residual_ln(l, b_mo[l], bufA, ['bufA'], KC, hcur, htoks, c_ln1g, c_ln1b)
            if l == 0 and s_ == 0:
                dump('h1', hcur, [('h32', hs)])
            stage('mixer')

            for cg in range(2):
                wt, wtok = wload(b_xq[l], 0, KC, cg * 512, 512)
                for jj in range(4):
                    j = cg * 4 + jj
                    bk, bt = nbank()
                    mm_acc(bk, [(wt[:, k, jj * 128:(jj + 1) * 128], hb[:, k, :]) for k in range(KC)], [wtok] + hbt, bt)
                    S.act(bufB[:, j, :], bk, AF.Identity, [bt], [('qT', j)], scale=1.0 / 16)
            def x_scores(h):
                psl = h % 2
                for mc in range(2):
                    sbk_, st_ = nbank()
                    mm_acc(sbk_, [(KT[:, l, 2 * h + dc, mc * 128:(mc + 1) * 128], bufB[:, 2 * h + dc, :]) for dc in range(2)],
                           [('KT', l), ('qT', 2 * h), ('qT', 2 * h + 1)], st_)
                    S.act(PT[:, psl, mc, :], sbk_, AF.Exp, [st_], [('PT', psl, mc)])

            def x_pv(h):
                psl = h % 2
                db_, dt2 = nbank()
                mm_acc(db_, [(ones_bf, PT[:, psl, mc, :]) for mc in range(2)], ['ones', ('PT', psl, 0), ('PT', psl, 1)], dt2)
                S.op('dve', lambda E, o=r32, i=db_: E.reciprocal(out=o, in_=i), [dt2], ['r32'])
                for dc in range(2):
                    ob_, ot_ = nbank()
                    mm_acc(ob_, [(Vb[:, l, mc, h * 256 + dc * 128:h * 256 + (dc + 1) * 128], PT[:, psl, mc, :]) for mc in range(2)],
                           [('Vb', l), ('PT', psl, 0), ('PT', psl, 1)], ot_)
                    S.tt('dve', bufA[:, 2 * h + dc, :], ob_, r32, ALU.mult, [ot_, 'r32'], [('xo', 2 * h + dc)])

            x_scores(0)
            for h in range(4):
                if h + 1 < 4:
                    x_scores(h + 1)
                x_pv(h)
            if l == 0 and s_ == 0:
                dump('xo', bufA, [('xo', j) for j in range(8)])
            proj_residual_ln(l, b_xo[l], bufA, [('xo', j) for j in range(8)], KC, hcur, htoks, c_ln2g, c_ln2b)
            if l == 0 and s_ == 0:
                dump('h2', hcur, [('h32', hs)])
            stage('xattn')

            for g in range(6):
                ncol = 512 if g < 5 else 256
                wg, wgt = wload(b_fi[l], 0, KC, g * 512, ncol)
                wu, wut = wload(b_fi[l], 0, KC, DFF + g * 512, ncol)
                for jj in range(ncol // 128):
                    j = g * 4 + jj
                    gb, gt = nbank()
                    mm_acc(gb, [(wg[:, k, jj * 128:(jj + 1) * 128], hb[:, k, :]) for k in range(KC)], [wgt] + hbt, gt)
                    ss = j % 2
                    S.act(sg32[:, ss, :], gb, AF.Silu, [gt], [('sg', ss)])
                    ub_, ut_ = nbank()
                    mm_acc(ub_, [(wu[:, k, jj * 128:(jj + 1) * 128], hb[:, k, :]) for k in range(KC)], [wut] + hbt, ut_)
                    S.tt('dve', big_bf[:, j, :], ub_, sg32[:, ss, :], ALU.mult, [ut_, ('sg', ss)], [('hid', j)])
            b1_, t1_ = nbank(pin=True)
            b2_, t2_ = nbank(pin=True)
            st3 = (b1_, t1_, b2_, t2_)
            for cg in range(2):
                accs = [nbank(pin=True) for _ in range(4)]
                for rg in range(3):
                    nk = 8 if rg < 2 else 6
                    wt, wtok = wload(b_fo[l], rg * 1024, nk, cg * 512, 512)
                    for jj in range(4):
                        for k in range(nk):
                            hc = rg * 8 + k
                            S.mm(accs[jj][0], wt[:, k, jj * 128:(jj + 1) * 128], big_bf[:, hc, :], (hc == 0), (hc == HC - 1),
                                 [wtok, ('hid', hc)], [accs[jj][1]])
                for jj in range(4):
                    j = cg * 4 + jj
                    unpin(accs[jj][1])
                    resid_stats(j, hcur, htoks, accs[jj][0], accs[jj][1], st3)
            unpin(t1_)
            unpin(t2_)
            if l < L - 1:
                layer_norm(hcur, htoks, c_ln3g, c_ln3b, KC, 1.0 / D, hcur, htoks, hb, hbt, pre=st3)
            else:
                layer_norm(hcur, htoks, c_ln3g, c_ln3b, KC, 1.0 / D, hcur, htoks, None, None, pre=st3)
        for k in range(KC):
            S.dma('sp', contrib[k * 128:(k + 1) * 128, :], hcur[:, k, :], 'hst0', reads=[htoks[k]], writes=['contrib'])
        S.dma('sp', outT[:, ti_out * NT:(ti_out + 1) * NT].rearrange("(kc p) t -> p kc t", p=128), hcur, 'hst1',
              reads=[('h32', 0)], writes=['out_dram'])
        if s_ == 0:
            for l in range(L):
                for hp in range(2):
                    S.ts('dve', S32[:, l, hp, :], S32[:, l, hp, :], fA, None, ALU.mult, None, [('S32', l, hp), 'fl'], [('S32', l, hp)])
                    for sl in range(2):
                        S.ts('dve', S_bf[:, l, hp, sl, :], S_bf[:, l, hp, sl, :], fA, None, ALU.mult, None,
                             [('S_bf', l, hp, sl), 'fl'], [('S_bf', l, hp, sl)])
                for ch in range(4):
                    S.ts('dve', ubuf[:, l, ch, 0:HALO], ubuf[:, l, ch, 0:HALO], fA, None, ALU.mult, None,
                         [('ubuf', l, ch), 'fl'], [('ubuf', l, ch)])
        if s_ < nsteps - 1:
            build_dg(0)
            S.cc(lambda E, g=groups: E.collective_compute("AllGather", ALU.bypass, replica_groups=g,
                                                          ins=[contrib.opt()], outs=[gath.opt()]),
                 'cc%d' % s_, reads=['contrib'], writes=['gath'])

    S.sbuf_free = nc.sbuf_bytes_remaining
    S.emit(None)
    return nc, S, dump_out


def make_consts():
    s = np.arange(128)[:, None]
    t = np.arange(128)[None, :]
    same = (s // 64) == (t // 64)
    U = np.where(same & (s <= t), -1.0 / 16.0, 0.0).astype(np.float32)
    Lm = np.where(same & (s > t), -1.0 / 16.0, 0.0).astype(np.float32)
    M = np.where(same & (s <= t), 1.0, 0.0).astype(np.float32)
    p = np.arange(128)[:, None]
    m0 = (p < 64).astype(np.float32)
    m1 = (p >= 64).astype(np.float32)
    return np.concatenate([U, Lm, np.tile(M, (1, 4)), m0, m1, 0.125 * m0, 0.125 * m1, np.eye(128, dtype=np.float32)], axis=1).astype(np.float32)


def pack_pcols(inp, layers, ln0_identity):
    pcw = np.zeros((128, NPC), np.float32)

    def fm(v):
        v = np.asarray(v, np.float32)
        return v.reshape(-1, 128).T

    if ln0_identity:
        pcw[:, 0:8] = 1.0
        pcw[:, 8:16] = 0.0
    else:
        pcw[:, 0:8] = fm(inp["ln0_g"])
        pcw[:, 8:16] = fm(inp["ln0_b"])
    for li, l in enumerate(layers):
        P0 = 16 + li * NPL
        for i, nm in enumerate(["ln1_g", "ln1_b", "ln2_g", "ln2_b", "ln3_g", "ln3_b"]):
            pcw[:, P0 + 8 * i:P0 + 8 * i + 8] = fm(inp[nm][l])
        cw = np.asarray(inp["conv_w"][l], np.float32)
        c = P0 + 48
        for ch in range(4):
            pcw[:, c + ch * CONVW:c + (ch + 1) * CONVW] = cw[:, ch * 128:(ch + 1) * 128].T
        c += 124
        pcw[:, c:c + 4] = fm(inp["conv_b"][l]); c += 4
        pcw[:, c:c + 4] = fm(inp["conv_ln_g"][l]); c += 4
        pcw[:, c:c + 4] = fm(inp["conv_ln_b"][l]); c += 4
        pcw[:, c:c + 1] = fm(inp["gla_norm_g"][l])
    return pcw


_CACHE = {}
WNAMES = ["w_in", "w_mix_out", "w_xq", "w_xkv", "w_xo", "w_ffn_in", "w_ffn_out"]


def make_in_maps(inp, nb, T):
    cst = make_consts()
    wa2e_all = np.concatenate([inp["w_a2"], inp["b_a"][:, None, :]], axis=1).astype(np.float32)
    stage = []
    for st in range(2):
        layers = [2 * st, 2 * st + 1]
        d = {nm: np.ascontiguousarray(np.asarray(inp[nm], np.float32)[layers[0]:layers[1] + 1]) for nm in WNAMES}
        d["pcols"] = pack_pcols(inp, layers, ln0_identity=(st == 1))
        d["wa2e"] = np.ascontiguousarray(wa2e_all[layers[0]:layers[1] + 1])
        d["consts"] = cst
        fl = np.zeros((128, 4), np.float32)
        fl[:, 0] = 1.0 if st == 0 else 0.0
        fl[:, 1] = 0.0 if st == 0 else 1.0
        d["flags"] = fl
        stage.append(d)
    in_maps = []
    for b in range(nb):
        xT = np.ascontiguousarray(np.asarray(inp["x"][b][:T]).T, np.float32)
        mT = np.ascontiguousarray(np.asarray(inp["mem"][b]).T, np.float32)
        for st in range(2):
            m = dict(stage[st])
            m["xT"] = xT
            m["memT"] = mT
            in_maps.append(m)
    return in_maps


def kernel(**inputs):
    inp = {k: np.asarray(v) for k, v in inputs.items()}
    x = inp["x"]
    B, T, _ = x.shape
    key = (T, 2 * B)
    if key not in _CACHE:
        _CACHE[key] = build_program(LSLOTS, T, ncores=2 * B)
    nc, S, _ = _CACHE[key]
    in_maps = make_in_maps(inp, B, T)
    res = run_bass_kernel_spmd(nc, in_maps, core_ids=list(range(2 * B)))
    out = np.stack([np.ascontiguousarray(res.results[2 * b + 1]["outT"].T) for b in range(B)], axis=0)
    return out.astype(np.float32)
```

```python
import numpy as np
import concourse.bass as bass
import concourse.mybir as mybir
from concourse.bass_utils import run_bass_kernel_spmd

F32 = mybir.dt.float32
BF16 = mybir.dt.bfloat16
AF = mybir.ActivationFunctionType
ALU = mybir.AluOpType

D = 1024
KC = 8
NT = 512
DFF = 2816
HC = 22
INC = 2576
MEM = 256
DEPTH = 4
SEQ = 8192
ALPHA = float(8 ** 0.25)
EPS = 1e-5
CONVW = 31
HALO = 30
NPL = 185
LSLOTS = 2
NPC = LSLOTS * NPL + 16

SAME_ENG_SYNC = True
DG_ENG = 'pool'


class Sched:
    def __init__(self, nc):
        self.nc = nc
        self.eng = {'pe': nc.tensor, 'act': nc.scalar, 'dve': nc.vector, 'pool': nc.gpsimd, 'sp': nc.sync}
        self.ops = []
        self.alias = {}
        self.sems = {}
        self.bank_i = 0
        self.frozen = False

    def op(self, eng, fn, reads=(), writes=()):
        if self.frozen:
            return
        self.ops.append(dict(eng=eng, fn=fn, r=list(reads), w=list(writes), dma=None))

    def dma(self, eng, out, in_, stream, reads=(), writes=()):
        if self.frozen:
            return
        self.ops.append(dict(eng=eng, fn=(lambda E, o=out, i=in_: E.dma_start(out=o, in_=i)),
                             r=list(reads), w=list(writes), dma=stream))

    def cc(self, fn, key, reads=(), writes=()):
        if self.frozen:
            return
        self.ops.append(dict(eng='pool', fn=fn, r=list(reads), w=list(writes), dma=key, inc=1))

    def mm(self, out, lhsT, rhs, start, stop, reads, writes, sgc=False):
        if sgc:
            self.op('pe', lambda E, o=out, l=lhsT, r=rhs, a=start, b=stop: E.matmul(o, l, r, start=a, stop=b, skip_group_check=True),
                    reads, writes)
        else:
            self.op('pe', lambda E, o=out, l=lhsT, r=rhs, a=start, b=stop: E.matmul(o, l, r, start=a, stop=b),
                    reads, writes)

    def act(self, out, in_, func, reads, writes, bias=0.0, scale=1.0):
        self.op('act', lambda E, o=out, i=in_, f=func, b=bias, s=scale: E.activation(out=o, in_=i, func=f, bias=b, scale=s),
                reads, writes)

    def tt(self, eng, out, in0, in1, op, reads, writes):
        self.op(eng, lambda E, o=out, a=in0, b=in1, p=op: E.tensor_tensor(out=o, in0=a, in1=b, op=p), reads, writes)

    def ts(self, eng, out, in0, s1, s2, op0, op1, reads, writes):
        if s2 is None:
            self.op(eng, lambda E, o=out, a=in0, x=s1, p=op0: E.tensor_scalar(out=o, in0=a, scalar1=x, scalar2=None, op0=p),
                    reads, writes)
        else:
            self.op(eng, lambda E, o=out, a=in0, x=s1, y=s2, p=op0, q=op1: E.tensor_scalar(out=o, in0=a, scalar1=x, scalar2=y, op0=p, op1=q),
                    reads, writes)

    def fence(self, eng, streams):
        if self.frozen:
            return
        self.ops.append(dict(eng=eng, fn=None, r=[], w=[], dma=None, fence=list(streams)))

    def stt(self, eng, out, in0, scalar, in1, op0, op1, reads, writes):
        self.op(eng, lambda E, o=out, a=in0, s=scalar, b=in1, p=op0, q=op1: E.scalar_tensor_tensor(out=o, in0=a, scalar=s, in1=b, op0=p, op1=q),
                reads, writes)

    def copy(self, eng, out, in_, reads, writes):
        if eng == 'act':
            self.act(out, in_, AF.Copy, reads, writes)
        else:
            self.op(eng, lambda E, o=out, i=in_: E.tensor_copy(out=o, in_=i), reads, writes)

    def memset(self, eng, ap, val, writes):
        self.op(eng, lambda E, a=ap, v=val: E.memset(a, v), (), writes)

    def _expand(self, toks):
        out = []
        for t in toks:
            a = self.alias.get(t)
            if a is None:
                out.append(t)
            else:
                out.extend(a)
        return out

    def _sem(self, v):
        s = self.sems.get(v)
        if s is None:
            s = self.nc.alloc_semaphore(name=("s_" + str(v)).replace("'", "").replace(" ", "").replace("(", "").replace(")", "").replace(",", "_"))
            self.sems[v] = s
        return s

    def emit(self, final_streams):
        ops = self.ops
        lastw = {}
        readers = {}
        last_in_stream = {}
        for i, o in enumerate(ops):
            v = ('dma', o['dma']) if o['dma'] else o['eng']
            o['v'] = v
            o['needed'] = False
            deps = set()
            R = self._expand(o['r'])
            W = self._expand(o['w'])
            for t in R:
                j = lastw.get(t)
                if j is not None:
                    deps.add(j)
            for t in W:
                j = lastw.get(t)
                if j is not None:
                    deps.add(j)
                for j in readers.get(t, {}).values():
                    deps.add(j)
            if o['dma']:
                j = last_in_stream.get(o['dma'])
                if j is not None:
                    deps.add(j)
                last_in_stream[o['dma']] = i
            deps.discard(i)
            dl = []
            for j in deps:
                vj = ops[j]['v']
                if vj == 'pe' and v == 'pe':
                    continue
                if (not SAME_ENG_SYNC) and vj == v and not o['dma']:
                    continue
                dl.append(j)
                ops[j]['needed'] = True
            o['deps'] = dl
            for t in W:
                lastw[t] = i
                readers[t] = {}
            for t in R:
                readers.setdefault(t, {})[v] = i
        cnt = {}
        seen = {e: {} for e in self.eng}
        for o in ops:
            e = o['eng']
            E = self.eng[e]
            v = o['v']
            if o['fn'] is None:
                for st in o['fence']:
                    vv = ('dma', st)
                    c = cnt.get(vv, 0)
                    if c and seen[e].get(vv, 0) < c:
                        E.wait_ge(self._sem(vv), c)
                        seen[e][vv] = c
                o['cnt'] = None
                continue
            need = {}
            for j in o['deps']:
                dj = ops[j]
                need[dj['v']] = max(need.get(dj['v'], 0), dj['cnt'])
            for vv, c in need.items():
                if seen[e].get(vv, 0) < c:
                    E.wait_ge(self._sem(vv), c)
                    seen[e][vv] = c
            ins = o['fn'](E)
            if o['dma']:
                inc = o.get('inc', 16)
                cnt[v] = cnt.get(v, 0) + inc
                ins.then_inc(self._sem(v), inc)
                o['cnt'] = cnt[v]
            elif o['needed']:
                cnt[v] = cnt.get(v, 0) + 1
                ins.then_inc(self._sem(v), 1)
                o['cnt'] = cnt[v]
            else:
                o['cnt'] = None
        for v in list(cnt):
            if isinstance(v, tuple) and v[0] == 'dma':
                self.eng['sp'].wait_ge(self._sem(v), cnt[v])
        self.max_cnt = dict(cnt)


def build_program(L=LSLOTS, TT=SEQ, dumps=(), stop_after=None, ncores=8):
    ntiles = TT // NT
    nsteps = ntiles + 1
    nc = bass.Bass("TRN2", target_bir_lowering=False)
    S = Sched(nc)
    dumps = set(dumps)
    dump_out = {}

    def din(name, shape, dt=F32):
        return nc.dram_tensor(name, list(shape), dt, kind="ExternalInput").ap()

    xT = din("xT", [D, TT])
    memT = din("memT", [D, MEM])
    w_in = din("w_in", [L, D, INC])
    w_mo = din("w_mix_out", [L, D, D])
    w_xq = din("w_xq", [L, D, D])
    w_xkv = din("w_xkv", [L, D, 2 * D])
    w_xo = din("w_xo", [L, D, D])
    w_fi = din("w_ffn_in", [L, D, 2 * DFF])
    w_fo = din("w_ffn_out", [L, DFF, D])
    pcols = din("pcols", [128, NPC])
    wa2e = din("wa2e", [L, 17, 256])
    consts = din("consts", [128, 900])
    flags = din("flags", [128, 4])
    outT = nc.dram_tensor("outT", [D, TT], F32, kind="ExternalOutput").ap()

    def dscr(name, shape, dt):
        return nc.dram_tensor(name, list(shape), dt).ap()

    contrib = dscr("contrib", [D, NT], F32)
    gath = dscr("gath", [2 * D, NT], F32)
    b_in = dscr("b_in", [L, D, INC], BF16)
    b_mo = dscr("b_mo", [L, D, D], BF16)
    b_xq = dscr("b_xq", [L, D, D], BF16)
    b_xkv = dscr("b_xkv", [L, D, 2 * D], BF16)
    b_xo = dscr("b_xo", [L, D, D], BF16)
    b_fi = dscr("b_fi", [L, D, 2 * DFF], BF16)
    b_fo = dscr("b_fo", [L, DFF, D], BF16)

    def sb(name, shape, dt):
        return nc.alloc_sbuf_tensor(name, list(shape), dt).ap()

    pc = sb("pc", [128, NPC], F32)
    cst = sb("cst", [128, 900], F32)
    Uneg = cst[:, 0:128]
    Lneg = cst[:, 128:256]
    mask4 = cst[:, 256:768]
    ones_bf = sb("ones_bf", [128, 128], BF16)
    wa2_f = sb("wa2_f", [32, 256], F32)
    wa2_b = sb("wa2_b", [32, L, 256], BF16)
    KT = sb("KT", [128, L, KC, MEM], BF16)
    Vb = sb("Vb", [128, L, 2, D], BF16)
    fl = sb("fl", [128, 4], F32)
    fA = fl[:, 0:1]
    fB = fl[:, 1:2]
    h32 = [sb("h32_%d" % i, [128, KC, NT], F32) for i in range(1)]
    hb = sb("hb", [128, KC, NT], BF16)
    NSLOT = 3
    wring = [sb("wr_%d" % i, [128, KC, 512], BF16) for i in range(NSLOT)]
    bufA = sb("bufA", [128, KC, NT], BF16)
    bufB = sb("bufB", [128, KC, NT], BF16)
    bigf = sb("big", [128, 12 * 512], F32)
    big = bigf.rearrange("p (a b) -> p a b", b=512)
    big_bf = bigf.bitcast(BF16).rearrange("p (j n) -> p j n", n=512)
    alT = sb("alT", [32, NT], BF16)
    lsb = sb("lsb", [128, 2, 256], F32)
    l_hi = sb("l_hi", [128, 4, 256], BF16)
    l_lo = sb("l_lo", [128, 4, 256], BF16)
    cst_b = sb("cst_b", [128, 384], BF16)
    ident_b = cst_b[:, 256:384]
    dg = sb("dg", [128, 4, CONVW, 128], BF16)
    Uneg_b = cst_b[:, 0:128]
    Lneg_b = cst_b[:, 128:256]
    ez = sb("ez", [128, 2, 256], F32)
    Ed = sb("Ed", [128, 4, 256], F32)
    EbT = sb("EbT", [128, 2, NT], F32)
    EnT = sb("EnT", [128, 2, NT], F32)
    sg32 = EbT
    decT = sb("decT", [128, 2, 8], F32)
    v_bf = sb("v_bf", [128, 4, 512], BF16)
    mem_b = v_bf.rearrange("p a b -> p (a b)").rearrange("p (k m) -> p k m", m=MEM)
    kd_bf = sb("kd_bf", [128, 4, 2, 256], BF16)
    qeT = sb("qeT", [128, 2, 2, NT], BF16)
    keT = sb("keT", [128, 2, NT], BF16)
    att_bf = sb("att_bf", [128, 2, 512], BF16)
    S32 = sb("S32", [128, L, 2, 256], F32)
    S_bf = sb("S_bf", [128, L, 2, 2, 256], BF16)
    ubuf = sb("ubuf", [128, L, 4, NT + HALO], BF16)
    m32 = sb("m32", [128, NT], F32)
    t32 = sb("t32", [128, NT], F32)
    r32 = sb("r32", [128, NT], F32)
    r32b = sb("r32b", [128, NT], F32)
    walo = sb("walo", [128, L, KC, 16], BF16)
    PT = EnT.rearrange("p a b -> p (a b)").bitcast(BF16).rearrange("p (s m t) -> p s m t", s=2, m=2)
    o32 = big[:, 0:4, :]
    cacc = big[:, 4:8, :]
    sr = big[:, 8:12, :]
    banks = [nc.alloc_psum_tensor("bank%d" % i, [128, 512], F32).ap() for i in range(8)]

    S.alias['bufA'] = [('bufA', k) for k in range(KC)]
    S.alias['bufB'] = [('bufB', k) for k in range(KC)]
    for k in range(KC):
        S.alias[('xo', k)] = [('bufA', k)]
        S.alias[('qT', k)] = [('bufB', k)]
    for k in range(4):
        S.alias[('mixin', 4 + k)] = [('bufA', 4 + k)]
    pinned = set()
    for i_ in range(2):
        S.alias[('sg', i_)] = [('EbT', i_)]
        for m_ in range(2):
            S.alias[('PT', i_, m_)] = [('EnT', i_)]
    for hs_ in range(1):
        S.alias[('h32', hs_)] = [('h32c', hs_, k) for k in range(KC)]
    S.alias['bigall'] = [('big', u) for u in range(12)]
    for h in range(4):
        S.alias[('o32', h)] = [('big', h)]
        S.alias[('cacc', h)] = [('big', 4 + h)]
        S.alias[('sr', h)] = [('big', 8 + h)]
    for j in range(HC):
        S.alias[('hid', j)] = [('big', j // 2)]

    def nbank(pin=False):
        i = S.bank_i
        while i in pinned:
            i = (i + 1) % 8
        S.bank_i = (i + 1) % 8
        if pin:
            pinned.add(i)
        return banks[i], ('bank', i)

    def unpin(tok):
        pinned.discard(tok[1])

    def stage(name):
        if stop_after == name:
            S.frozen = True

    def dump(name, ap, reads):
        if name not in dumps or S.frozen:
            return
        d = nc.dram_tensor("dbg_" + name, list(ap.shape), ap.dtype, kind="ExternalOutput").ap()
        dump_out[name] = d
        S.dma('sp', d, ap, 'dbg_' + name, reads=reads, writes=[('dbg', name)])

    S.dma('sp', pc, pcols, 'misc', writes=['pc'])
    S.dma('sp', cst, consts, 'misc', writes=['cst'])
    S.dma('sp', fl, flags, 'misc', writes=['fl'])
    S.memset('dve', ones_bf, 1.0, ['ones'])
    S.copy('dve', cst_b[:, 0:256], cst[:, 0:256], ['cst'], ['cst_b'])
    S.copy('dve', cst_b[:, 256:384], cst[:, 772:900], ['cst'], ['cst_b'])
    S.memset('pool', wa2_b, 0.0, [('wa2_b', l_) for l_ in range(L)])
    S.memset('pool', alT, 1.0, ['alT'])

    cast_rr = [0]
    stage_i = [0]

    def convert(src, dst, R, C):
        cw = 4096
        for rc in range(R // 128):
            c0 = 0
            while c0 < C:
                n = min(cw, C - c0)
                si = stage_i[0] % 2
                stage_i[0] += 1
                s32 = (h32[0].rearrange("p a b -> p (a b)") if si == 0 else bigf)[:, 0:n]
                sbf = (bufA if si == 0 else bufB).rearrange("p a b -> p (a b)")[:, 0:n]
                t32 = ('h32', 0) if si == 0 else 'bigall'
                tbf = 'bufA' if si == 0 else 'bufB'
                S.dma('sp', s32, src[rc * 128:(rc + 1) * 128, c0:c0 + n], 'cv_in%d' % si, writes=[t32])
                e = ['act', 'dve'][cast_rr[0] % 2]
                cast_rr[0] += 1
                S.copy(e, sbf, s32, [t32], [tbf])
                S.dma('sp', dst[rc * 128:(rc + 1) * 128, c0:c0 + n], sbf, 'cv_out%d' % si, reads=[tbf], writes=[])
                c0 += n

    wsets = [(w_in, b_in, D, INC), (w_mo, b_mo, D, D), (w_xq, b_xq, D, D), (w_xkv, b_xkv, D, 2 * D),
             (w_xo, b_xo, D, D), (w_fi, b_fi, D, 2 * DFF), (w_fo, b_fo, DFF, D)]
    wb_tok = {}
    for l in range(L):
        for (src, dst, R, C) in wsets:
            convert(src[l], dst[l], R, C)
    S.fence('sp', ['cv_out0', 'cv_out1'])
    stage('prologue')

    ring_i = [0]

    def wload(src2d, r0, nk, c0, ncols):
        s = ring_i[0] % NSLOT
        ring_i[0] += 1
        dst = wring[s][:, 0:nk, 0:ncols]
        S.dma('sp', dst, src2d[r0:r0 + nk * 128, c0:c0 + ncols].rearrange("(kc p) c -> p kc c", p=128),
              'wr%d' % s, reads=[], writes=[('wr', s)])
        return wring[s], ('wr', s)

    def mm_acc(out, pairs, reads, btok):
        n = len(pairs)
        for i, (l, r) in enumerate(pairs):
            S.mm(out, l, r, i == 0, i == n - 1, reads, [btok])

    def layer_norm(y, ytok_list, gcol, bcol, nk, inv_n, out32, out32_tok, outb, outb_tok, func=AF.Identity,
                   flagged=False, pre=None):
        if pre is None:
            ybv = bufA[:, 0:nk, :]
            ysv = bufB[:, 0:nk, :]
            ta = [('bufA', k) for k in range(nk)]
            tb = [('bufB', k) for k in range(nk)]
            S.copy('dve', ybv, y, ytok_list, ta)
            S.act(ysv, y, AF.Square, ytok_list, tb)
            b1, t1 = nbank()
            mm_acc(b1, [(ones_bf, ybv[:, k, :]) for k in range(nk)], ['ones'] + ta, t1)
            b2, t2 = nbank()
            mm_acc(b2, [(ones_bf, ysv[:, k, :]) for k in range(nk)], ['ones'] + tb, t2)
        else:
            b1, t1, b2, t2 = pre
        S.ts('dve', m32, b1, inv_n, None, ALU.mult, ALU.bypass, [t1], ['m32'])
        if flagged:
            S.ts('dve', m32, m32, fA, None, ALU.mult, None, ['m32', 'fl'], ['m32'])
        S.tt('dve', t32, m32, m32, ALU.mult, ['m32'], ['t32'])
        S.stt('dve', r32, b2, inv_n, t32, ALU.mult, ALU.subtract, [t2, 't32'], ['r32'])
        S.ts('dve', r32, r32, EPS, None, ALU.add, None, ['r32'], ['r32'])
        S.act(r32, r32, AF.Ln, ['r32'], ['r32'])
        S.act(r32, r32, AF.Exp, ['r32'], ['r32'], scale=-0.5)
        if flagged:
            S.ts('dve', r32, r32, fA, fB, ALU.mult, ALU.add, ['r32', 'fl'], ['r32'])
        for k in range(nk):
            S.tt('dve', y[:, k, :], y[:, k, :], m32, ALU.subtract, [ytok_list[k], 'm32'], [ytok_list[k]])
        for k in range(nk):
            S.tt('dve', y[:, k, :], y[:, k, :], r32, ALU.mult, [ytok_list[k], 'r32'], [ytok_list[k]])
            if outb is not None:
                S.act(outb[:, k, :], y[:, k, :], func, [ytok_list[k], 'pc'], [outb_tok[k]],
                      bias=pc[:, bcol + k:bcol + k + 1], scale=pc[:, gcol + k:gcol + k + 1])
        if out32 is not None:
            for k in range(nk):
                S.act(out32[:, k, :], y[:, k, :], func, [ytok_list[k], 'pc'], [out32_tok[k]],
                      bias=pc[:, bcol + k:bcol + k + 1], scale=pc[:, gcol + k:gcol + k + 1])

    def resid_stats(j, hcur, htoks, acc_bank, acc_tok, st):
        b1, t1, b2, t2 = st
        S.stt('dve', hcur[:, j, :], hcur[:, j, :], ALPHA, acc_bank, ALU.mult, ALU.add, [htoks[j], acc_tok], [htoks[j]])
        S.copy('dve', hb[:, j, :], hcur[:, j, :], [htoks[j]], [('hb', j)])
        S.act(bufB[:, j, :], hcur[:, j, :], AF.Square, [htoks[j]], [('bufB', j)])
        S.mm(b1, ones_bf, hb[:, j, :], (j == 0), (j == KC - 1), ['ones', ('hb', j)], [t1], sgc=True)
        S.mm(b2, ones_bf, bufB[:, j, :], (j == 0), (j == KC - 1), ['ones', ('bufB', j)], [t2], sgc=True)

    def proj_residual_ln(l, wsrc, rhs_buf, rhs_tok, nkin, hcur, htoks, gcol, bcol):
        b1, t1 = nbank(pin=True)
        b2, t2 = nbank(pin=True)
        st = (b1, t1, b2, t2)
        for cg in range(2):
            wt, wtok = wload(wsrc, 0, nkin, cg * 512, 512)
            for jj in range(4):
                j = cg * 4 + jj
                bk, bt = nbank()
                mm_acc(bk, [(wt[:, k, jj * 128:(jj + 1) * 128], rhs_buf[:, k, :]) for k in range(nkin)],
                       [wtok] + rhs_tok, bt)
                resid_stats(j, hcur, htoks, bk, bt, st)
        unpin(t1)
        unpin(t2)
        layer_norm(hcur, htoks, gcol, bcol, KC, 1.0 / D, hcur, htoks, hb, [('hb', k) for k in range(KC)], pre=st)

    hbt = [('hb', k) for k in range(KC)]

    hbt = [('hb', k) for k in range(KC)]
    mem32 = big[:, 0:4, :].rearrange("p a b -> p (a b)").rearrange("p (k m) -> p k m", m=MEM)
    S.dma('sp', mem32, memT.rearrange("(kc p) m -> p kc m", p=128), 'misc', writes=[('o32', h) for h in range(4)])
    S.copy('dve', mem_b, mem32, [('o32', h) for h in range(4)], [('v_bf', i) for i in range(4)])
    S.alias['mem_b'] = [('v_bf', i) for i in range(4)]
    sslot = [[0, 0] for _ in range(L)]
    for l in range(L):
        S.dma('sp', walo[:, l, :, :], b_in[l][:, 2560:2576].rearrange("(kc p) c -> p kc c", p=128), 'misc', writes=[('walo', l)])
        S.dma('sp', wa2_f[0:17, :], wa2e[l], 'misc', writes=['wa2_f'])
        S.copy('dve', wa2_b[0:17, l, :], wa2_f[0:17, :], ['wa2_f'], [('wa2_b', l)])
        S.memset('dve', S32[:, l, :, :], 0.0, [('S32', l, 0), ('S32', l, 1)])
        S.memset('pool', S_bf[:, l, :, :, :], 0.0, [('S_bf', l, a, b) for a in range(2) for b in range(2)])
        S.memset('pool', ubuf[:, l, :, :], 0.0, [('ubuf', l, c) for c in range(4)])
        for cg in range(2):
            wt, wtok = wload(b_xkv[l], 0, KC, cg * 512, 512)
            for jj in range(4):
                j = cg * 4 + jj
                bk, bt = nbank()
                mm_acc(bk[:, 0:MEM], [(wt[:, k, jj * 128:(jj + 1) * 128], mem_b[:, k, :]) for k in range(KC)],
                       [wtok, 'mem_b'], bt)
                S.copy('act', KT[:, l, j, :], bk[:, 0:MEM], [bt], [('KT', l)])
        for cg in range(2):
            wt, wtok = wload(b_xkv[l], 0, KC, D + cg * 512, 512)
            for mc in range(2):
                bk, bt = nbank()
                mm_acc(bk, [(mem_b[:, k, mc * 128:(mc + 1) * 128], wt[:, k, :]) for k in range(KC)],
                       [wtok, 'mem_b'], bt)
                S.copy('act', Vb[:, l, mc, cg * 512:(cg + 1) * 512], bk, [bt], [('Vb', l)])
    S.dma('sp', gath[0:D, :], xT[:, 0:NT], 'misc', writes=['gath'])
    groups = [[2 * i, 2 * i + 1] for i in range(ncores // 2)]
    hcur = h32[0]
    htoks = [('h32c', 0, k) for k in range(KC)]
    hs = 0
    xa32 = bufA.rearrange("p a b -> p (a b)").bitcast(F32).rearrange("p (k t) -> p k t", t=NT)
    xb32 = bufB.rearrange("p a b -> p (a b)").bitcast(F32).rearrange("p (k t) -> p k t", t=NT)

    def build_dg(l_):
        c_cw_ = 16 + l_ * NPL + 48
        for ch in range(4):
            for k in range(CONVW):
                wc_ = pc[:, c_cw_ + ch * CONVW + k:c_cw_ + ch * CONVW + k + 1]
                if ch < 2:
                    S.act(dg[:, ch, k, :], ident_b, AF.Identity, ['cst_b', 'pc'], [('dg', ch)], scale=wc_)
                else:
                    S.ts('dve', dg[:, ch, k, :], ident_b, wc_, None, ALU.mult, None, ['cst_b', 'pc'], [('dg', ch)])

    build_dg(0)
    for s_ in range(nsteps):
        ti_in = min(s_, ntiles - 1)
        ti_out = max(s_ - 1, 0)
        for half in range(2):
            ksl = slice(half * 4, (half + 1) * 4)
            S.dma('sp', xa32, xT[half * 512:(half + 1) * 512, ti_in * NT:(ti_in + 1) * NT].rearrange("(kc p) t -> p kc t", p=128),
                  'xin', writes=['bufA'])
            S.dma('sp', xb32, gath[half * 512:(half + 1) * 512, :].rearrange("(kc p) t -> p kc t", p=128),
                  'gin', reads=['gath'], writes=['bufB'])
            S.ts('dve', hcur[:, ksl, :], xa32, fA, None, ALU.mult, None, ['bufA', 'fl'], htoks[half * 4:(half + 1) * 4])
            S.stt('dve', hcur[:, ksl, :], xb32, fB, hcur[:, ksl, :], ALU.mult, ALU.add,
                  ['bufB', 'fl'] + htoks[half * 4:(half + 1) * 4], htoks[half * 4:(half + 1) * 4])
        layer_norm(hcur, htoks, 0, 8, KC, 1.0 / D, hcur, htoks, hb, hbt, flagged=True)
        for l in range(L):
            P0 = 16 + l * NPL
            c_ln1g, c_ln1b, c_ln2g, c_ln2b, c_ln3g, c_ln3b = [P0 + 8 * i for i in range(6)]
            c_cw = P0 + 48
            c_cb = c_cw + 124
            c_clg = c_cb + 4
            c_clb = c_clg + 4
            c_gg = c_clb + 4

            bk, bt = nbank()
            mm_acc(bk[0:16, :], [(walo[:, l, k, :], hb[:, k, :]) for k in range(KC)], [('walo', l)] + hbt, bt)
            S.copy('act', alT[0:16, :], bk[0:16, :], [bt], ['alT'])
            bc0, bct0 = nbank(pin=True)
            bc1, bct1 = nbank(pin=True)
            bcs = [(bc0, bct0), (bc1, bct1)]
            for sbk in range(4):
                tsl = slice(sbk * 128, (sbk + 1) * 128)
                zb, zt = nbank()
                S.mm(zb[:, 0:256], alT[0:17, tsl], wa2_b[0:17, l, :], True, True, ['alT', ('wa2_b', l)], [zt])
                es = sbk % 2
                S.act(ez[:, es, :], zb[:, 0:256], AF.Exp, [zt], [('ez', es)], scale=-1.0)
                S.act(lsb[:, es, :], ez[:, es, :], AF.Ln, [('ez', es)], [('lsb', es)], bias=1.0)
                S.copy('dve', l_hi[:, sbk, :], lsb[:, es, :], [('lsb', es)], [('l_hi', sbk)])
                S.tt('dve', l_lo[:, sbk, :], lsb[:, es, :], l_hi[:, sbk, :], ALU.subtract, [('lsb', es), ('l_hi', sbk)], [('l_lo', sbk)])
                for fc in range(2):
                    S.mm(bcs[fc][0][:, tsl], l_hi[:, sbk, fc * 128:(fc + 1) * 128], Uneg_b, (sbk == 0), False,
                         [('l_hi', sbk), 'cst_b'], [bcs[fc][1]], sgc=True)
                    S.mm(bcs[fc][0][:, tsl], l_lo[:, sbk, fc * 128:(fc + 1) * 128], Uneg_b, False, True,
                         [('l_lo', sbk), 'cst_b'], [bcs[fc][1]], sgc=True)
                db, dt_ = nbank()
                S.mm(db[:, 0:256], Lneg_b, l_hi[:, sbk, :], True, False, [('l_hi', sbk), 'cst_b'], [dt_])
                S.mm(db[:, 0:256], Lneg_b, l_lo[:, sbk, :], False, True, [('l_lo', sbk), 'cst_b'], [dt_])
                S.act(Ed[:, sbk, :], db[:, 0:256], AF.Exp, [dt_], [('Ed', sbk)])
            for fc in range(2):
                S.act(EbT[:, fc, :], bcs[fc][0], AF.Exp, [bcs[fc][1]], [('EbT', fc)])
                S.act(EnT[:, fc, :], bcs[fc][0], AF.Exp, [bcs[fc][1]], [('EnT', fc)], scale=-1.0)
                S.act(decT[:, fc, :], bcs[fc][0].rearrange("p (c t) -> p c t", t=64)[:, :, 63], AF.Exp,
                      [bcs[fc][1]], [('decT', fc)])
            unpin(bct0)
            unpin(bct1)
            stage('gates')
            if l > 0:
                build_dg(l)
            wt3, wtok3 = wload(b_in[l], 0, KC, 1024, 512)
            wt4, wtok4 = wload(b_in[l], 0, KC, 1536, 512)
            for sbk in range(4):
                tsl = slice(sbk * 128, (sbk + 1) * 128)
                vb_, vt_ = nbank()
                mm_acc(vb_, [(hb[:, k, tsl], wt4[:, k, :]) for k in range(KC)], [wtok4] + hbt, vt_)
                S.copy('act', v_bf[:, sbk, :], vb_, [vt_], [('v_bf', sbk)])
                kb_, kt_ = nbank()
                mm_acc(kb_[:, 0:256], [(hb[:, k, tsl], wt3[:, k, 256:512]) for k in range(KC)], [wtok3] + hbt, kt_)
                for cc in range(2):
                    S.stt('dve', kd_bf[:, sbk, cc, :], kb_[:, 0:256], cst[:, 768 + cc:769 + cc], Ed[:, sbk, :], ALU.mult, ALU.mult,
                          [kt_, ('Ed', sbk), 'cst'], [('kd_bf', sbk)])
            stage('kv')
            for fc in range(2):
                qb_, qt_ = nbank()
                mm_acc(qb_, [(wt3[:, k, fc * 128:(fc + 1) * 128], hb[:, k, :]) for k in range(KC)], [wtok3] + hbt, qt_)
                for hl in range(2):
                    S.stt('dve', qeT[:, fc, hl, :], qb_, cst[:, 770 + hl:771 + hl], EbT[:, fc, :], ALU.mult, ALU.mult,
                          [qt_, ('EbT', fc), 'cst'], [('qeT', fc)])
                kb_, kt_ = nbank()
                mm_acc(kb_, [(wt3[:, k, 256 + fc * 128:256 + (fc + 1) * 128], hb[:, k, :]) for k in range(KC)], [wtok3] + hbt, kt_)
                S.tt('dve', keT[:, fc, :], kb_, EnT[:, fc, :], ALU.mult, [kt_, ('EnT', fc)], [('keT', fc)])
            if l == 0 and s_ == 0:
                dump('qeT', qeT, [('qeT', 0), ('qeT', 1)])
                dump('keT', keT, [('keT', 0), ('keT', 1)])
                dump('kd', kd_bf, [('kd_bf', i) for i in range(4)])
                dump('vbf', v_bf, [('v_bf', i) for i in range(4)])
                dump('decT', decT, [('decT', 0), ('decT', 1)])
            stage('qk')
            wt1, wtok1 = wload(b_in[l], 0, KC, 0, 512)
            wt2, wtok2 = wload(b_in[l], 0, KC, 512, 512)

            def conv_a(ch):
                gb, gt = nbank()
                mm_acc(gb, [(wt2[:, k, ch * 128:(ch + 1) * 128], hb[:, k, :]) for k in range(KC)], [wtok2] + hbt, gt)
                ss = ch % 2
                S.act(sg32[:, ss, :], gb, AF.Sigmoid, [gt], [('sg', ss)])
                ab_, at_ = nbank()
                mm_acc(ab_, [(wt1[:, k, ch * 128:(ch + 1) * 128], hb[:, k, :]) for k in range(KC)], [wtok1] + hbt, at_)
                S.tt('dve', ubuf[:, l, ch, HALO:HALO + NT], ab_, sg32[:, ss, :], ALU.mult, [at_, ('sg', ss)], [('ubuf', l, ch)])

            def conv_b(ch):
                cb_, ct_ = nbank()
                mm_acc(cb_, [(dg[:, ch, k, :], ubuf[:, l, ch, k:k + NT]) for k in range(CONVW)], [('dg', ch), ('ubuf', l, ch)], ct_)
                S.act(cacc[:, ch, :], cb_, AF.Identity, [ct_, 'pc'], [('cacc', ch)], bias=pc[:, c_cb + ch:c_cb + ch + 1])
            for sbk in range(4):
                tsl = slice(sbk * 128, (sbk + 1) * 128)
                ab, at = nbank()
                for h in range(4):
                    fc, hl = h // 2, h % 2
                    S.mm(ab[:, h * 128:(h + 1) * 128], keT[:, fc, tsl], qeT[:, fc, hl, tsl],
                         True, True, [('keT', fc), ('qeT', fc)], [at])
                asl = sbk % 2
                S.tt('dve', att_bf[:, asl, :], ab, mask4, ALU.mult, [at, 'cst'], [('att', asl)])
                stage('att')
                ob, ot = nbank(pin=True)
                for h in range(4):
                    S.mm(ob[:, h * 128:(h + 1) * 128], v_bf[:, sbk, h * 128:(h + 1) * 128], att_bf[:, asl, h * 128:(h + 1) * 128],
                         (h == 0), False, [('v_bf', sbk), ('att', asl)], [ot], sgc=True)
                stage('intra')
                for cc in range(2):
                    ci = sbk * 2 + cc
                    csl = slice(sbk * 128 + cc * 64, sbk * 128 + (cc + 1) * 64)
                    for h in range(4):
                        fc, hl = h // 2, h % 2
                        sl = sslot[l][fc]
                        S.mm(ob[:, h * 128 + cc * 64:h * 128 + (cc + 1) * 64],
                             S_bf[:, l, fc, sl, hl * 128:(hl + 1) * 128],
                             qeT[:, fc, hl, csl], False, (cc == 1),
                             [('S_bf', l, fc, sl), ('qeT', fc)], [ot], sgc=True)
                    stage('inter0')
                    for hp in range(2):
                        ub, ut = nbank()
                        S.mm(ub[:, 0:256], kd_bf[:, sbk, cc, hp * 128:(hp + 1) * 128],
                             v_bf[:, sbk, hp * 256:(hp + 1) * 256], True, True,
                             [('kd_bf', sbk), ('v_bf', sbk)], [ut])
                        S.stt('dve', S32[:, l, hp, :], S32[:, l, hp, :], decT[:, hp, ci:ci + 1], ub[:, 0:256], ALU.mult, ALU.add,
                              [('S32', l, hp), ('decT', hp), ut], [('S32', l, hp)])
                        ns = 1 - sslot[l][hp]
                        S.copy('act', S_bf[:, l, hp, ns, :], S32[:, l, hp, :], [('S32', l, hp)], [('S_bf', l, hp, ns)])
                        sslot[l][hp] = ns
                        stage('upd0')
                    if cc == 0:
                        conv_a(sbk)
                    else:
                        conv_b(sbk)
                unpin(ot)
                S.copy('act', o32[:, :, tsl], ob.rearrange("p (h c) -> p h c", c=128), [ot], [('o32', h) for h in range(4)])
            if l == 0 and s_ == 0:
                dump('o32', o32, [('o32', h) for h in range(4)])
            stage('gla')
            for ch in range(4):
                S.copy('pool', ubuf[:, l, ch, 0:HALO], ubuf[:, l, ch, NT:NT + HALO], [('ubuf', l, ch)], [('ubuf', l, ch)])
            if l == 0 and s_ == 0:
                dump('cacc', cacc, [('cacc', c) for c in range(4)])
            wt5, wtok5 = wload(b_in[l], 0, KC, 2048, 512)
            for h in range(4):
                rb, rt = nbank()
                mm_acc(rb, [(wt5[:, k, h * 128:(h + 1) * 128], hb[:, k, :]) for k in range(KC)], [wtok5] + hbt, rt)
                S.act(sr[:, h, :], rb, AF.Silu, [rt], [('sr', h)])
            S.act(bufB[:, 0:4, :], o32, AF.Square, [('o32', h) for h in range(4)], [('bufB', k) for k in range(4)],
                  scale=float(128 ** -0.5))
            for h in range(4):
                rr, rrt = (r32, 'r32') if h % 2 == 0 else (r32b, 'r32b')
                mb, mt = nbank()
                S.mm(mb, ones_bf, bufB[:, h, :], True, True, ['ones', ('bufB', h)], [mt])
                S.ts('dve', rr, mb, EPS, None, ALU.add, None, [mt], [rrt])
                S.act(rr, rr, AF.Ln, [rrt], [rrt])
                S.act(rr, rr, AF.Exp, [rrt], [rrt], scale=-0.5)
                S.tt('dve', o32[:, h, :], o32[:, h, :], rr, ALU.mult, [('o32', h), rrt], [('o32', h)])
                S.stt('dve', bufA[:, 4 + h, :], o32[:, h, :], pc[:, c_gg:c_gg + 1], sr[:, h, :], ALU.mult, ALU.mult,
                      [('o32', h), ('sr', h), 'pc'], [('mixin', 4 + h)])
            layer_norm(cacc, [('cacc', c) for c in range(4)], c_clg, c_clb, 4, 1.0 / 512, None, None,
                       bufA[:, 0:4, :], [('bufA', c) for c in range(4)], func=AF.Silu)
            if l == 0 and s_ == 0:
                dump('mixin', bufA, ['bufA'])
            proj_residual_ln(l, b_mo[l], bufA, ['bufA'], KC, hcur, htoks, c_ln1g, c_ln1b)
            if l == 0 and s_ == 0:
                dump('h1', hcur, [('h32', hs)])
            stage('mixer')

            for cg in range(2):
                wt, wtok = wload(b_xq[l], 0, KC, cg * 512, 512)
                for jj in range(4):
                    j = cg * 4 + jj
                    bk, bt = nbank()
                    mm_acc(bk, [(wt[:, k, jj * 128:(jj + 1) * 128], hb[:, k, :]) for k in range(KC)], [wtok] + hbt, bt)
                    S.act(bufB[:, j, :], bk, AF.Identity, [bt], [('qT', j)], scale=1.0 / 16)
            def x_scores(h):
                psl = h % 2
                for mc in range(2):
                    sbk_, st_ = nbank()
                    mm_acc(sbk_, [(KT[:, l, 2 * h + dc, mc * 128:(mc + 1) * 128], bufB[:, 2 * h + dc, :]) for dc in range(2)],
                           [('KT', l), ('qT', 2 * h), ('qT', 2 * h + 1)], st_)
                    S.act(PT[:, psl, mc, :], sbk_, AF.Exp, [st_], [('PT', psl, mc)])

            def x_pv(h):
                psl = h % 2
                db_, dt2 = nbank()
                mm_acc(db_, [(ones_bf, PT[:, psl, mc, :]) for mc in range(2)], ['ones', ('PT', psl, 0), ('PT', psl, 1)], dt2)
                rr, rrt = (r32, 'r32') if h % 2 == 0 else (r32b, 'r32b')
                S.op('dve', lambda E, o=rr, i=db_: E.reciprocal(out=o, in_=i), [dt2], [rrt])
                for dc in range(2):
                    ob_, ot_ = nbank()
                    mm_acc(ob_, [(Vb[:, l, mc, h * 256 + dc * 128:h * 256 + (dc + 1) * 128], PT[:, psl, mc, :]) for mc in range(2)],
                           [('Vb', l), ('PT', psl, 0), ('PT', psl, 1)], ot_)
                    S.tt('dve', bufA[:, 2 * h + dc, :], ob_, rr, ALU.mult, [ot_, rrt], [('xo', 2 * h + dc)])

            x_scores(0)
            for h in range(4):
                if h + 1 < 4:
                    x_scores(h + 1)
                x_pv(h)
            if l == 0 and s_ == 0:
                dump('xo', bufA, [('xo', j) for j in range(8)])
            proj_residual_ln(l, b_xo[l], bufA, [('xo', j) for j in range(8)], KC, hcur, htoks, c_ln2g, c_ln2b)
            if l == 0 and s_ == 0:
                dump('h2', hcur, [('h32', hs)])
            stage('xattn')

            for g in range(6):
                ncol = 512 if g < 5 else 256
                wg, wgt = wload(b_fi[l], 0, KC, g * 512, ncol)
                wu, wut = wload(b_fi[l], 0, KC, DFF + g * 512, ncol)
                for jj in range(ncol // 128):
                    j = g * 4 + jj
                    gb, gt = nbank()
                    mm_acc(gb, [(wg[:, k, jj * 128:(jj + 1) * 128], hb[:, k, :]) for k in range(KC)], [wgt] + hbt, gt)
                    ss = j % 2
                    S.act(sg32[:, ss, :], gb, AF.Silu, [gt], [('sg', ss)])
                    ub_, ut_ = nbank()
                    mm_acc(ub_, [(wu[:, k, jj * 128:(jj + 1) * 128], hb[:, k, :]) for k in range(KC)], [wut] + hbt, ut_)
                    S.tt('dve', big_bf[:, j, :], ub_, sg32[:, ss, :], ALU.mult, [ut_, ('sg', ss)], [('hid', j)])
            b1_, t1_ = nbank(pin=True)
            b2_, t2_ = nbank(pin=True)
            st3 = (b1_, t1_, b2_, t2_)
            for cg in range(2):
                accs = [nbank(pin=True) for _ in range(4)]
                for rg in range(3):
                    nk = 8 if rg < 2 else 6
                    wt, wtok = wload(b_fo[l], rg * 1024, nk, cg * 512, 512)
                    for jj in range(4):
                        for k in range(nk):
                            hc = rg * 8 + k
                            S.mm(accs[jj][0], wt[:, k, jj * 128:(jj + 1) * 128], big_bf[:, hc, :], (hc == 0), (hc == HC - 1),
                                 [wtok, ('hid', hc)], [accs[jj][1]])
                for jj in range(4):
                    j = cg * 4 + jj
                    unpin(accs[jj][1])
                    resid_stats(j, hcur, htoks, accs[jj][0], accs[jj][1], st3)
            unpin(t1_)
            unpin(t2_)
            if l < L - 1:
                layer_norm(hcur, htoks, c_ln3g, c_ln3b, KC, 1.0 / D, hcur, htoks, hb, hbt, pre=st3)
            else:
                layer_norm(hcur, htoks, c_ln3g, c_ln3b, KC, 1.0 / D, hcur, htoks, None, None, pre=st3)
        for k in range(KC):
            S.dma('sp', contrib[k * 128:(k + 1) * 128, :], hcur[:, k, :], 'hst0', reads=[htoks[k]], writes=['contrib'])
        S.dma('sp', outT[:, ti_out * NT:(ti_out + 1) * NT].rearrange("(kc p) t -> p kc t", p=128), hcur, 'hst1',
              reads=[('h32', 0)], writes=['out_dram'])
        if s_ == 0:
            for l in range(L):
                for hp in range(2):
                    S.ts('dve', S32[:, l, hp, :], S32[:, l, hp, :], fA, None, ALU.mult, None, [('S32', l, hp), 'fl'], [('S32', l, hp)])
                    for sl in range(2):
                        S.ts('dve', S_bf[:, l, hp, sl, :], S_bf[:, l, hp, sl, :], fA, None, ALU.mult, None,
                             [('S_bf', l, hp, sl), 'fl'], [('S_bf', l, hp, sl)])
                for ch in range(4):
                    S.ts('dve', ubuf[:, l, ch, 0:HALO], ubuf[:, l, ch, 0:HALO], fA, None, ALU.mult, None,
                         [('ubuf', l, ch), 'fl'], [('ubuf', l, ch)])
        if s_ < nsteps - 1:
            build_dg(0)
            S.cc(lambda E, g=groups: E.collective_compute("AllGather", ALU.bypass, replica_groups=g,
                                                          ins=[contrib.opt()], outs=[gath.opt()]),
                 'cc%d' % s_, reads=['contrib'], writes=['gath'])

    S.sbuf_free = nc.sbuf_bytes_remaining
    S.emit(None)
    return nc, S, dump_out


def make_consts():
    s = np.arange(128)[:, None]
    t = np.arange(128)[None, :]
    same = (s // 64) == (t // 64)
    U = np.where(same & (s <= t), -1.0 / 16.0, 0.0).astype(np.float32)
    Lm = np.where(same & (s > t), -1.0 / 16.0, 0.0).astype(np.float32)
    M = np.where(same & (s <= t), 1.0, 0.0).astype(np.float32)
    p = np.arange(128)[:, None]
    m0 = (p < 64).astype(np.float32)
    m1 = (p >= 64).astype(np.float32)
    return np.concatenate([U, Lm, np.tile(M, (1, 4)), m0, m1, 0.125 * m0, 0.125 * m1, np.eye(128, dtype=np.float32)], axis=1).astype(np.float32)


def pack_pcols(inp, layers, ln0_identity):
    pcw = np.zeros((128, NPC), np.float32)

    def fm(v):
        v = np.asarray(v, np.float32)
        return v.reshape(-1, 128).T

    if ln0_identity:
        pcw[:, 0:8] = 1.0
        pcw[:, 8:16] = 0.0
    else:
        pcw[:, 0:8] = fm(inp["ln0_g"])
        pcw[:, 8:16] = fm(inp["ln0_b"])
    for li, l in enumerate(layers):
        P0 = 16 + li * NPL
        for i, nm in enumerate(["ln1_g", "ln1_b", "ln2_g", "ln2_b", "ln3_g", "ln3_b"]):
            pcw[:, P0 + 8 * i:P0 + 8 * i + 8] = fm(inp[nm][l])
        cw = np.asarray(inp["conv_w"][l], np.float32)
        c = P0 + 48
        for ch in range(4):
            pcw[:, c + ch * CONVW:c + (ch + 1) * CONVW] = cw[:, ch * 128:(ch + 1) * 128].T
        c += 124
        pcw[:, c:c + 4] = fm(inp["conv_b"][l]); c += 4
        pcw[:, c:c + 4] = fm(inp["conv_ln_g"][l]); c += 4
        pcw[:, c:c + 4] = fm(inp["conv_ln_b"][l]); c += 4
        pcw[:, c:c + 1] = fm(inp["gla_norm_g"][l])
    return pcw


_CACHE = {}
WNAMES = ["w_in", "w_mix_out", "w_xq", "w_xkv", "w_xo", "w_ffn_in", "w_ffn_out"]


def make_in_maps(inp, nb, T):
    cst = make_consts()
    wa2e_all = np.concatenate([inp["w_a2"], inp["b_a"][:, None, :]], axis=1).astype(np.float32)
    stage = []
    for st in range(2):
        layers = [2 * st, 2 * st + 1]
        d = {nm: np.ascontiguousarray(np.asarray(inp[nm], np.float32)[layers[0]:layers[1] + 1]) for nm in WNAMES}
        d["pcols"] = pack_pcols(inp, layers, ln0_identity=(st == 1))
        d["wa2e"] = np.ascontiguousarray(wa2e_all[layers[0]:layers[1] + 1])
        d["consts"] = cst
        fl = np.zeros((128, 4), np.float32)
        fl[:, 0] = 1.0 if st == 0 else 0.0
        fl[:, 1] = 0.0 if st == 0 else 1.0
        d["flags"] = fl
        stage.append(d)
    in_maps = []
    for b in range(nb):
        xT = np.ascontiguousarray(np.asarray(inp["x"][b][:T]).T, np.float32)
        mT = np.ascontiguousarray(np.asarray(inp["mem"][b]).T, np.float32)
        for st in range(2):
            m = dict(stage[st])
            m["xT"] = xT
            m["memT"] = mT
            in_maps.append(m)
    return in_maps


def kernel(**inputs):
    inp = {k: np.asarray(v) for k, v in inputs.items()}
    x = inp["x"]
    B, T, _ = x.shape
    key = (T, 2 * B)
    if key not in _CACHE:
        _CACHE[key] = build_program(LSLOTS, T, ncores=2 * B)
    nc, S, _ = _CACHE[key]
    in_maps = make_in_maps(inp, B, T)
    res = run_bass_kernel_spmd(nc, in_maps, core_ids=list(range(2 * B)))
    out = np.stack([np.ascontiguousarray(res.results[2 * b + 1]["outT"].T) for b in range(B)], axis=0)
    return out.astype(np.float32)
```

```python
import numpy as np
import concourse.bass as bass
import concourse.mybir as mybir
from concourse.bass_utils import run_bass_kernel_spmd

F32 = mybir.dt.float32
BF16 = mybir.dt.bfloat16
AF = mybir.ActivationFunctionType
ALU = mybir.AluOpType

D = 1024
KC = 8
NT = 512
DFF = 2816
HC = 22
INC = 2576
MEM = 256
DEPTH = 4
SEQ = 8192
ALPHA = float(8 ** 0.25)
EPS = 1e-5
CONVW = 31
HALO = 30
NPL = 185
LSLOTS = 2
NPC = LSLOTS * NPL + 16

SAME_ENG_SYNC = True
DG_ENG = 'pool'


class Sched:
    def __init__(self, nc):
        self.nc = nc
        self.eng = {'pe': nc.tensor, 'act': nc.scalar, 'dve': nc.vector, 'pool': nc.gpsimd, 'sp': nc.sync}
        self.ops = []
        self.alias = {}
        self.sems = {}
        self.bank_i = 0
        self.frozen = False

    def op(self, eng, fn, reads=(), writes=()):
        if self.frozen:
            return
        self.ops.append(dict(eng=eng, fn=fn, r=list(reads), w=list(writes), dma=None))

    def dma(self, eng, out, in_, stream, reads=(), writes=()):
        if self.frozen:
            return
        self.ops.append(dict(eng=eng, fn=(lambda E, o=out, i=in_: E.dma_start(out=o, in_=i)),
                             r=list(reads), w=list(writes), dma=stream))

    def cc(self, fn, key, reads=(), writes=()):
        if self.frozen:
            return
        self.ops.append(dict(eng='pool', fn=fn, r=list(reads), w=list(writes), dma=key, inc=1))

    def mm(self, out, lhsT, rhs, start, stop, reads, writes, sgc=False):
        if sgc:
            self.op('pe', lambda E, o=out, l=lhsT, r=rhs, a=start, b=stop: E.matmul(o, l, r, start=a, stop=b, skip_group_check=True),
                    reads, writes)
        else:
            self.op('pe', lambda E, o=out, l=lhsT, r=rhs, a=start, b=stop: E.matmul(o, l, r, start=a, stop=b),
                    reads, writes)

    def act(self, out, in_, func, reads, writes, bias=0.0, scale=1.0):
        self.op('act', lambda E, o=out, i=in_, f=func, b=bias, s=scale: E.activation(out=o, in_=i, func=f, bias=b, scale=s),
                reads, writes)

    def tt(self, eng, out, in0, in1, op, reads, writes):
        self.op(eng, lambda E, o=out, a=in0, b=in1, p=op: E.tensor_tensor(out=o, in0=a, in1=b, op=p), reads, writes)

    def ts(self, eng, out, in0, s1, s2, op0, op1, reads, writes):
        if s2 is None:
            self.op(eng, lambda E, o=out, a=in0, x=s1, p=op0: E.tensor_scalar(out=o, in0=a, scalar1=x, scalar2=None, op0=p),
                    reads, writes)
        else:
            self.op(eng, lambda E, o=out, a=in0, x=s1, y=s2, p=op0, q=op1: E.tensor_scalar(out=o, in0=a, scalar1=x, scalar2=y, op0=p, op1=q),
                    reads, writes)

    def fence(self, eng, streams):
        if self.frozen:
            return
        self.ops.append(dict(eng=eng, fn=None, r=[], w=[], dma=None, fence=list(streams)))

    def stt(self, eng, out, in0, scalar, in1, op0, op1, reads, writes):
        self.op(eng, lambda E, o=out, a=in0, s=scalar, b=in1, p=op0, q=op1: E.scalar_tensor_tensor(out=o, in0=a, scalar=s, in1=b, op0=p, op1=q),
                reads, writes)

    def copy(self, eng, out, in_, reads, writes):
        if eng == 'act':
            self.act(out, in_, AF.Copy, reads, writes)
        else:
            self.op(eng, lambda E, o=out, i=in_: E.tensor_copy(out=o, in_=i), reads, writes)

    def memset(self, eng, ap, val, writes):
        self.op(eng, lambda E, a=ap, v=val: E.memset(a, v), (), writes)

    def _expand(self, toks):
        out = []
        for t in toks:
            a = self.alias.get(t)
            if a is None:
                out.append(t)
            else:
                out.extend(a)
        return out

    def _sem(self, v):
        s = self.sems.get(v)
        if s is None:
            s = self.nc.alloc_semaphore(name=("s_" + str(v)).replace("'", "").replace(" ", "").replace("(", "").replace(")", "").replace(",", "_"))
            self.sems[v] = s
        return s

    def emit(self, final_streams):
        ops = self.ops
        lastw = {}
        readers = {}
        last_in_stream = {}
        for i, o in enumerate(ops):
            v = ('dma', o['dma']) if o['dma'] else o['eng']
            o['v'] = v
            o['needed'] = False
            deps = set()
            R = self._expand(o['r'])
            W = self._expand(o['w'])
            for t in R:
                j = lastw.get(t)
                if j is not None:
                    deps.add(j)
            for t in W:
                j = lastw.get(t)
                if j is not None:
                    deps.add(j)
                for j in readers.get(t, {}).values():
                    deps.add(j)
            if o['dma']:
                j = last_in_stream.get(o['dma'])
                if j is not None:
                    deps.add(j)
                last_in_stream[o['dma']] = i
            deps.discard(i)
            dl = []
            for j in deps:
                vj = ops[j]['v']
                if vj == 'pe' and v == 'pe':
                    continue
                if (not SAME_ENG_SYNC) and vj == v and not o['dma']:
                    continue
                dl.append(j)
                ops[j]['needed'] = True
            o['deps'] = dl
            for t in W:
                lastw[t] = i
                readers[t] = {}
            for t in R:
                readers.setdefault(t, {})[v] = i
        cnt = {}
        seen = {e: {} for e in self.eng}
        for o in ops:
            e = o['eng']
            E = self.eng[e]
            v = o['v']
            if o['fn'] is None:
                for st in o['fence']:
                    vv = ('dma', st)
                    c = cnt.get(vv, 0)
                    if c and seen[e].get(vv, 0) < c:
                        E.wait_ge(self._sem(vv), c)
                        seen[e][vv] = c
                o['cnt'] = None
                continue
            need = {}
            for j in o['deps']:
                dj = ops[j]
                need[dj['v']] = max(need.get(dj['v'], 0), dj['cnt'])
            for vv, c in need.items():
                if seen[e].get(vv, 0) < c:
                    E.wait_ge(self._sem(vv), c)
                    seen[e][vv] = c
            ins = o['fn'](E)
            if o['dma']:
                inc = o.get('inc', 16)
                cnt[v] = cnt.get(v, 0) + inc
                ins.then_inc(self._sem(v), inc)
                o['cnt'] = cnt[v]
            elif o['needed']:
                cnt[v] = cnt.get(v, 0) + 1
                ins.then_inc(self._sem(v), 1)
                o['cnt'] = cnt[v]
            else:
                o['cnt'] = None
        for v in list(cnt):
            if isinstance(v, tuple) and v[0] == 'dma':
                self.eng['sp'].wait_ge(self._sem(v), cnt[v])
        self.max_cnt = dict(cnt)


def build_program(L=LSLOTS, TT=SEQ, dumps=(), stop_after=None, ncores=8):
    ntiles = TT // NT
    nsteps = ntiles + 1
    nc = bass.Bass("TRN2", target_bir_lowering=False)
    S = Sched(nc)
    dumps = set(dumps)
    dump_out = {}

    def din(name, shape, dt=F32):
        return nc.dram_tensor(name, list(shape), dt, kind="ExternalInput").ap()

    xT = din("xT", [D, TT])
    memT = din("memT", [D, MEM])
    w_in = din("w_in", [L, D, INC])
    w_mo = din("w_mix_out", [L, D, D])
    w_xq = din("w_xq", [L, D, D])
    w_xkv = din("w_xkv", [L, D, 2 * D])
    w_xo = din("w_xo", [L, D, D])
    w_fi = din("w_ffn_in", [L, D, 2 * DFF])
    w_fo = din("w_ffn_out", [L, DFF, D])
    pcols = din("pcols", [128, NPC])
    wa2e = din("wa2e", [L, 17, 256])
    consts = din("consts", [128, 900])
    flags = din("flags", [128, 4])
    outT = nc.dram_tensor("outT", [D, TT], F32, kind="ExternalOutput").ap()

    def dscr(name, shape, dt):
        return nc.dram_tensor(name, list(shape), dt).ap()

    contrib = dscr("contrib", [D, NT], F32)
    gath = dscr("gath", [2 * D, NT], F32)
    b_in = dscr("b_in", [L, D, INC], BF16)
    b_mo = dscr("b_mo", [L, D, D], BF16)
    b_xq = dscr("b_xq", [L, D, D], BF16)
    b_xkv = dscr("b_xkv", [L, D, 2 * D], BF16)
    b_xo = dscr("b_xo", [L, D, D], BF16)
    b_fi = dscr("b_fi", [L, D, 2 * DFF], BF16)
    b_fo = dscr("b_fo", [L, DFF, D], BF16)

    def sb(name, shape, dt):
        return nc.alloc_sbuf_tensor(name, list(shape), dt).ap()

    pc = sb("pc", [128, NPC], F32)
    cst = sb("cst", [128, 900], F32)
    Uneg = cst[:, 0:128]
    Lneg = cst[:, 128:256]
    mask4 = cst[:, 256:768]
    ones_bf = sb("ones_bf", [128, 128], BF16)
    wa2_f = sb("wa2_f", [32, 256], F32)
    wa2_b = sb("wa2_b", [32, L, 256], BF16)
    KT = sb("KT", [128, L, KC, MEM], BF16)
    Vb = sb("Vb", [128, L, 2, D], BF16)
    fl = sb("fl", [128, 4], F32)
    fA = fl[:, 0:1]
    fB = fl[:, 1:2]
    h32 = [sb("h32_%d" % i, [128, KC, NT], F32) for i in range(1)]
    hb = sb("hb", [128, KC, NT], BF16)
    NSLOT = 3
    wring = [sb("wr_%d" % i, [128, KC, 512], BF16) for i in range(NSLOT)]
    bufA = sb("bufA", [128, KC, NT], BF16)
    bufB = sb("bufB", [128, KC, NT], BF16)
    bigf = sb("big", [128, 12 * 512], F32)
    big = bigf.rearrange("p (a b) -> p a b", b=512)
    big_bf = bigf.bitcast(BF16).rearrange("p (j n) -> p j n", n=512)
    alT = sb("alT", [32, NT], BF16)
    lsb = sb("lsb", [128, 2, 256], F32)
    l_hi = sb("l_hi", [128, 4, 256], BF16)
    l_lo = sb("l_lo", [128, 4, 256], BF16)
    cst_b = sb("cst_b", [128, 384], BF16)
    ident_b = cst_b[:, 256:384]
    dg = sb("dg", [128, 4, CONVW, 128], BF16)
    Uneg_b = cst_b[:, 0:128]
    Lneg_b = cst_b[:, 128:256]
    ez = sb("ez", [128, 2, 256], F32)
    Ed = sb("Ed", [128, 4, 256], F32)
    EbT = sb("EbT", [128, 2, NT], F32)
    EnT = sb("EnT", [128, 2, NT], F32)
    sg32 = EbT
    decT = sb("decT", [128, 2, 8], F32)
    v_bf = sb("v_bf", [128, 4, 512], BF16)
    mem_b = v_bf.rearrange("p a b -> p (a b)").rearrange("p (k m) -> p k m", m=MEM)
    kd_bf = sb("kd_bf", [128, 4, 2, 256], BF16)
    qeT = sb("qeT", [128, 2, 2, NT], BF16)
    keT = sb("keT", [128, 2, NT], BF16)
    att_bf = sb("att_bf", [128, 2, 512], BF16)
    S32 = sb("S32", [128, L, 2, 256], F32)
    S_bf = sb("S_bf", [128, L, 2, 2, 256], BF16)
    ubuf = sb("ubuf", [128, L, 4, NT + HALO], BF16)
    m32 = sb("m32", [128, NT], F32)
    t32 = sb("t32", [128, NT], F32)
    r32 = sb("r32", [128, NT], F32)
    r32b = sb("r32b", [128, NT], F32)
    walo = sb("walo", [128, L, KC, 16], BF16)
    PT = EnT.rearrange("p a b -> p (a b)").bitcast(BF16).rearrange("p (s m t) -> p s m t", s=2, m=2)
    o32 = big[:, 0:4, :]
    cacc = big[:, 4:8, :]
    sr = big[:, 8:12, :]
    banks = [nc.alloc_psum_tensor("bank%d" % i, [128, 512], F32).ap() for i in range(8)]

    S.alias['bufA'] = [('bufA', k) for k in range(KC)]
    S.alias['bufB'] = [('bufB', k) for k in range(KC)]
    for k in range(KC):
        S.alias[('xo', k)] = [('bufA', k)]
        S.alias[('qT', k)] = [('bufB', k)]
    for k in range(4):
        S.alias[('mixin', 4 + k)] = [('bufA', 4 + k)]
    pinned = set()
    for i_ in range(2):
        S.alias[('sg', i_)] = [('EbT', i_)]
        for m_ in range(2):
            S.alias[('PT', i_, m_)] = [('EnT', i_)]
    for hs_ in range(1):
        S.alias[('h32', hs_)] = [('h32c', hs_, k) for k in range(KC)]
    S.alias['bigall'] = [('big', u) for u in range(12)]
    for h in range(4):
        S.alias[('o32', h)] = [('big', h)]
        S.alias[('cacc', h)] = [('big', 4 + h)]
        S.alias[('sr', h)] = [('big', 8 + h)]
    for j in range(HC):
        S.alias[('hid', j)] = [('big', j // 2)]

    def nbank(pin=False):
        i = S.bank_i
        while i in pinned:
            i = (i + 1) % 8
        S.bank_i = (i + 1) % 8
        if pin:
            pinned.add(i)
        return banks[i], ('bank', i)

    def unpin(tok):
        pinned.discard(tok[1])

    def stage(name):
        if stop_after == name:
            S.frozen = True

    def dump(name, ap, reads):
        if name not in dumps or S.frozen:
            return
        d = nc.dram_tensor("dbg_" + name, list(ap.shape), ap.dtype, kind="ExternalOutput").ap()
        dump_out[name] = d
        S.dma('sp', d, ap, 'dbg_' + name, reads=reads, writes=[('dbg', name)])

    S.dma('sp', pc, pcols, 'misc', writes=['pc'])
    S.dma('sp', cst, consts, 'misc', writes=['cst'])
    S.dma('sp', fl, flags, 'misc', writes=['fl'])
    S.memset('dve', ones_bf, 1.0, ['ones'])
    S.copy('dve', cst_b[:, 0:256], cst[:, 0:256], ['cst'], ['cst_b'])
    S.copy('dve', cst_b[:, 256:384], cst[:, 772:900], ['cst'], ['cst_b'])
    S.memset('pool', wa2_b, 0.0, [('wa2_b', l_) for l_ in range(L)])
    S.memset('pool', alT, 1.0, ['alT'])

    cast_rr = [0]
    stage_i = [0]

    def convert(src, dst, R, C):
        cw = 4096
        for rc in range(R // 128):
            c0 = 0
            while c0 < C:
                n = min(cw, C - c0)
                si = stage_i[0] % 2
                stage_i[0] += 1
                s32 = (h32[0].rearrange("p a b -> p (a b)") if si == 0 else bigf)[:, 0:n]
                sbf = (bufA if si == 0 else bufB).rearrange("p a b -> p (a b)")[:, 0:n]
                t32 = ('h32', 0) if si == 0 else 'bigall'
                tbf = 'bufA' if si == 0 else 'bufB'
                S.dma('sp', s32, src[rc * 128:(rc + 1) * 128, c0:c0 + n], 'cv_in%d' % si, writes=[t32])
                e = ['act', 'dve'][cast_rr[0] % 2]
                cast_rr[0] += 1
                S.copy(e, sbf, s32, [t32], [tbf])
                S.dma('sp', dst[rc * 128:(rc + 1) * 128, c0:c0 + n], sbf, 'cv_out%d' % si, reads=[tbf], writes=[])
                c0 += n

    wsets = [(w_in, b_in, D, INC), (w_mo, b_mo, D, D), (w_xq, b_xq, D, D), (w_xkv, b_xkv, D, 2 * D),
             (w_xo, b_xo, D, D), (w_fi, b_fi, D, 2 * DFF), (w_fo, b_fo, DFF, D)]
    wb_tok = {}
    for l in range(L):
        for (src, dst, R, C) in wsets:
            convert(src[l], dst[l], R, C)
    S.fence('sp', ['cv_out0', 'cv_out1'])
    stage('prologue')

    ring_i = [0]

    def wload(src2d, r0, nk, c0, ncols):
        s = ring_i[0] % NSLOT
        ring_i[0] += 1
        dst = wring[s][:, 0:nk, 0:ncols]
        S.dma('sp', dst, src2d[r0:r0 + nk * 128, c0:c0 + ncols].rearrange("(kc p) c -> p kc c", p=128),
              'wr%d' % s, reads=[], writes=[('wr', s)])
        return wring[s], ('wr', s)

    def mm_acc(out, pairs, reads, btok):
        n = len(pairs)
        for i, (l, r) in enumerate(pairs):
            S.mm(out, l, r, i == 0, i == n - 1, reads, [btok])

    def layer_norm(y, ytok_list, gcol, bcol, nk, inv_n, out32, out32_tok, outb, outb_tok, func=AF.Identity,
                   flagged=False, pre=None):
        if pre is None:
            ybv = bufA[:, 0:nk, :]
            ysv = bufB[:, 0:nk, :]
            ta = [('bufA', k) for k in range(nk)]
            tb = [('bufB', k) for k in range(nk)]
            S.copy('dve', ybv, y, ytok_list, ta)
            S.act(ysv, y, AF.Square, ytok_list, tb)
            b1, t1 = nbank()
            mm_acc(b1, [(ones_bf, ybv[:, k, :]) for k in range(nk)], ['ones'] + ta, t1)
            b2, t2 = nbank()
            mm_acc(b2, [(ones_bf, ysv[:, k, :]) for k in range(nk)], ['ones'] + tb, t2)
        else:
            b1, t1, b2, t2 = pre
        S.ts('dve', m32, b1, inv_n, None, ALU.mult, ALU.bypass, [t1], ['m32'])
        if flagged:
            S.ts('dve', m32, m32, fA, None, ALU.mult, None, ['m32', 'fl'], ['m32'])
        S.tt('dve', t32, m32, m32, ALU.mult, ['m32'], ['t32'])
        S.stt('dve', r32, b2, inv_n, t32, ALU.mult, ALU.subtract, [t2, 't32'], ['r32'])
        S.ts('dve', r32, r32, EPS, None, ALU.add, None, ['r32'], ['r32'])
        S.act(r32, r32, AF.Ln, ['r32'], ['r32'])
        S.act(r32, r32, AF.Exp, ['r32'], ['r32'], scale=-0.5)
        if flagged:
            S.ts('dve', r32, r32, fA, fB, ALU.mult, ALU.add, ['r32', 'fl'], ['r32'])
        for k in range(nk):
            S.tt('dve', y[:, k, :], y[:, k, :], m32, ALU.subtract, [ytok_list[k], 'm32'], [ytok_list[k]])
        for k in range(nk):
            S.tt('dve', y[:, k, :], y[:, k, :], r32, ALU.mult, [ytok_list[k], 'r32'], [ytok_list[k]])
            if outb is not None:
                S.act(outb[:, k, :], y[:, k, :], func, [ytok_list[k], 'pc'], [outb_tok[k]],
                      bias=pc[:, bcol + k:bcol + k + 1], scale=pc[:, gcol + k:gcol + k + 1])
        if out32 is not None:
            for k in range(nk):
                S.act(out32[:, k, :], y[:, k, :], func, [ytok_list[k], 'pc'], [out32_tok[k]],
                      bias=pc[:, bcol + k:bcol + k + 1], scale=pc[:, gcol + k:gcol + k + 1])

    def resid_stats(j, hcur, htoks, acc_bank, acc_tok, st):
        b1, t1, b2, t2 = st
        S.stt('dve', hcur[:, j, :], hcur[:, j, :], ALPHA, acc_bank, ALU.mult, ALU.add, [htoks[j], acc_tok], [htoks[j]])
        S.copy('dve', hb[:, j, :], hcur[:, j, :], [htoks[j]], [('hb', j)])
        S.act(bufB[:, j, :], hcur[:, j, :], AF.Square, [htoks[j]], [('bufB', j)])

    def stats_mm(j, st):
        b1, t1, b2, t2 = st
        S.mm(b1, ones_bf, hb[:, j, :], (j == 0), (j == KC - 1), ['ones', ('hb', j)], [t1], sgc=True)
        S.mm(b2, ones_bf, bufB[:, j, :], (j == 0), (j == KC - 1), ['ones', ('bufB', j)], [t2], sgc=True)

    def proj_residual_ln(l, wsrc, rhs_buf, rhs_tok, nkin, hcur, htoks, gcol, bcol):
        b1, t1 = nbank(pin=True)
        b2, t2 = nbank(pin=True)
        st = (b1, t1, b2, t2)
        for cg in range(2):
            wt, wtok = wload(wsrc, 0, nkin, cg * 512, 512)
            for jj in range(4):
                j = cg * 4 + jj
                bk, bt = nbank()
                mm_acc(bk, [(wt[:, k, jj * 128:(jj + 1) * 128], rhs_buf[:, k, :]) for k in range(nkin)],
                       [wtok] + rhs_tok, bt)
                resid_stats(j, hcur, htoks, bk, bt, st)
                if j > 0:
                    stats_mm(j - 1, st)
        stats_mm(KC - 1, st)
        unpin(t1)
        unpin(t2)
        layer_norm(hcur, htoks, gcol, bcol, KC, 1.0 / D, hcur, htoks, hb, [('hb', k) for k in range(KC)], pre=st)

    hbt = [('hb', k) for k in range(KC)]

    hbt = [('hb', k) for k in range(KC)]
    mem32 = big[:, 0:4, :].rearrange("p a b -> p (a b)").rearrange("p (k m) -> p k m", m=MEM)
    S.dma('sp', mem32, memT.rearrange("(kc p) m -> p kc m", p=128), 'misc', writes=[('o32', h) for h in range(4)])
    S.copy('dve', mem_b, mem32, [('o32', h) for h in range(4)], [('v_bf', i) for i in range(4)])
    S.alias['mem_b'] = [('v_bf', i) for i in range(4)]
    sslot = [[0, 0] for _ in range(L)]
    for l in range(L):
        S.dma('sp', walo[:, l, :, :], b_in[l][:, 2560:2576].rearrange("(kc p) c -> p kc c", p=128), 'misc', writes=[('walo', l)])
        S.dma('sp', wa2_f[0:17, :], wa2e[l], 'misc', writes=['wa2_f'])
        S.copy('dve', wa2_b[0:17, l, :], wa2_f[0:17, :], ['wa2_f'], [('wa2_b', l)])
        S.memset('dve', S32[:, l, :, :], 0.0, [('S32', l, 0), ('S32', l, 1)])
        S.memset('pool', S_bf[:, l, :, :, :], 0.0, [('S_bf', l, a, b) for a in range(2) for b in range(2)])
        S.memset('pool', ubuf[:, l, :, :], 0.0, [('ubuf', l, c) for c in range(4)])
        for cg in range(2):
            wt, wtok = wload(b_xkv[l], 0, KC, cg * 512, 512)
            for jj in range(4):
                j = cg * 4 + jj
                bk, bt = nbank()
                mm_acc(bk[:, 0:MEM], [(wt[:, k, jj * 128:(jj + 1) * 128], mem_b[:, k, :]) for k in range(KC)],
                       [wtok, 'mem_b'], bt)
                S.copy('act', KT[:, l, j, :], bk[:, 0:MEM], [bt], [('KT', l)])
        for cg in range(2):
            wt, wtok = wload(b_xkv[l], 0, KC, D + cg * 512, 512)
            for mc in range(2):
                bk, bt = nbank()
                mm_acc(bk, [(mem_b[:, k, mc * 128:(mc + 1) * 128], wt[:, k, :]) for k in range(KC)],
                       [wtok, 'mem_b'], bt)
                S.copy('act', Vb[:, l, mc, cg * 512:(cg + 1) * 512], bk, [bt], [('Vb', l)])
    S.dma('sp', gath[0:D, :], xT[:, 0:NT], 'misc', writes=['gath'])
    groups = [[2 * i, 2 * i + 1] for i in range(ncores // 2)]
    hcur = h32[0]
    htoks = [('h32c', 0, k) for k in range(KC)]
    hs = 0
    xa32 = bufA.rearrange("p a b -> p (a b)").bitcast(F32).rearrange("p (k t) -> p k t", t=NT)
    xb32 = bufB.rearrange("p a b -> p (a b)").bitcast(F32).rearrange("p (k t) -> p k t", t=NT)

    def build_dg(l_):
        c_cw_ = 16 + l_ * NPL + 48
        for ch in range(4):
            for k in range(CONVW):
                wc_ = pc[:, c_cw_ + ch * CONVW + k:c_cw_ + ch * CONVW + k + 1]
                if ch < 2:
                    S.act(dg[:, ch, k, :], ident_b, AF.Identity, ['cst_b', 'pc'], [('dg', ch)], scale=wc_)
                else:
                    S.ts('dve', dg[:, ch, k, :], ident_b, wc_, None, ALU.mult, None, ['cst_b', 'pc'], [('dg', ch)])

    build_dg(0)
    for s_ in range(nsteps):
        ti_in = min(s_, ntiles - 1)
        ti_out = max(s_ - 1, 0)
        for half in range(2):
            ksl = slice(half * 4, (half + 1) * 4)
            S.dma('sp', xa32, xT[half * 512:(half + 1) * 512, ti_in * NT:(ti_in + 1) * NT].rearrange("(kc p) t -> p kc t", p=128),
                  'xin', writes=['bufA'])
            S.dma('sp', xb32, gath[half * 512:(half + 1) * 512, :].rearrange("(kc p) t -> p kc t", p=128),
                  'gin', reads=['gath'], writes=['bufB'])
            S.ts('dve', hcur[:, ksl, :], xa32, fA, None, ALU.mult, None, ['bufA', 'fl'], htoks[half * 4:(half + 1) * 4])
            S.stt('dve', hcur[:, ksl, :], xb32, fB, hcur[:, ksl, :], ALU.mult, ALU.add,
                  ['bufB', 'fl'] + htoks[half * 4:(half + 1) * 4], htoks[half * 4:(half + 1) * 4])
        layer_norm(hcur, htoks, 0, 8, KC, 1.0 / D, hcur, htoks, hb, hbt, flagged=True)
        for l in range(L):
            P0 = 16 + l * NPL
            c_ln1g, c_ln1b, c_ln2g, c_ln2b, c_ln3g, c_ln3b = [P0 + 8 * i for i in range(6)]
            c_cw = P0 + 48
            c_cb = c_cw + 124
            c_clg = c_cb + 4
            c_clb = c_clg + 4
            c_gg = c_clb + 4

            bk, bt = nbank()
            mm_acc(bk[0:16, :], [(walo[:, l, k, :], hb[:, k, :]) for k in range(KC)], [('walo', l)] + hbt, bt)
            S.copy('act', alT[0:16, :], bk[0:16, :], [bt], ['alT'])
            bc0, bct0 = nbank(pin=True)
            bc1, bct1 = nbank(pin=True)
            bcs = [(bc0, bct0), (bc1, bct1)]
            for sbk in range(4):
                tsl = slice(sbk * 128, (sbk + 1) * 128)
                zb, zt = nbank()
                S.mm(zb[:, 0:256], alT[0:17, tsl], wa2_b[0:17, l, :], True, True, ['alT', ('wa2_b', l)], [zt])
                es = sbk % 2
                S.act(ez[:, es, :], zb[:, 0:256], AF.Exp, [zt], [('ez', es)], scale=-1.0)
                S.act(lsb[:, es, :], ez[:, es, :], AF.Ln, [('ez', es)], [('lsb', es)], bias=1.0)
                S.copy('dve', l_hi[:, sbk, :], lsb[:, es, :], [('lsb', es)], [('l_hi', sbk)])
                S.tt('dve', l_lo[:, sbk, :], lsb[:, es, :], l_hi[:, sbk, :], ALU.subtract, [('lsb', es), ('l_hi', sbk)], [('l_lo', sbk)])
                for fc in range(2):
                    S.mm(bcs[fc][0][:, tsl], l_hi[:, sbk, fc * 128:(fc + 1) * 128], Uneg_b, (sbk == 0), False,
                         [('l_hi', sbk), 'cst_b'], [bcs[fc][1]], sgc=True)
                    S.mm(bcs[fc][0][:, tsl], l_lo[:, sbk, fc * 128:(fc + 1) * 128], Uneg_b, False, True,
                         [('l_lo', sbk), 'cst_b'], [bcs[fc][1]], sgc=True)
                db, dt_ = nbank()
                S.mm(db[:, 0:256], Lneg_b, l_hi[:, sbk, :], True, False, [('l_hi', sbk), 'cst_b'], [dt_])
                S.mm(db[:, 0:256], Lneg_b, l_lo[:, sbk, :], False, True, [('l_lo', sbk), 'cst_b'], [dt_])
                S.act(Ed[:, sbk, :], db[:, 0:256], AF.Exp, [dt_], [('Ed', sbk)])
            for fc in range(2):
                S.act(EbT[:, fc, :], bcs[fc][0], AF.Exp, [bcs[fc][1]], [('EbT', fc)])
                S.act(EnT[:, fc, :], bcs[fc][0], AF.Exp, [bcs[fc][1]], [('EnT', fc)], scale=-1.0)
                S.act(decT[:, fc, :], bcs[fc][0].rearrange("p (c t) -> p c t", t=64)[:, :, 63], AF.Exp,
                      [bcs[fc][1]], [('decT', fc)])
            unpin(bct0)
            unpin(bct1)
            stage('gates')
            if l > 0:
                build_dg(l)
            wt3, wtok3 = wload(b_in[l], 0, KC, 1024, 512)
            wt4, wtok4 = wload(b_in[l], 0, KC, 1536, 512)
            for sbk in range(4):
                tsl = slice(sbk * 128, (sbk + 1) * 128)
                vb_, vt_ = nbank()
                mm_acc(vb_, [(hb[:, k, tsl], wt4[:, k, :]) for k in range(KC)], [wtok4] + hbt, vt_)
                S.copy('act', v_bf[:, sbk, :], vb_, [vt_], [('v_bf', sbk)])
                kb_, kt_ = nbank()
                mm_acc(kb_[:, 0:256], [(hb[:, k, tsl], wt3[:, k, 256:512]) for k in range(KC)], [wtok3] + hbt, kt_)
                for cc in range(2):
                    S.stt('dve', kd_bf[:, sbk, cc, :], kb_[:, 0:256], cst[:, 768 + cc:769 + cc], Ed[:, sbk, :], ALU.mult, ALU.mult,
                          [kt_, ('Ed', sbk), 'cst'], [('kd_bf', sbk)])
            stage('kv')
            for fc in range(2):
                qb_, qt_ = nbank()
                mm_acc(qb_, [(wt3[:, k, fc * 128:(fc + 1) * 128], hb[:, k, :]) for k in range(KC)], [wtok3] + hbt, qt_)
                for hl in range(2):
                    S.stt('dve', qeT[:, fc, hl, :], qb_, cst[:, 770 + hl:771 + hl], EbT[:, fc, :], ALU.mult, ALU.mult,
                          [qt_, ('EbT', fc), 'cst'], [('qeT', fc)])
                kb_, kt_ = nbank()
                mm_acc(kb_, [(wt3[:, k, 256 + fc * 128:256 + (fc + 1) * 128], hb[:, k, :]) for k in range(KC)], [wtok3] + hbt, kt_)
                S.tt('dve', keT[:, fc, :], kb_, EnT[:, fc, :], ALU.mult, [kt_, ('EnT', fc)], [('keT', fc)])
            if l == 0 and s_ == 0:
                dump('qeT', qeT, [('qeT', 0), ('qeT', 1)])
                dump('keT', keT, [('keT', 0), ('keT', 1)])
                dump('kd', kd_bf, [('kd_bf', i) for i in range(4)])
                dump('vbf', v_bf, [('v_bf', i) for i in range(4)])
                dump('decT', decT, [('decT', 0), ('decT', 1)])
            stage('qk')
            wt1, wtok1 = wload(b_in[l], 0, KC, 0, 512)
            wt2, wtok2 = wload(b_in[l], 0, KC, 512, 512)

            def conv_a(ch):
                gb, gt = nbank()
                mm_acc(gb, [(wt2[:, k, ch * 128:(ch + 1) * 128], hb[:, k, :]) for k in range(KC)], [wtok2] + hbt, gt)
                ss = ch % 2
                S.act(sg32[:, ss, :], gb, AF.Sigmoid, [gt], [('sg', ss)])
                ab_, at_ = nbank()
                mm_acc(ab_, [(wt1[:, k, ch * 128:(ch + 1) * 128], hb[:, k, :]) for k in range(KC)], [wtok1] + hbt, at_)
                S.tt('dve', ubuf[:, l, ch, HALO:HALO + NT], ab_, sg32[:, ss, :], ALU.mult, [at_, ('sg', ss)], [('ubuf', l, ch)])

            def conv_b(ch):
                cb_, ct_ = nbank()
                mm_acc(cb_, [(dg[:, ch, k, :], ubuf[:, l, ch, k:k + NT]) for k in range(CONVW)], [('dg', ch), ('ubuf', l, ch)], ct_)
                S.act(cacc[:, ch, :], cb_, AF.Identity, [ct_, 'pc'], [('cacc', ch)], bias=pc[:, c_cb + ch:c_cb + ch + 1])
            for sbk in range(4):
                tsl = slice(sbk * 128, (sbk + 1) * 128)
                ab, at = nbank()
                for h in range(4):
                    fc, hl = h // 2, h % 2
                    S.mm(ab[:, h * 128:(h + 1) * 128], keT[:, fc, tsl], qeT[:, fc, hl, tsl],
                         True, True, [('keT', fc), ('qeT', fc)], [at])
                asl = sbk % 2
                S.tt('dve', att_bf[:, asl, :], ab, mask4, ALU.mult, [at, 'cst'], [('att', asl)])
                stage('att')
                ob, ot = nbank(pin=True)
                for h in range(4):
                    S.mm(ob[:, h * 128:(h + 1) * 128], v_bf[:, sbk, h * 128:(h + 1) * 128], att_bf[:, asl, h * 128:(h + 1) * 128],
                         (h == 0), False, [('v_bf', sbk), ('att', asl)], [ot], sgc=True)
                stage('intra')
                for cc in range(2):
                    ci = sbk * 2 + cc
                    csl = slice(sbk * 128 + cc * 64, sbk * 128 + (cc + 1) * 64)
                    for h in range(4):
                        fc, hl = h // 2, h % 2
                        sl = sslot[l][fc]
                        S.mm(ob[:, h * 128 + cc * 64:h * 128 + (cc + 1) * 64],
                             S_bf[:, l, fc, sl, hl * 128:(hl + 1) * 128],
                             qeT[:, fc, hl, csl], False, (cc == 1),
                             [('S_bf', l, fc, sl), ('qeT', fc)], [ot], sgc=True)
                    stage('inter0')
                    for hp in range(2):
                        ub, ut = nbank()
                        S.mm(ub[:, 0:256], kd_bf[:, sbk, cc, hp * 128:(hp + 1) * 128],
                             v_bf[:, sbk, hp * 256:(hp + 1) * 256], True, True,
                             [('kd_bf', sbk), ('v_bf', sbk)], [ut])
                        S.stt('dve', S32[:, l, hp, :], S32[:, l, hp, :], decT[:, hp, ci:ci + 1], ub[:, 0:256], ALU.mult, ALU.add,
                              [('S32', l, hp), ('decT', hp), ut], [('S32', l, hp)])
                        ns = 1 - sslot[l][hp]
                        S.copy('act', S_bf[:, l, hp, ns, :], S32[:, l, hp, :], [('S32', l, hp)], [('S_bf', l, hp, ns)])
                        sslot[l][hp] = ns
                        stage('upd0')
                    if cc == 0:
                        conv_a(sbk)
                    else:
                        conv_b(sbk)
                unpin(ot)
                S.copy('act', o32[:, :, tsl], ob.rearrange("p (h c) -> p h c", c=128), [ot], [('o32', h) for h in range(4)])
            if l == 0 and s_ == 0:
                dump('o32', o32, [('o32', h) for h in range(4)])
            stage('gla')
            for ch in range(4):
                S.copy('pool', ubuf[:, l, ch, 0:HALO], ubuf[:, l, ch, NT:NT + HALO], [('ubuf', l, ch)], [('ubuf', l, ch)])
            if l == 0 and s_ == 0:
                dump('cacc', cacc, [('cacc', c) for c in range(4)])
            wt5, wtok5 = wload(b_in[l], 0, KC, 2048, 512)
            for h in range(4):
                rb, rt = nbank()
                mm_acc(rb, [(wt5[:, k, h * 128:(h + 1) * 128], hb[:, k, :]) for k in range(KC)], [wtok5] + hbt, rt)
                S.act(sr[:, h, :], rb, AF.Silu, [rt], [('sr', h)])
            S.act(bufB[:, 0:4, :], o32, AF.Square, [('o32', h) for h in range(4)], [('bufB', k) for k in range(4)],
                  scale=float(128 ** -0.5))
            for h in range(4):
                rr, rrt = (r32, 'r32') if h % 2 == 0 else (r32b, 'r32b')
                mb, mt = nbank()
                S.mm(mb, ones_bf, bufB[:, h, :], True, True, ['ones', ('bufB', h)], [mt])
                S.ts('dve', rr, mb, EPS, None, ALU.add, None, [mt], [rrt])
                S.act(rr, rr, AF.Ln, [rrt], [rrt])
                S.act(rr, rr, AF.Exp, [rrt], [rrt], scale=-0.5)
                S.tt('dve', o32[:, h, :], o32[:, h, :], rr, ALU.mult, [('o32', h), rrt], [('o32', h)])
                S.stt('dve', bufA[:, 4 + h, :], o32[:, h, :], pc[:, c_gg:c_gg + 1], sr[:, h, :], ALU.mult, ALU.mult,
                      [('o32', h), ('sr', h), 'pc'], [('mixin', 4 + h)])
            layer_norm(cacc, [('cacc', c) for c in range(4)], c_clg, c_clb, 4, 1.0 / 512, None, None,
                       bufA[:, 0:4, :], [('bufA', c) for c in range(4)], func=AF.Silu)
            if l == 0 and s_ == 0:
                dump('mixin', bufA, ['bufA'])
            proj_residual_ln(l, b_mo[l], bufA, ['bufA'], KC, hcur, htoks, c_ln1g, c_ln1b)
            if l == 0 and s_ == 0:
                dump('h1', hcur, [('h32', hs)])
            stage('mixer')

            for cg in range(2):
                wt, wtok = wload(b_xq[l], 0, KC, cg * 512, 512)
                for jj in range(4):
                    j = cg * 4 + jj
                    bk, bt = nbank()
                    mm_acc(bk, [(wt[:, k, jj * 128:(jj + 1) * 128], hb[:, k, :]) for k in range(KC)], [wtok] + hbt, bt)
                    S.act(bufB[:, j, :], bk, AF.Identity, [bt], [('qT', j)], scale=1.0 / 16)
            def x_scores(h):
                psl = h % 2
                for mc in range(2):
                    sbk_, st_ = nbank()
                    mm_acc(sbk_, [(KT[:, l, 2 * h + dc, mc * 128:(mc + 1) * 128], bufB[:, 2 * h + dc, :]) for dc in range(2)],
                           [('KT', l), ('qT', 2 * h), ('qT', 2 * h + 1)], st_)
                    S.act(PT[:, psl, mc, :], sbk_, AF.Exp, [st_], [('PT', psl, mc)])

            def x_pv(h):
                psl = h % 2
                db_, dt2 = nbank()
                mm_acc(db_, [(ones_bf, PT[:, psl, mc, :]) for mc in range(2)], ['ones', ('PT', psl, 0), ('PT', psl, 1)], dt2)
                rr, rrt = (r32, 'r32') if h % 2 == 0 else (r32b, 'r32b')
                S.op('dve', lambda E, o=rr, i=db_: E.reciprocal(out=o, in_=i), [dt2], [rrt])
                for dc in range(2):
                    ob_, ot_ = nbank()
                    mm_acc(ob_, [(Vb[:, l, mc, h * 256 + dc * 128:h * 256 + (dc + 1) * 128], PT[:, psl, mc, :]) for mc in range(2)],
                           [('Vb', l), ('PT', psl, 0), ('PT', psl, 1)], ot_)
                    S.tt('dve', bufA[:, 2 * h + dc, :], ob_, rr, ALU.mult, [ot_, rrt], [('xo', 2 * h + dc)])

            x_scores(0)
            for h in range(4):
                if h + 1 < 4:
                    x_scores(h + 1)
                x_pv(h)
            if l == 0 and s_ == 0:
                dump('xo', bufA, [('xo', j) for j in range(8)])
            proj_residual_ln(l, b_xo[l], bufA, [('xo', j) for j in range(8)], KC, hcur, htoks, c_ln2g, c_ln2b)
            if l == 0 and s_ == 0:
                dump('h2', hcur, [('h32', hs)])
            stage('xattn')

            for g in range(6):
                ncol = 512 if g < 5 else 256
                wg, wgt = wload(b_fi[l], 0, KC, g * 512, ncol)
                wu, wut = wload(b_fi[l], 0, KC, DFF + g * 512, ncol)
                for jj in range(ncol // 128):
                    j = g * 4 + jj
                    gb, gt = nbank()
                    mm_acc(gb, [(wg[:, k, jj * 128:(jj + 1) * 128], hb[:, k, :]) for k in range(KC)], [wgt] + hbt, gt)
                    ss = j % 2
                    S.act(sg32[:, ss, :], gb, AF.Silu, [gt], [('sg', ss)])
                    ub_, ut_ = nbank()
                    mm_acc(ub_, [(wu[:, k, jj * 128:(jj + 1) * 128], hb[:, k, :]) for k in range(KC)], [wut] + hbt, ut_)
                    S.tt('dve', big_bf[:, j, :], ub_, sg32[:, ss, :], ALU.mult, [ut_, ('sg', ss)], [('hid', j)])
            b1_, t1_ = nbank(pin=True)
            b2_, t2_ = nbank(pin=True)
            st3 = (b1_, t1_, b2_, t2_)
            for cg in range(2):
                accs = [nbank(pin=True) for _ in range(4)]
                for rg in range(3):
                    nk = 8 if rg < 2 else 6
                    wt, wtok = wload(b_fo[l], rg * 1024, nk, cg * 512, 512)
                    for jj in range(4):
                        for k in range(nk):
                            hc = rg * 8 + k
                            S.mm(accs[jj][0], wt[:, k, jj * 128:(jj + 1) * 128], big_bf[:, hc, :], (hc == 0), (hc == HC - 1),
                                 [wtok, ('hid', hc)], [accs[jj][1]])
                if cg == 1:
                    for j_ in range(4):
                        stats_mm(j_, st3)
                for jj in range(4):
                    j = cg * 4 + jj
                    unpin(accs[jj][1])
                    resid_stats(j, hcur, htoks, accs[jj][0], accs[jj][1], st3)
                    if cg == 1 and jj > 0:
                        stats_mm(j - 1, st3)
            stats_mm(KC - 1, st3)
            unpin(t1_)
            unpin(t2_)
            if l < L - 1:
                layer_norm(hcur, htoks, c_ln3g, c_ln3b, KC, 1.0 / D, hcur, htoks, hb, hbt, pre=st3)
            else:
                layer_norm(hcur, htoks, c_ln3g, c_ln3b, KC, 1.0 / D, hcur, htoks, None, None, pre=st3)
        for k in range(KC):
            S.dma('sp', contrib[k * 128:(k + 1) * 128, :], hcur[:, k, :], 'hst0', reads=[htoks[k]], writes=['contrib'])
        S.dma('sp', outT[:, ti_out * NT:(ti_out + 1) * NT].rearrange("(kc p) t -> p kc t", p=128), hcur, 'hst1',
              reads=[('h32', 0)], writes=['out_dram'])
        if s_ == 0:
            for l in range(L):
                for hp in range(2):
                    S.ts('dve', S32[:, l, hp, :], S32[:, l, hp, :], fA, None, ALU.mult, None, [('S32', l, hp), 'fl'], [('S32', l, hp)])
                    for sl in range(2):
                        S.ts('dve', S_bf[:, l, hp, sl, :], S_bf[:, l, hp, sl, :], fA, None, ALU.mult, None,
                             [('S_bf', l, hp, sl), 'fl'], [('S_bf', l, hp, sl)])
                for ch in range(4):
                    S.ts('dve', ubuf[:, l, ch, 0:HALO], ubuf[:, l, ch, 0:HALO], fA, None, ALU.mult, None,
                         [('ubuf', l, ch), 'fl'], [('ubuf', l, ch)])
        if s_ < nsteps - 1:
            build_dg(0)
            S.cc(lambda E, g=groups: E.collective_compute("AllGather", ALU.bypass, replica_groups=g,
                                                          ins=[contrib.opt()], outs=[gath.opt()]),
                 'cc%d' % s_, reads=['contrib'], writes=['gath'])

    S.sbuf_free = nc.sbuf_bytes_remaining
    S.emit(None)
    return nc, S, dump_out


def make_consts():
    s = np.arange(128)[:, None]
    t = np.arange(128)[None, :]
    same = (s // 64) == (t // 64)
    U = np.where(same & (s <= t), -1.0 / 16.0, 0.0).astype(np.float32)
    Lm = np.where(same & (s > t), -1.0 / 16.0, 0.0).astype(np.float32)
    M = np.where(same & (s <= t), 1.0, 0.0).astype(np.float32)
    p = np.arange(128)[:, None]
    m0 = (p < 64).astype(np.float32)
    m1 = (p >= 64).astype(np.float32)
    return np.concatenate([U, Lm, np.tile(M, (1, 4)), m0, m1, 0.125 * m0, 0.125 * m1, np.eye(128, dtype=np.float32)], axis=1).astype(np.float32)


def pack_pcols(inp, layers, ln0_identity):
    pcw = np.zeros((128, NPC), np.float32)

    def fm(v):
        v = np.asarray(v, np.float32)
        return v.reshape(-1, 128).T

    if ln0_identity:
        pcw[:, 0:8] = 1.0
        pcw[:, 8:16] = 0.0
    else:
        pcw[:, 0:8] = fm(inp["ln0_g"])
        pcw[:, 8:16] = fm(inp["ln0_b"])
    for li, l in enumerate(layers):
        P0 = 16 + li * NPL
        for i, nm in enumerate(["ln1_g", "ln1_b", "ln2_g", "ln2_b", "ln3_g", "ln3_b"]):
            pcw[:, P0 + 8 * i:P0 + 8 * i + 8] = fm(inp[nm][l])
        cw = np.asarray(inp["conv_w"][l], np.float32)
        c = P0 + 48
        for ch in range(4):
            pcw[:, c + ch * CONVW:c + (ch + 1) * CONVW] = cw[:, ch * 128:(ch + 1) * 128].T
        c += 124
        pcw[:, c:c + 4] = fm(inp["conv_b"][l]); c += 4
        pcw[:, c:c + 4] = fm(inp["conv_ln_g"][l]); c += 4
        pcw[:, c:c + 4] = fm(inp["conv_ln_b"][l]); c += 4
        pcw[:, c:c + 1] = fm(inp["gla_norm_g"][l])
    return pcw


_CACHE = {}
WNAMES = ["w_in", "w_mix_out", "w_xq", "w_xkv", "w_xo", "w_ffn_in", "w_ffn_out"]


def make_in_maps(inp, nb, T):
    cst = make_consts()
    wa2e_all = np.concatenate([inp["w_a2"], inp["b_a"][:, None, :]], axis=1).astype(np.float32)
    stage = []
    for st in range(2):
        layers = [2 * st, 2 * st + 1]
        d = {nm: np.ascontiguousarray(np.asarray(inp[nm], np.float32)[layers[0]:layers[1] + 1]) for nm in WNAMES}
        d["pcols"] = pack_pcols(inp, layers, ln0_identity=(st == 1))
        d["wa2e"] = np.ascontiguousarray(wa2e_all[layers[0]:layers[1] + 1])
        d["consts"] = cst
        fl = np.zeros((128, 4), np.float32)
        fl[:, 0] = 1.0 if st == 0 else 0.0
        fl[:, 1] = 0.0 if st == 0 else 1.0
        d["flags"] = fl
        stage.append(d)
    in_maps = []
    for b in range(nb):
        xT = np.ascontiguousarray(np.asarray(inp["x"][b][:T]).T, np.float32)
        mT = np.ascontiguousarray(np.asarray(inp["mem"][b]).T, np.float32)
        for st in range(2):
            m = dict(stage[st])
            m["xT"] = xT
            m["memT"] = mT
            in_maps.append(m)
    return in_maps


def kernel(**inputs):
    inp = {k: np.asarray(v) for k, v in inputs.items()}
    x = inp["x"]
    B, T, _ = x.shape
    key = (T, 2 * B)
    if key not in _CACHE:
        _CACHE[key] = build_program(LSLOTS, T, ncores=2 * B)
    nc, S, _ = _CACHE[key]
    in_maps = make_in_maps(inp, B, T)
    res = run_bass_kernel_spmd(nc, in_maps, core_ids=list(range(2 * B)))
    out = np.stack([np.ascontiguousarray(res.results[2 * b + 1]["outT"].T) for b in range(B)], axis=0)
    return out.astype(np.float32)
```
